# Optimizing a Trainium2 kernel written in Bass

```python
import math
import jax, jax.numpy as jnp
from jax import lax
import numpy as np

D_MODEL = 2048
BATCH = 2
SEQ = 16384
DEPTH = 1

HEAD_DIM = 128
HEADS_PER_GROUP = 4
ATTN_PATTERNS = ((128, 1), (512, 4), (2048, 16))
N_GROUPS = len(ATTN_PATTERNS)
N_ATTN_HEADS = N_GROUPS * HEADS_PER_GROUP
ATTN_WIDTH = N_ATTN_HEADS * HEAD_DIM
ATTN_OUT_WIDTH = HEADS_PER_GROUP * HEAD_DIM
ATTN_BLOCK = 128
ROPE_THETA = 500000.0
ROPE_DIM = HEAD_DIM // 4
CONV_WIDTH = D_MODEL // 2
CONV_KERNEL = 3
D_FF = 5632
NORM_EPS = 1e-5
IN_SPLIT_SIZES = (ATTN_WIDTH,) * 3 + (CONV_WIDTH,) * 3 + (D_MODEL,) * 2
IN_COLS = sum(IN_SPLIT_SIZES)

kernel_name = "hybrid_gated_conv_dilated_attn_macaron"


def rmsnorm(x, g):
    xf = x.astype(jnp.float32)
    y = xf * lax.rsqrt(jnp.mean(xf * xf, axis=-1, keepdims=True) + NORM_EPS)
    return (y * g.astype(jnp.float32)).astype(x.dtype)


def swiglu(h, w_in, w_out):
    gu = h @ w_in
    gate, up = jnp.split(gu, 2, axis=-1)
    return (jax.nn.silu(gate) * up) @ w_out


def partial_rotary(x, positions):
    half = ROPE_DIM // 2
    inv_freq = ROPE_THETA ** (-(jnp.arange(half, dtype=jnp.float32) * 2.0) / ROPE_DIM)
    ang = positions.astype(jnp.float32)[:, None] * inv_freq[None, :]
    cos = jnp.cos(ang)[None, :, None, :]
    sin = jnp.sin(ang)[None, :, None, :]
    xr = x[..., :ROPE_DIM].astype(jnp.float32)
    x1, x2 = xr[..., :half], xr[..., half:]
    rot = jnp.concatenate([x1 * cos - x2 * sin, x2 * cos + x1 * sin], axis=-1)
    return jnp.concatenate([rot.astype(x.dtype), x[..., ROPE_DIM:]], axis=-1)


def dilated_window_attention(q, k, v, window, dilation):
    B, S, H, Dh = q.shape
    span = window // dilation
    assert span <= ATTN_BLOCK
    unit = dilation * ATTN_BLOCK
    s_pad = -(-S // unit) * unit
    L = s_pad // dilation
    nblk = L // ATTN_BLOCK

    def to_strided(t):
        t = jnp.pad(t, ((0, 0), (0, s_pad - S), (0, 0), (0, 0)))
        t = t.reshape(B, L, dilation, H, Dh).transpose(0, 2, 3, 1, 4)
        return t.reshape(B, dilation, H, nblk, ATTN_BLOCK, Dh)

    def with_prev(t):
        prev = jnp.pad(t, ((0, 0), (0, 0), (0, 0), (1, 0), (0, 0), (0, 0)))[:, :, :, :-1]
        return jnp.concatenate([prev, t], axis=4)

    qs = to_strided(q)
    kk = with_prev(to_strided(k))
    vv = with_prev(to_strided(v))

    scores = jnp.einsum('bdhnqe,bdhnke->bdhnqk', qs, kk).astype(jnp.float32) * (Dh ** -0.5)
    qi = jnp.arange(ATTN_BLOCK)[:, None]
    kj = jnp.arange(2 * ATTN_BLOCK)[None, :]
    dist = ATTN_BLOCK + qi - kj
    band = (dist >= 0) & (dist <= span)
    blk = jnp.arange(nblk)[:, None, None]
    mask = band[None] & ((blk > 0) | (kj >= ATTN_BLOCK)[None])
    scores = jnp.where(mask, scores, -jnp.inf)
    m = jnp.max(scores, axis=-1, keepdims=True)
    p = jnp.exp(scores - m)
    l = jnp.sum(p, axis=-1, keepdims=True)
    o = jnp.einsum('bdhnqk,bdhnke->bdhnqe', p.astype(vv.dtype), vv).astype(jnp.float32) / l
    lse = m[..., 0] + jnp.log(l[..., 0])

    o = o.reshape(B, dilation, H, L, Dh).transpose(0, 3, 1, 2, 4).reshape(B, s_pad, H, Dh)[:, :S]
    lse = lse.reshape(B, dilation, H, L).transpose(0, 3, 1, 2).reshape(B, s_pad, H)[:, :S]
    return o, lse


def short_conv_branch(xc, b_gate, c_gate, w_conv):
    u = c_gate * xc
    kern = w_conv.reshape(CONV_KERNEL, 1, CONV_WIDTH).astype(u.dtype)
    conv = lax.conv_general_dilated(
        u, kern, window_strides=(1,), padding=[(CONV_KERNEL - 1, 0)],
        dimension_numbers=('NWC', 'WIO', 'NWC'), feature_group_count=CONV_WIDTH)
    return b_gate * conv


def dilated_attention_branch(q, k, v, positions):
    B, S, _ = q.shape
    q = partial_rotary(q.reshape(B, S, N_ATTN_HEADS, HEAD_DIM), positions)
    k = partial_rotary(k.reshape(B, S, N_ATTN_HEADS, HEAD_DIM), positions)
    v = v.reshape(B, S, N_ATTN_HEADS, HEAD_DIM)
    outs, lses = [], []
    for g, (window, dilation) in enumerate(ATTN_PATTERNS):
        hs = slice(g * HEADS_PER_GROUP, (g + 1) * HEADS_PER_GROUP)
        o, lse = dilated_window_attention(q[:, :, hs], k[:, :, hs], v[:, :, hs], window, dilation)
        outs.append(o)
        lses.append(lse)
    wts = jax.nn.softmax(jnp.stack(lses, axis=0), axis=0)
    o = jnp.einsum('gbsh,gbshe->bshe', wts, jnp.stack(outs, axis=0))
    return o.reshape(B, S, ATTN_OUT_WIDTH).astype(q.dtype)


def token_mixer(h, positions, w_in, w_conv, w_conv_out, w_attn_out, w_o):
    proj = h @ w_in
    splits = [int(i) for i in np.cumsum(IN_SPLIT_SIZES)[:-1]]
    q, k, v, xc, b_gate, c_gate, g_conv, g_attn = jnp.split(proj, splits, axis=-1)
    y_conv = short_conv_branch(xc, b_gate, c_gate, w_conv) @ w_conv_out
    y_attn = dilated_attention_branch(q, k, v, positions) @ w_attn_out
    merged = jax.nn.sigmoid(g_conv) * y_conv + jax.nn.sigmoid(g_attn) * y_attn
    return merged @ w_o


def setup_inputs(seed: int = 0) -> dict:
    key = jax.random.key(seed)
    ks = jax.random.split(key, 16)
    f32 = jnp.float32

    def w(k, shape, fan_in, scale=1.0):
        return jax.random.normal(k, shape, f32) * (scale * fan_in ** -0.5)

    def gain(k, shape):
        return jnp.ones(shape, f32) + 0.02 * jax.random.normal(k, shape, f32)

    return {
        "x": jax.random.normal(ks[0], (BATCH, SEQ, D_MODEL), f32),
        "ffn1_norm": gain(ks[1], (DEPTH, D_MODEL)),
        "w_ffn1_in": w(ks[2], (DEPTH, D_MODEL, 2 * D_FF), D_MODEL),
        "w_ffn1_out": w(ks[3], (DEPTH, D_FF, D_MODEL), D_FF),
        "mix_norm": gain(ks[4], (DEPTH, D_MODEL)),
        "w_in": w(ks[5], (DEPTH, D_MODEL, IN_COLS), D_MODEL),
        "w_conv": w(ks[6], (DEPTH, CONV_KERNEL, CONV_WIDTH), CONV_KERNEL),
        "w_conv_out": w(ks[7], (DEPTH, CONV_WIDTH, D_MODEL), CONV_WIDTH),
        "w_attn_out": w(ks[8], (DEPTH, ATTN_OUT_WIDTH, D_MODEL), ATTN_OUT_WIDTH),
        "w_o": w(ks[9], (DEPTH, D_MODEL, D_MODEL), D_MODEL),
        "ffn2_norm": gain(ks[10], (DEPTH, D_MODEL)),
        "w_ffn2_in": w(ks[11], (DEPTH, D_MODEL, 2 * D_FF), D_MODEL),
        "w_ffn2_out": w(ks[12], (DEPTH, D_FF, D_MODEL), D_FF),
        "final_norm": gain(ks[13], (D_MODEL,)),
    }


def reference(x, ffn1_norm, w_ffn1_in, w_ffn1_out, mix_norm, w_in, w_conv, w_conv_out,
              w_attn_out, w_o, ffn2_norm, w_ffn2_in, w_ffn2_out, final_norm):
    S = x.shape[1]
    positions = jnp.arange(S, dtype=jnp.int32)
    for l in range(DEPTH):
        x = x + 0.5 * swiglu(rmsnorm(x, ffn1_norm[l]), w_ffn1_in[l], w_ffn1_out[l])
        h = rmsnorm(x, mix_norm[l])
        x = x + token_mixer(h, positions, w_in[l], w_conv[l], w_conv_out[l], w_attn_out[l], w_o[l])
        x = x + 0.5 * swiglu(rmsnorm(x, ffn2_norm[l]), w_ffn2_in[l], w_ffn2_out[l])
    return rmsnorm(x, final_norm)
```

```python
import numpy as np
import ml_dtypes
from contextlib import ExitStack
import concourse.bass as bass
import concourse.mybir as mybir
from concourse.bass_utils import run_bass_kernel_spmd

F32 = mybir.dt.float32
BF16 = mybir.dt.bfloat16
AF = mybir.ActivationFunctionType
ALU = mybir.AluOpType

D = 2048
DFF = 5632
NC = 8
TT = 512
N_HALO_T = 4
N_OWN_T = 8
NT = N_HALO_T + N_OWN_T
T_OWN = N_OWN_T * TT
T_ALL = NT * TT
NGRAN = 12
GRAN = 1024
NWSEM = 8
CAST_LEAD = 120
NCAST = 8
NPS = 8
OFF_Q, OFF_K, OFF_V, OFF_XC, OFF_B, OFF_C, OFF_GC, OFF_GA = 0, 1536, 3072, 4608, 5632, 6656, 7680, 9728
ROPE_THETA = 500000.0
SCALE = 128.0 ** -0.5
EPS = 1e-5


class _Dummy:
    def __getitem__(self, k):
        return self

    def __getattr__(self, k):
        return lambda *a, **kw: self


DUMMY = _Dummy()


class EngW:
    def __init__(self, name, sem, self_sync):
        self.name = name
        self.sem = sem
        self.count = 0
        self.waited = {}
        self.self_sync = self_sync
        self.items = []

    def wait(self, tok):
        sem, val = tok
        if sem is self.sem and not self.self_sync:
            return
        k = id(sem)
        if self.waited.get(k, 0) >= val:
            return
        self.waited[k] = val
        self.items.append(("wait", sem, val))


class Prog:
    def __init__(self, sems):
        self.dry = False
        self.E = {
            "pe": EngW("pe", sems["pe"], False),
            "act": EngW("act", sems["act"], True),
            "dve": EngW("dve", sems["dve"], True),
            "pool": EngW("pool", sems["pool"], True),
            "sp": EngW("sp", sems["sp"], False),
        }
        self.lastw = {}
        self.readers = {}
        self.dma_cnt = {}

    def _deps(self, reads, writes):
        d = []
        for k in reads:
            t = self.lastw.get(k)
            if t is not None:
                d.append(t)
        for k in writes:
            t = self.lastw.get(k)
            if t is not None:
                d.append(t)
            r = self.readers.get(k)
            if r:
                d.extend(r.values())
        return d

    def _record(self, tok, reads, writes):
        for k in reads:
            r = self.readers.setdefault(k, {})
            sid = id(tok[0])
            old = r.get(sid)
            if old is None or old[1] < tok[1]:
                r[sid] = tok
        for k in writes:
            self.lastw[k] = tok
            self.readers[k] = {}

    def op(self, eng, fn, reads=(), writes=(), signal=True):
        if self.dry:
            return
        E = self.E[eng]
        px = [k for k in reads if k[0] == "ps"]
        if px:
            reads = [k for k in reads if k[0] != "ps"]
            writes = list(writes) + px
        for tok in self._deps(reads, writes):
            E.wait(tok)
        if signal:
            E.count += 1
            tok = (E.sem, E.count)
        else:
            tok = (E.sem, E.count + 1)
        E.items.append(("op", fn, signal))
        self._record(tok, reads, writes)

    def dma(self, eng, out_ap, in_ap, sem, reads=(), writes=()):
        if self.dry:
            return None
        E = self.E[eng]
        for tok in self._deps(reads, writes):
            E.wait(tok)
        c = self.dma_cnt.get(id(sem), 0)
        if c > 0:
            E.wait((sem, c))
        c += 16
        self.dma_cnt[id(sem)] = c
        E.items.append(("dma", out_ap, in_ap, sem))
        self._record((sem, c), reads, writes)
        return (sem, c)

    def wait_tok(self, eng, tok):
        if not self.dry and tok is not None:
            self.E[eng].wait(tok)

    def replay(self, eng, h):
        E = self.E[eng]
        for it in E.items:
            if it[0] == "wait":
                h.wait_ge(it[1], it[2])
            elif it[0] == "op":
                ins = it[1](h)
                if it[2]:
                    ins.then_inc(E.sem, 1)
            else:
                h.dma_start(out=it[1], in_=it[2]).then_inc(it[3], 16)


def perm(g, ap):
    if g == 0:
        return ap.rearrange("p (q n) -> p q n", q=4)
    return ap.rearrange("p (n q) -> p q n", q=4)


def blk(ap):
    return ap.rearrange("p (q n) -> p q n", q=4)


class Builder:
    def __init__(self, n_tiles=NT, dbg=None):
        self.n_tiles = n_tiles
        self.dbg = dbg or {}

    def build(self):
        nc = bass.Bass("TRN2", target_bir_lowering=False)
        self.nc = nc
        dr = {}

        def din(name, shape, dt=F32):
            dr[name] = nc.dram_tensor(name, shape, dt, kind="ExternalInput").ap()

        din("xin", [T_ALL, D])
        din("w_ffn1_in", [D, 2 * DFF]); din("w_ffn1_out", [DFF, D])
        din("w_in", [D, 11776]); din("w_conv_out", [1024, D]); din("w_attn_out", [512, D]); din("w_o", [D, D])
        din("w_ffn2_in", [D, 2 * DFF]); din("w_ffn2_out", [DFF, D])
        din("gains", [128, 3, 16]); din("gF", [128, D]); din("wcv", [128, 8, 3])
        din("cosd", [NT, 32, TT]); din("sind", [NT, 32, TT])
        din("masks", [128, 8, 128], BF16); din("ident", [128, 128], BF16); din("ones", [128, 128], BF16)
        din("pswap", [32, 32])
        dr["y"] = nc.dram_tensor("y", [T_OWN, D], F32, kind="ExternalOutput").ap()
        for k, shp in self.dbg.items():
            dr[k] = nc.dram_tensor(k, shp, F32, kind="ExternalOutput").ap()
        self.dr = dr

        self.P = Prog({k: None for k in ["pe", "act", "dve", "pool", "sp"]})
        self.P.dry = True
        self.dry = True
        self.useq = []
        self.t = {}
        self.emit_all()
        useq = self.useq
        uniq = []
        seen = {}
        for s in useq:
            if s not in seen:
                seen[s] = len(uniq)
                uniq.append(s)
        self.uidx = seen
        self.uoff = {}
        off = 0
        for sp in uniq:
            self.uoff[sp] = off
            off += 128 * sp[2] * sp[4]
        dr["scr"] = nc.dram_tensor("scr", [off], BF16, kind="Internal").ap()

        with ExitStack() as es:
            def sb(name, shape, dt):
                return es.enter_context(nc.sbuf_tensor("sb_" + name, shape, dt))

            def sem(name):
                return es.enter_context(nc.semaphore(name))

            t = {}
            t["xt"] = sb("xt", [128, 4, D], F32)
            t["hT"] = sb("hT", [128, 16, TT], BF16)
            t["sB"] = sb("sB", [128, 44 * 512], BF16)
            for g, ns in enumerate([2, 2, 5]):
                t[f"KT{g}"] = sb(f"KT{g}", [128, ns, 4, TT], BF16)
                t[f"V{g}"] = sb(f"V{g}", [128, ns, 4, TT], BF16)
            t["wring"] = sb("wring", [128, NGRAN * GRAN], BF16)
            t["gF"] = sb("gF", [128, D], F32)
            t["cosT"] = sb("cosT", [32, TT], F32)
            t["sinT"] = sb("sinT", [32, TT], F32)
            t["sgt"] = sb("sgt", [128, TT], F32)
            t["uh"] = sb("uh", [128, 8, 2], F32)
            t["masks"] = sb("masks_sb", [128, 8, 128], BF16)
            t["ident"] = sb("ident_sb", [128, 128], BF16)
            t["ones"] = sb("ones_sb", [128, 128], BF16)
            t["pswap"] = sb("pswap_sb", [32, 32], F32)
            t["gains"] = sb("gains_sb", [128, 3, 16], F32)
            t["wcv"] = sb("wcv_sb", [128, 8, 3], F32)
            t["ss"] = sb("ss", [128, 4], F32)
            t["ssp"] = sb("ssp", [128, 16], F32)
            t["sq"] = sb("sq", [128, 4], F32)
            t["rstd"] = sb("rstd", [128, 4], F32)
            t["eps"] = sb("eps_sb", [128, 1], F32)
            if self.dbg:
                t["dtmp"] = sb("dtmp", [128, TT], F32)
            t["pb"] = [es.enter_context(nc.psum_tensor(f"pb{i}", [128, 512], F32)) for i in range(NPS)]
            self.t = t
            sems = {k: sem("s_" + k) for k in ["pe", "act", "dve", "pool", "sp"]}
            self.wsem = [sem(f"w{i}") for i in range(NWSEM)]
            self.csem = [sem(f"c{i}") for i in range(NCAST)]
            self.xsem = [sem(f"x{i}") for i in range(4)]
            self.osem = [sem(f"o{i}") for i in range(4)]
            self.msem = [sem(f"m{i}") for i in range(12)]
            self.dsem = [sem(f"dbg{i}") for i in range(4)]
            self.P = Prog(sems)
            self.dry = False
            self.emit_all()
            P = self.P
            with nc.Block() as block:
                @block.tensor
                def _(h):
                    P.replay("pe", h)

                @block.scalar
                def _(h):
                    P.replay("act", h)

                @block.vector
                def _(h):
                    P.replay("dve", h)

                @block.gpsimd
                def _(h):
                    P.replay("pool", h)

                @block.sync
                def _(h):
                    P.replay("sp", h)
        return nc

    def S(self, a, n=1, f32=False):
        keys = [("sB", a + i) for i in range(n)]
        if self.dry:
            return DUMMY, keys
        ap = self.t["sB"][:, a * 512:(a + n) * 512]
        if f32:
            ap = ap.bitcast(F32)
        return ap, keys

    def alloc(self):
        i = self.psn % NPS
        self.psn += 1
        if self.dry:
            return DUMMY, ("ps", i)
        return self.t["pb"][i], ("ps", i)

    def wnext(self, spec):
        if self.dry:
            self.useq.append(spec)
            return DUMMY, [("wr", 0)], None
        n = self.wpos
        self.wpos += 1
        assert self.useq[n] == spec, (n, self.useq[n], spec)
        self.prefetch()
        assert self.wissued > n, ("weight ring too small / unit not released", n, spec, self.wactive)
        g0, ng = self.wplace[n]
        nr, ncols = spec[2], spec[4]
        ap = self.t["wring"][:, g0 * GRAN:g0 * GRAN + nr * ncols].rearrange("p (r n) -> p r n", r=nr)
        return ap, [("wr", g0 + i) for i in range(ng)], n

    def wdone(self, h):
        if self.dry:
            return
        self.wactive = [a for a in self.wactive if a[0] != h]
        self.prefetch()

    def prefetch(self):
        P = self.P
        while self.wissued < len(self.useq):
            m = self.wissued
            spec = self.useq[m]
            name, r0, nr, c0, ncols = spec
            ng = -(-(nr * ncols) // GRAN)
            cand = [self.whead] if self.whead + ng <= NGRAN else []
            cand.append(0)
            g0 = None
            for c in cand:
                if all(c + ng <= a[1] or c >= a[1] + a[2] for a in self.wactive):
                    g0 = c
                    break
            if g0 is None:
                return
            self.whead = g0 + ng
            self.wactive.append((m, g0, ng))
            self.wplace[m] = (g0, ng)
            sid = self.uidx[spec]
            o0 = self.uoff[spec]
            scr_v = self.dr["scr"][o0:o0 + 128 * nr * ncols].rearrange("(p r n) -> p r n", p=128, r=nr)
            if sid not in self.casted:
                self.casted.add(sid)
                src = self.dr[name][r0 * 128:(r0 + nr) * 128, c0:c0 + ncols].rearrange("(r p) n -> p r n", p=128)
                k = self.ncast % NCAST
                self.ncast += 1
                if m >= CAST_LEAD:
                    P.wait_tok("pool", self.wtok[m - CAST_LEAD])
                P.dma("pool", scr_v, src, self.csem[k], writes=[("scr", sid)])
            dst = self.t["wring"][:, g0 * GRAN:g0 * GRAN + nr * ncols].rearrange("p (r n) -> p r n", r=nr)
            k = self.nwdma % NWSEM
            self.nwdma += 1
            self.wtok[m] = P.dma("sp", dst, scr_v, self.wsem[k], reads=[("scr", sid)], writes=[("wr", g0 + i) for i in range(ng)])
            self.wissued += 1

    def mm(self, out_ap, lhsT, rhs, start, stop, reads, bank, signal=None):
        sig = stop if signal is None else signal
        self.P.op("pe", lambda e: e.matmul(out_ap, lhsT=lhsT, rhs=rhs, start=start, stop=stop),
                  reads=reads, writes=[bank], signal=sig)

    def xkeys(self, s, dgs=range(4)):
        return [("x", s, dg) for dg in dgs]

    def emit_all(self):
        P = self.P
        t = self.t if not self.dry else None
        self.psn = 0
        self.wpos = 0
        self.wissued = 0
        self.whead = 0
        self.wactive = []
        self.wplace = {}
        self.wtok = {}
        self.nwdma = 0
        self.casted = set()
        self.ncast = 0
        self.store_toks = []
        dr = self.dr
        if not self.dry:
            for i, (nm, dst) in enumerate([("gains", t["gains"]), ("gF", t["gF"]), ("wcv", t["wcv"]), ("masks", t["masks"]),
                                           ("ident", t["ident"]), ("ones", t["ones"]), ("pswap", t["pswap"])]):
                P.dma("sp", dst[:], dr[nm], self.msem[i], writes=[("c", nm)])
            P.op("dve", lambda e: e.memset(t["eps"][:], EPS), writes=[("c", "eps")])
            P.op("dve", lambda e: e.memset(t["uh"][:], 0.0), writes=[("uh", ch) for ch in range(8)])
        for T in range(self.n_tiles):
            own = T >= N_HALO_T
            if T == 0:
                self.load_x(T)
            self.ffn(T, 0)
            self.mixer(T, own)
            if own:
                self.ffn(T, 2)
                self.final(T)
        if not self.dry:
            for tok in self.store_toks:
                P.wait_tok("sp", tok)

    def load_x(self, T, subs=range(4)):
        if self.dry or T >= self.n_tiles:
            return
        for s in subs:
            self.P.dma("sp", self.t["xt"][:, s, :], self.dr["xin"][T * TT + s * 128:T * TT + (s + 1) * 128, :],
                       self.xsem[s], writes=self.xkeys(s))

    def dump(self, name, T, src_ap, keys, rows=None):
        if self.dry or name not in self.dbg:
            return
        k = self.ndbg = getattr(self, "ndbg", 0) + 1
        tok = self.P.dma("sp", rows, src_ap, self.dsem[k % 4], reads=keys)
        self.store_toks.append(tok)

    def dump_any(self, name, row0, src_ap, keys, eng="dve"):
        if self.dry or name not in self.dbg:
            return
        P = self.P
        t = self.t
        P.op(eng, (lambda e: e.tensor_copy(out=t["dtmp"][:], in_=src_ap)) if eng == "dve" else (lambda e: e.activation(out=t["dtmp"][:], in_=src_ap, func=AF.Copy)),
             reads=keys, writes=[("dtmp",)])
        k = self.ndbg = getattr(self, "ndbg", 0) + 1
        tok = P.dma("sp", self.dr[name][row0:row0 + 128, :], t["dtmp"][:], self.dsem[k % 4], reads=[("dtmp",)])
        self.store_toks.append(tok)

    def stats(self, s, partial, junk, junkk):
        P = self.P
        t = self.t
        xs = t["xt"][:, s, :]
        if partial:
            P.op("dve", lambda e: e.reduce_sum(out=t["ss"][:, s:s + 1], in_=t["ssp"][:, 4 * s:4 * s + 4], axis=mybir.AxisListType.X),
                 reads=[("ssp", s, dg) for dg in range(4)], writes=[("ss", s)])
        else:
            P.op("act", lambda e: e.activation(out=junk, in_=xs, func=AF.Square, accum_out=t["ss"][:, s:s + 1]),
                 reads=self.xkeys(s), writes=junkk + [("ss", s)])
        P.op("act", lambda e: e.activation(out=t["sq"][:, s:s + 1], in_=t["ss"][:, s:s + 1], func=AF.Sqrt, scale=1.0 / D, bias=t["eps"][:]),
             reads=[("ss", s), ("c", "eps")], writes=[("sq", s)])
        P.op("dve", lambda e: e.reciprocal(out=t["rstd"][:, s:s + 1], in_=t["sq"][:, s:s + 1]),
             reads=[("sq", s)], writes=[("rstd", s)])

    def sq_partial(self, s, dg):
        P = self.P
        t = self.t
        xs = t["xt"][:, s, dg * 512:(dg + 1) * 512]
        P.op("act", lambda e: e.activation(out=t["sgt"][:], in_=xs, func=AF.Square, accum_out=t["ssp"][:, 4 * s + dg:4 * s + dg + 1]),
             reads=[("x", s, dg)], writes=[("sgt",), ("ssp", s, dg)])

    def norm_T(self, gi, partial=False):
        P = self.P
        t = self.t
        dry = self.dry
        for s in range(4):
            hn, hk = self.S(4 * s, 4)
            if not dry:
                xs = t["xt"][:, s, :]
                self.stats(s, partial, hn, hk)
                P.op("dve", lambda e, hn=hn, xs=xs, s=s: e.tensor_scalar(out=hn, in0=xs, scalar1=t["rstd"][:, s:s + 1], scalar2=0.0, op0=ALU.mult, op1=ALU.add),
                     reads=self.xkeys(s) + [("rstd", s)], writes=hk)
        for c in range(16):
            bank, bk = self.alloc()
            if dry:
                continue
            bb = bank[:].bitcast(BF16)
            for s in range(4):
                hn, hk = self.S(4 * s, 4)
                P.op("pe", lambda e, bb=bb, hn=hn, s=s, c=c: e.transpose(out=bb[:, s * 128:(s + 1) * 128], in_=hn[:, c * 128:(c + 1) * 128], identity=t["ident"][:]),
                     reads=hk + [("c", "ident")], writes=[bk], signal=(s == 3))
            eng = "act" if c % 2 == 0 else "dve"
            if eng == "act":
                P.op("act", lambda e, bb=bb, c=c: e.activation(out=t["hT"][:, c, :], in_=bb[:, 0:TT], func=AF.Copy, scale=t["gains"][:, gi, c:c + 1]),
                     reads=[bk, ("c", "gains")], writes=[("hT", c)])
            else:
                P.op("dve", lambda e, bb=bb, c=c: e.tensor_scalar(out=t["hT"][:, c, :], in0=bb[:, 0:TT], scalar1=t["gains"][:, gi, c:c + 1], scalar2=0.0, op0=ALU.mult, op1=ALU.add),
                     reads=[bk, ("c", "gains")], writes=[("hT", c)])

    def ffn(self, T, gi):
        P = self.P
        t = self.t
        dry = self.dry
        w_in = "w_ffn1_in" if gi == 0 else "w_ffn2_in"
        w_out = "w_ffn1_out" if gi == 0 else "w_ffn2_out"
        self.norm_T(gi, partial=(gi == 2))
        if gi == 0 and T == 0 and not dry:
            for c in range(16):
                self.dump_any("d_hT", c * 128, t["hT"][:, c, :], [("hT", c)])
        for f in range(44):
            wg, kg, hg = self.wnext((w_in, 0, 16, 128 * f, 128))
            wu, ku, hu = self.wnext((w_in, 0, 16, DFF + 128 * f, 128))
            bg, bgk = self.alloc()
            for c in range(16):
                if not dry:
                    self.mm(bg[:], wg[:, c, :], t["hT"][:, c, :], c == 0, c == 15, kg + [("hT", c)], bgk)
            self.wdone(hg)
            bu, buk = self.alloc()
            for c in range(16):
                if not dry:
                    self.mm(bu[:], wu[:, c, :], t["hT"][:, c, :], c == 0, c == 15, ku + [("hT", c)], buk)
            self.wdone(hu)
            a, ak = self.S(f, 1)
            if not dry and gi == 0 and T == 0 and f in (0, 1, 43):
                fi = (0, 1, 43).index(f)
                self.dump_any("d_gate", fi * 128, bg[:], [bgk])
                self.dump_any("d_up", fi * 128, bu[:], [buk])
            if not dry:
                P.op("act", lambda e, bg=bg: e.activation(out=t["sgt"][:], in_=bg[:], func=AF.Silu), reads=[bgk], writes=[("sgt",)])
                P.op("dve", lambda e, bu=bu, a=a: e.tensor_tensor(out=a, in0=bu[:], in1=t["sgt"][:], op=ALU.mult), reads=[buk, ("sgt",)], writes=ak)
        if gi == 0 and T == 0 and not dry:
            for fi, f in enumerate((0, 1, 43)):
                a, ak = self.S(f, 1)
                self.dump_any("d_act", fi * 128, a, ak)
        fgs = [(0, 8), (8, 8), (16, 8), (24, 8), (32, 8), (40, 4)]
        for dg in range(4):
            banks = [self.alloc() for _ in range(4)]
            for (f0, nf) in fgs:
                wo, ko, ho = self.wnext((w_out, f0, nf, 512 * dg, 512))
                for s in range(4):
                    for r in range(nf):
                        f = f0 + r
                        if not dry:
                            a, ak = self.S(f, 1)
                            last_use = (s == 3 and r == nf - 1)
                            self.mm(banks[s][0][:], a[:, s * 128:(s + 1) * 128], wo[:, r, :], f == 0, f == 43, ak + ko, banks[s][1],
                                    signal=(f == 43 or last_use))
                self.wdone(ho)
            for s in range(4):
                if not dry:
                    xs = t["xt"][:, s, dg * 512:(dg + 1) * 512]
                    P.op("dve", lambda e, b=banks[s][0], xs=xs: e.scalar_tensor_tensor(out=xs, in0=b[:], scalar=0.5, in1=xs, op0=ALU.mult, op1=ALU.add),
                         reads=[banks[s][1], ("x", s, dg)], writes=[("x", s, dg)])
                    self.sq_partial(s, dg)
        if "x1" in self.dbg and gi == 0 and not dry:
            for s in range(4):
                self.dump("x1", T, t["xt"][:, s, :], self.xkeys(s), rows=self.dr["x1"][T * TT + s * 128:T * TT + (s + 1) * 128, :])

    def rope_A(self, w, wk, i, g, dst, dstk, wh=None):
        P = self.P
        t = self.t
        dry = self.dry
        bank, bk = self.alloc()
        for c in range(16):
            if not dry:
                self.mm(bank[:], w[:, c, i * 128:(i + 1) * 128], t["hT"][:, c, :], c == 0, c == 15, wk + [("hT", c)], bk)
        if wh is not None:
            self.wdone(wh)
        par = self.rope_par = 1 - getattr(self, "rope_par", 0)
        r32, rk = self.S(28 + 2 * par, 2, f32=True)
        if not dry:
            P.op("act", lambda e: e.activation(out=blk(dst), in_=perm(g, bank[:]), func=AF.Copy), reads=[bk], writes=dstk)
            P.op("act", lambda e: e.activation(out=r32[0:32, :], in_=bank[0:32, :], func=AF.Copy), reads=[bk], writes=rk)
        return (r32, rk, dst, dstk, g)

    def rope_B(self, st):
        if st is None:
            return
        P = self.P
        t = self.t
        r32, rk, dst, dstk, g = st
        t1, t1k = self.S(32, 2, f32=True)
        t2, t2k = self.S(34, 2, f32=True)
        sw, swk = self.alloc()
        if self.dry:
            return
        self.mm(sw[0:32, :], t["pswap"][:], r32[0:32, :], True, True, rk + [("c", "pswap")], swk)
        P.op("dve", lambda e: e.tensor_tensor(out=t1[0:32, :], in0=r32[0:32, :], in1=t["cosT"][:], op=ALU.mult), reads=rk + [("tab",)], writes=t1k)
        P.op("dve", lambda e: e.tensor_tensor(out=t2[0:32, :], in0=sw[0:32, :], in1=t["sinT"][:], op=ALU.mult), reads=[swk, ("tab",)], writes=t2k)
        P.op("dve", lambda e: e.tensor_tensor(out=blk(dst[0:32, :]), in0=perm(g, t1[0:32, :]), in1=perm(g, t2[0:32, :]), op=ALU.add),
             reads=t1k + t2k, writes=dstk)

    def mixer(self, T, own):
        P = self.P
        t = self.t
        dry = self.dry
        dr = self.dr
        sl = [T % 2, T % 2, T % 5]
        last_halo = (T == N_HALO_T - 1)
        if not dry:
            P.dma("sp", t["cosT"][:], dr["cosd"][T], self.msem[8], writes=[("tab",)])
            P.dma("sp", t["sinT"][:], dr["sind"][T], self.msem[9], writes=[("tab",)])
        self.norm_T(1, partial=True)
        if not own:
            self.load_x(T + 1)
        pending = None
        kv_groups = (0, 1, 2) if (own or last_halo) else (2,)
        if own:
            for u in range(6):
                w, wk, wh = self.wnext(("w_in", 0, 16, OFF_Q + 256 * u, 256))
                for i in range(2):
                    h = 2 * u + i
                    dst, dstk = self.S(16 + h, 1)
                    st = self.rope_A(w, wk, i, h // 4, dst, dstk, wh if i == 1 else None)
                    self.rope_B(pending)
                    pending = st
        for u in range(6):
            if u // 2 not in kv_groups:
                continue
            w, wk, wh = self.wnext(("w_in", 0, 16, OFF_K + 256 * u, 256))
            for i in range(2):
                h = 2 * u + i
                g, j = h // 4, h % 4
                dst = DUMMY if dry else t[f"KT{g}"][:, sl[g], j, :]
                st = self.rope_A(w, wk, i, g, dst, [("KT", g, sl[g], j)], wh if i == 1 else None)
                self.rope_B(pending)
                pending = st
        for u in range(6):
            if u // 2 not in kv_groups:
                continue
            w, wk, wh = self.wnext(("w_in", 0, 16, OFF_V + 256 * u, 256))
            g, half = u // 2, u % 2
            for qbp in range(2):
                bank, bk = self.alloc()
                for qq in range(2):
                    qb = 2 * qbp + qq
                    for c in range(16):
                        if not dry:
                            hc = t["hT"][:, c, :]
                            lhsT = hc[:, qb * 128:(qb + 1) * 128] if g == 0 else hc[:, qb:TT:4]
                            self.mm(bank[:, qq * 256:(qq + 1) * 256], lhsT, w[:, c, :], c == 0, c == 15, wk + [("hT", c)], bk)
                if not dry:
                    dst = t[f"V{g}"][:, sl[g], 2 * qbp:2 * qbp + 2, half * 256:(half + 1) * 256]
                    P.op("act", lambda e, dst=dst, bank=bank: e.activation(out=dst, in_=bank[:].rearrange("p (q n) -> p q n", q=2), func=AF.Copy),
                         reads=[bk], writes=[("V", g, sl[g], qbp, half)])
                if pending is not None:
                    self.rope_B(pending)
                    pending = None
            self.wdone(wh)
        if not own:
            if last_halo:
                self.conv_halo()
            return
        self.attention(T, sl)
        self.conv()
        self.merge_out()
        if "x2" in self.dbg and not dry:
            for s in range(4):
                self.dump("x2", T, t["xt"][:, s, :], self.xkeys(s), rows=dr["x2"][T * TT + s * 128:T * TT + (s + 1) * 128, :])

    def attention(self, T, sl):
        pend = None
        idx = 0
        for j in range(4):
            for g in range(3):
                st = self.attn_A(T, sl, j, g, idx % 2)
                if pend is not None:
                    self.attn_B(pend)
                pend = st
                idx += 1
        self.attn_B(pend)

    def attn_A(self, T, sl, j, g, pbuf):
        P = self.P
        t = self.t
        dry = self.dry
        halo = lambda TT_: TT_ < N_HALO_T
        h = 4 * g + j
        QT, QTk = self.S(16 + h, 1)
        deltas = []
        if g == 0:
            own_l, prev_l = [], []
            for qb in range(4):
                own_l.append((sl[0], qb))
                prev_l.append((sl[0], qb - 1) if qb > 0 else ((T - 1) % 2, 3))
            deltas.append((own_l, [0] * 4))
            deltas.append((prev_l, [2 if (halo(T - 1)) else 1] + [1] * 3))
        elif g == 1:
            deltas.append(([(sl[1], qb) for qb in range(4)], [0] * 4))
            deltas.append(([((T - 1) % 2, qb) for qb in range(4)], [2 if halo(T - 1) else 1] * 4))
        else:
            for dl in range(5):
                Tk = T - dl
                if dl == 0:
                    mi = 3
                elif dl < 4:
                    mi = 6 if halo(Tk) else 4
                else:
                    mi = 7 if halo(Tk) else 5
                deltas.append(([(Tk % 5, qb) for qb in range(4)], [mi] * 4))
        pts = []
        for di, (blks, mis) in enumerate(deltas):
            sc, sck = self.alloc()
            pt, ptk = self.S(5 * pbuf + di, 1)
            pts.append((pt, ptk))
            if dry:
                continue
            for qb, (slot, kb) in enumerate(blks):
                self.mm(sc[:, qb * 128:(qb + 1) * 128], t[f"KT{g}"][:, slot, j, kb * 128:(kb + 1) * 128], QT[:, qb * 128:(qb + 1) * 128],
                        True, True, [("KT", g, slot, j)] + QTk, sck, signal=(qb == 3))
            P.op("act", lambda e, pt=pt, sc=sc: e.activation(out=pt, in_=sc[:], func=AF.Exp, scale=SCALE), reads=[sck], writes=ptk)
            if len(set(mis)) == 1:
                mi = mis[0]
                P.op("dve", lambda e, pt=pt, mi=mi: e.tensor_tensor(out=blk(pt), in0=blk(pt), in1=t["masks"][:, mi, :].unsqueeze(1).to_broadcast([128, 4, 128]), op=ALU.mult),
                     reads=ptk + [("c", "masks")], writes=ptk)
            else:
                mi0, mi = mis[0], mis[1]
                P.op("dve", lambda e, pt=pt, mi0=mi0: e.tensor_tensor(out=pt[:, 0:128], in0=pt[:, 0:128], in1=t["masks"][:, mi0, :], op=ALU.mult),
                     reads=ptk + [("c", "masks")], writes=ptk)
                P.op("dve", lambda e, pt=pt, mi=mi: e.tensor_tensor(out=pt[:, 128:512].rearrange("p (q n) -> p q n", q=3), in0=pt[:, 128:512].rearrange("p (q n) -> p q n", q=3),
                                                                    in1=t["masks"][:, mi, :].unsqueeze(1).to_broadcast([128, 3, 128]), op=ALU.mult),
                     reads=ptk + [("c", "masks")], writes=ptk)
        return (j, g, deltas, pts)

    def attn_B(self, st):
        P = self.P
        t = self.t
        dry = self.dry
        j, g, deltas, pts = st
        buf = j % 2
        OT, OTk = self.S(28 + 2 * buf, 2, f32=True)
        LB, LBk = self.S(32 + 2 * buf, 2, f32=True)
        ob, obk = self.alloc()
        lb, lbk = self.alloc()
        nd = len(deltas)
        if not dry:
            for qb in range(4):
                for di, (blks, mis) in enumerate(deltas):
                    slot, kb = blks[qb]
                    pt, ptk = pts[di]
                    self.mm(ob[:, qb * 128:(qb + 1) * 128], t[f"V{g}"][:, slot, kb, j * 128:(j + 1) * 128], pt[:, qb * 128:(qb + 1) * 128],
                            di == 0, di == nd - 1, [("V", g, slot, kb // 2, j // 2)] + ptk, obk, signal=(qb == 3 and di == nd - 1))
            for di in range(nd):
                pt, ptk = pts[di]
                self.mm(lb[:], t["ones"][:], pt, di == 0, di == nd - 1, ptk + [("c", "ones")], lbk)
            if g == 0:
                P.op("dve", lambda e: e.tensor_copy(out=OT, in_=ob[:]), reads=[obk], writes=OTk)
                P.op("dve", lambda e: e.tensor_copy(out=LB, in_=lb[:]), reads=[lbk], writes=LBk)
            else:
                P.op("dve", lambda e: e.tensor_tensor(out=perm(g, OT), in0=blk(ob[:]), in1=perm(g, OT), op=ALU.add), reads=[obk] + OTk, writes=OTk)
                P.op("dve", lambda e: e.tensor_tensor(out=perm(g, LB), in0=blk(lb[:]), in1=perm(g, LB), op=ALU.add), reads=[lbk] + LBk, writes=LBk)
        if g == 2:
            rc, rck = self.S(10, 2, f32=True)
            on, onk = self.S(40 + j, 1)
            if not dry:
                P.op("dve", lambda e: e.reciprocal(out=rc, in_=LB), reads=LBk, writes=rck)
                P.op("dve", lambda e: e.tensor_tensor(out=on, in0=OT, in1=rc, op=ALU.mult), reads=OTk + rck, writes=onk)

    def conv_halo(self):
        P = self.P
        t = self.t
        dry = self.dry
        for ch in range(8):
            wx, wxk, hx = self.wnext(("w_in", 0, 16, OFF_XC + 128 * ch, 128))
            wc, wck, hc = self.wnext(("w_in", 0, 16, OFF_C + 128 * ch, 128))
            bx, bxk = self.alloc()
            bc, bck = self.alloc()
            xs, xsk = self.S(36, 2, f32=True)
            if not dry:
                for c in range(16):
                    self.mm(bx[:, 0:2], wx[:, c, :], t["hT"][:, c, TT - 2:TT], c == 0, c == 15, wxk + [("hT", c)], bxk)
                for c in range(16):
                    self.mm(bc[:, 0:2], wc[:, c, :], t["hT"][:, c, TT - 2:TT], c == 0, c == 15, wck + [("hT", c)], bck)
                P.op("act", lambda e, bx=bx, xs=xs: e.activation(out=xs[:, 0:2], in_=bx[:, 0:2], func=AF.Copy), reads=[bxk], writes=xsk)
                P.op("dve", lambda e, bc=bc, xs=xs, ch=ch: e.tensor_tensor(out=t["uh"][:, ch, :], in0=bc[:, 0:2], in1=xs[:, 0:2], op=ALU.mult),
                     reads=[bck] + xsk, writes=[("uh", ch)])
            self.wdone(hx)
            self.wdone(hc)

    def conv(self):
        P = self.P
        t = self.t
        dry = self.dry
        for ch in range(8):
            wx, wxk, hx = self.wnext(("w_in", 0, 16, OFF_XC + 128 * ch, 128))
            wc, wck, hc = self.wnext(("w_in", 0, 16, OFF_C + 128 * ch, 128))
            wb, wbk, hb = self.wnext(("w_in", 0, 16, OFF_B + 128 * ch, 128))
            for _one in range(1):
                bx, bxk = self.alloc()
                bc, bck = self.alloc()
                bb, bbk = self.alloc()
                xs, xsk = self.S(36, 2, f32=True)
                acc, acck = self.S(38, 2, f32=True)
                ub, ubk = self.S(8, 3, f32=True)
                cv, cvk = self.S(28 + ch, 1)
                if dry:
                    continue
                for (b, bk_, w_, wk_) in ((bx, bxk, wx, wxk), (bc, bck, wc, wck), (bb, bbk, wb, wbk)):
                    for c in range(16):
                        self.mm(b[:], w_[:, c, :], t["hT"][:, c, :], c == 0, c == 15, wk_ + [("hT", c)], bk_)
                self.wdone(hx)
                self.wdone(hc)
                self.wdone(hb)
                cw = t["wcv"]
                P.op("act", lambda e, bx=bx, xs=xs: e.activation(out=xs, in_=bx[:], func=AF.Copy), reads=[bxk], writes=xsk)
                P.op("dve", lambda e, ub=ub, ch=ch: e.tensor_copy(out=ub[:, 0:2], in_=t["uh"][:, ch, :]), reads=[("uh", ch)], writes=ubk)
                P.op("dve", lambda e, ub=ub, bc=bc, xs=xs: e.tensor_tensor(out=ub[:, 2:514], in0=bc[:], in1=xs, op=ALU.mult), reads=[bck] + xsk, writes=ubk)
                P.op("dve", lambda e, ub=ub, ch=ch: e.tensor_copy(out=t["uh"][:, ch, :], in_=ub[:, 512:514]), reads=ubk, writes=[("uh", ch)])
                P.op("dve", lambda e, ub=ub, acc=acc, ch=ch: e.tensor_scalar(out=acc, in0=ub[:, 2:514], scalar1=cw[:, ch, 2:3], scalar2=0.0, op0=ALU.mult, op1=ALU.add),
                     reads=ubk + [("c", "wcv")], writes=acck)
                P.op("dve", lambda e, ub=ub, acc=acc, ch=ch: e.scalar_tensor_tensor(out=acc, in0=ub[:, 1:513], scalar=cw[:, ch, 1:2], in1=acc, op0=ALU.mult, op1=ALU.add),
                     reads=ubk + acck + [("c", "wcv")], writes=acck)
                P.op("dve", lambda e, ub=ub, acc=acc, ch=ch: e.scalar_tensor_tensor(out=acc, in0=ub[:, 0:512], scalar=cw[:, ch, 0:1], in1=acc, op0=ALU.mult, op1=ALU.add),
                     reads=ubk + acck + [("c", "wcv")], writes=acck)
                P.op("dve", lambda e, bb=bb, acc=acc, cv=cv: e.tensor_tensor(out=cv, in0=bb[:], in1=acc, op=ALU.mult), reads=[bbk] + acck, writes=cvk)

    def merge_out(self):
        P = self.P
        t = self.t
        dry = self.dry
        for d in range(16):
            wgc, kgc, hgc = self.wnext(("w_in", 0, 16, OFF_GC + 128 * d, 128))
            wga, kga, hga = self.wnext(("w_in", 0, 16, OFF_GA + 128 * d, 128))
            wco, kco, hco = self.wnext(("w_conv_out", 0, 8, 128 * d, 128))
            wao, kao, hao = self.wnext(("w_attn_out", 0, 4, 128 * d, 128))
            for _one in range(1):
                b1, b1k = self.alloc()
                b2, b2k = self.alloc()
                b3, b3k = self.alloc()
                b4, b4k = self.alloc()
                par = d % 2
                s1, s1k = self.S(16 + 2 * par, 2, f32=True)
                s2, s2k = self.S(20 + 2 * par, 2, f32=True)
                m1, m1k = self.S(24, 2, f32=True)
                m2, m2k = self.S(26, 2, f32=True)
                mg, mgk = self.S(d, 1)
                if dry:
                    continue
                for c in range(16):
                    self.mm(b1[:], wgc[:, c, :], t["hT"][:, c, :], c == 0, c == 15, kgc + [("hT", c)], b1k)
                for c in range(16):
                    self.mm(b2[:], wga[:, c, :], t["hT"][:, c, :], c == 0, c == 15, kga + [("hT", c)], b2k)
                for ch in range(8):
                    cv, cvk = self.S(28 + ch, 1)
                    self.mm(b3[:], wco[:, ch, :], cv, ch == 0, ch == 7, kco + cvk, b3k)
                for jj in range(4):
                    on, onk = self.S(40 + jj, 1)
                    self.mm(b4[:], wao[:, jj, :], on, jj == 0, jj == 3, kao + onk, b4k)
                for hh in (hgc, hga, hco, hao):
                    self.wdone(hh)
                P.op("act", lambda e, b1=b1, s1=s1: e.activation(out=s1, in_=b1[:], func=AF.Sigmoid), reads=[b1k], writes=s1k)
                P.op("act", lambda e, b2=b2, s2=s2: e.activation(out=s2, in_=b2[:], func=AF.Sigmoid), reads=[b2k], writes=s2k)
                P.op("dve", lambda e, b3=b3, s1=s1, m1=m1: e.tensor_tensor(out=m1, in0=b3[:], in1=s1, op=ALU.mult), reads=[b3k] + s1k, writes=m1k)
                P.op("dve", lambda e, b4=b4, s2=s2, m2=m2: e.tensor_tensor(out=m2, in0=b4[:], in1=s2, op=ALU.mult), reads=[b4k] + s2k, writes=m2k)
                P.op("dve", lambda e, m1=m1, m2=m2, mg=mg: e.tensor_tensor(out=mg, in0=m1, in1=m2, op=ALU.add), reads=m1k + m2k, writes=mgk)
        for dg in range(4):
            banks = [self.alloc() for _ in range(4)]
            for half in range(2):
                wo, ko, ho = self.wnext(("w_o", 8 * half, 8, 512 * dg, 512))
                for s in range(4):
                    for r in range(8):
                        dch = 8 * half + r
                        if not dry:
                            mg, mgk = self.S(dch, 1)
                            self.mm(banks[s][0][:], mg[:, s * 128:(s + 1) * 128], wo[:, r, :], dch == 0, dch == 15, mgk + ko, banks[s][1],
                                    signal=(dch == 15 or (s == 3 and r == 7)))
                self.wdone(ho)
            for s in range(4):
                if not dry:
                    xs = t["xt"][:, s, dg * 512:(dg + 1) * 512]
                    P.op("dve", lambda e, b=banks[s][0], xs=xs: e.tensor_tensor(out=xs, in0=b[:], in1=xs, op=ALU.add),
                         reads=[banks[s][1], ("x", s, dg)], writes=[("x", s, dg)])
                    self.sq_partial(s, dg)

    def final(self, T):
        P = self.P
        t = self.t
        if self.dry:
            return
        for s in range(4):
            xs = t["xt"][:, s, :]
            self.stats(s, True, None, None)
            ob, obk = self.S(12 + 8 * s, 8, f32=True)
            P.op("dve", lambda e, xs=xs, s=s, ob=ob: e.scalar_tensor_tensor(out=ob, in0=xs, scalar=t["rstd"][:, s:s + 1], in1=t["gF"][:], op0=ALU.mult, op1=ALU.mult),
                 reads=self.xkeys(s) + [("rstd", s), ("c", "gF")], writes=obk)
            r0 = (T - N_HALO_T) * TT + s * 128
            tok = P.dma("sp", self.dr["y"][r0:r0 + 128, :], ob, self.osem[s], reads=obk)
            self.store_toks.append(tok)
            self.load_x(T + 1, [s])


def _const_inputs(jchunk):
    p = np.arange(128)
    k = p[:, None]
    q = p[None, :]
    same = (k % 4) == (q % 4)
    valid = 1.0 if jchunk > 0 else 0.0
    m = np.zeros((128, 8, 128), np.float32)
    m[:, 0] = (k <= q)
    m[:, 1] = (k >= q)
    m[:, 2] = (k >= q) * valid
    m[:, 3] = same & (k <= q)
    m[:, 4] = same
    m[:, 5] = same & (k >= q)
    m[:, 6] = same * valid
    m[:, 7] = (same & (k >= q)) * valid
    psw = np.zeros((32, 32), np.float32)
    for i in range(16):
        psw[i + 16, i] = 1.0
        psw[i, i + 16] = 1.0
    pos = (jchunk * T_OWN - N_HALO_T * TT + np.arange(T_ALL)).astype(np.float32)
    half = 16
    inv_freq = (np.float32(ROPE_THETA) ** (-(np.arange(half, dtype=np.float32) * np.float32(2.0)) / np.float32(32))).astype(np.float32)
    ang = (pos[:, None] * inv_freq[None, :]).astype(np.float32)
    cos = np.cos(ang).astype(np.float32).T
    sin = np.sin(ang).astype(np.float32).T
    cosd = np.concatenate([cos, cos], 0).reshape(32, NT, TT).transpose(1, 0, 2)
    sind = np.concatenate([-sin, sin], 0).reshape(32, NT, TT).transpose(1, 0, 2)
    return {
        "masks": m.astype(ml_dtypes.bfloat16),
        "ident": np.eye(128, dtype=np.float32).astype(ml_dtypes.bfloat16),
        "ones": np.ones((128, 128), np.float32).astype(ml_dtypes.bfloat16),
        "pswap": psw,
        "cosd": np.ascontiguousarray(cosd, dtype=np.float32),
        "sind": np.ascontiguousarray(sind, dtype=np.float32),
    }


def make_in_maps(inputs):
    x = np.asarray(inputs["x"], dtype=np.float32)
    shared = {}
    for nm in ["w_ffn1_in", "w_ffn1_out", "w_in", "w_conv_out", "w_attn_out", "w_o", "w_ffn2_in", "w_ffn2_out"]:
        shared[nm] = np.ascontiguousarray(np.asarray(inputs[nm], dtype=np.float32)[0])
    gains = np.stack([np.asarray(inputs[k], np.float32)[0].reshape(16, 128).T for k in ["ffn1_norm", "mix_norm", "ffn2_norm"]], axis=1)
    shared["gains"] = np.ascontiguousarray(gains)
    shared["gF"] = np.ascontiguousarray(np.broadcast_to(np.asarray(inputs["final_norm"], np.float32)[None, :], (128, D)))
    wc = np.asarray(inputs["w_conv"], np.float32)[0]
    shared["wcv"] = np.ascontiguousarray(wc.reshape(3, 8, 128).transpose(2, 1, 0))
    in_maps = []
    for c in range(NC):
        b, j = c // 4, c % 4
        xin = np.zeros((T_ALL, D), np.float32)
        xin[N_HALO_T * TT:] = x[b, j * T_OWN:(j + 1) * T_OWN]
        if j > 0:
            xin[:N_HALO_T * TT] = x[b, j * T_OWN - N_HALO_T * TT:j * T_OWN]
        m = dict(shared)
        m["xin"] = xin
        m.update(_const_inputs(j))
        in_maps.append(m)
    return in_maps


_NC_CACHE = {}


def kernel(**inputs):
    if "nc" not in _NC_CACHE:
        _NC_CACHE["nc"] = Builder().build()
    nc = _NC_CACHE["nc"]
    in_maps = make_in_maps(inputs)
    res = run_bass_kernel_spmd(nc, in_maps, core_ids=list(range(NC)))
    out = np.zeros((2, 16384, D), np.float32)
    for c in range(NC):
        b, j = c // 4, c % 4
        out[b, j * T_OWN:(j + 1) * T_OWN] = res.results[c]["y"]
    return out
```

```python
import numpy as np
import ml_dtypes
from contextlib import ExitStack
import concourse.bass as bass
import concourse.mybir as mybir
from concourse.bass_utils import run_bass_kernel_spmd

F32 = mybir.dt.float32
BF16 = mybir.dt.bfloat16
AF = mybir.ActivationFunctionType
ALU = mybir.AluOpType

D = 2048
DFF = 5632
NC = 8
TT = 512
N_HALO_T = 4
N_OWN_T = 8
NT = N_HALO_T + N_OWN_T
T_OWN = N_OWN_T * TT
T_ALL = NT * TT
NGRAN = 12
GRAN = 1024
NWSEM = 8
CAST_LEAD = 120
NCAST = 8
NPS = 8
OFF_Q, OFF_K, OFF_V, OFF_XC, OFF_B, OFF_C, OFF_GC, OFF_GA = 0, 1536, 3072, 4608, 5632, 6656, 7680, 9728
ROPE_THETA = 500000.0
SCALE = 128.0 ** -0.5
EPS = 1e-5


class _Dummy:
    def __getitem__(self, k):
        return self

    def __getattr__(self, k):
        return lambda *a, **kw: self


DUMMY = _Dummy()


class EngW:
    def __init__(self, name, sem, self_sync):
        self.name = name
        self.sem = sem
        self.count = 0
        self.waited = {}
        self.self_sync = self_sync
        self.items = []

    def wait(self, tok):
        sem, val = tok
        if sem is self.sem and not self.self_sync:
            return
        k = id(sem)
        if self.waited.get(k, 0) >= val:
            return
        self.waited[k] = val
        self.items.append(("wait", sem, val))


class Prog:
    def __init__(self, sems):
        self.dry = False
        self.E = {
            "pe": EngW("pe", sems["pe"], False),
            "act": EngW("act", sems["act"], True),
            "dve": EngW("dve", sems["dve"], True),
            "pool": EngW("pool", sems["pool"], True),
            "sp": EngW("sp", sems["sp"], False),
        }
        self.lastw = {}
        self.readers = {}
        self.dma_cnt = {}

    def _deps(self, reads, writes):
        d = []
        for k in reads:
            t = self.lastw.get(k)
            if t is not None:
                d.append(t)
        for k in writes:
            t = self.lastw.get(k)
            if t is not None:
                d.append(t)
            r = self.readers.get(k)
            if r:
                d.extend(r.values())
        return d

    def _record(self, tok, reads, writes):
        for k in reads:
            r = self.readers.setdefault(k, {})
            sid = id(tok[0])
            old = r.get(sid)
            if old is None or old[1] < tok[1]:
                r[sid] = tok
        for k in writes:
            self.lastw[k] = tok
            self.readers[k] = {}

    def op(self, eng, fn, reads=(), writes=(), signal=True):
        if self.dry:
            return
        E = self.E[eng]
        px = [k for k in reads if k[0] == "ps"]
        if px:
            reads = [k for k in reads if k[0] != "ps"]
            writes = list(writes) + px
        for tok in self._deps(reads, writes):
            E.wait(tok)
        if signal:
            E.count += 1
            tok = (E.sem, E.count)
        else:
            tok = (E.sem, E.count + 1)
        E.items.append(("op", fn, signal))
        self._record(tok, reads, writes)

    def dma(self, eng, out_ap, in_ap, sem, reads=(), writes=()):
        if self.dry:
            return None
        E = self.E[eng]
        for tok in self._deps(reads, writes):
            E.wait(tok)
        c = self.dma_cnt.get(id(sem), 0)
        if c > 0:
            E.wait((sem, c))
        c += 16
        self.dma_cnt[id(sem)] = c
        E.items.append(("dma", out_ap, in_ap, sem))
        self._record((sem, c), reads, writes)
        return (sem, c)

    def wait_tok(self, eng, tok):
        if not self.dry and tok is not None:
            self.E[eng].wait(tok)

    def replay(self, eng, h):
        E = self.E[eng]
        for it in E.items:
            if it[0] == "wait":
                h.wait_ge(it[1], it[2])
            elif it[0] == "op":
                ins = it[1](h)
                if it[2]:
                    ins.then_inc(E.sem, 1)
            else:
                h.dma_start(out=it[1], in_=it[2]).then_inc(it[3], 16)


def perm(g, ap):
    if g == 0:
        return ap.rearrange("p (q n) -> p q n", q=4)
    return ap.rearrange("p (n q) -> p q n", q=4)


def blk(ap):
    return ap.rearrange("p (q n) -> p q n", q=4)


class Builder:
    def __init__(self, n_tiles=NT, dbg=None):
        self.n_tiles = n_tiles
        self.dbg = dbg or {}

    def build(self):
        nc = bass.Bass("TRN2", target_bir_lowering=False)
        self.nc = nc
        dr = {}

        def din(name, shape, dt=F32):
            dr[name] = nc.dram_tensor(name, shape, dt, kind="ExternalInput").ap()

        din("xin", [T_ALL, D])
        din("w_ffn1_in", [D, 2 * DFF]); din("w_ffn1_out", [DFF, D])
        din("w_in", [D, 11776]); din("w_conv_out", [1024, D]); din("w_attn_out", [512, D]); din("w_o", [D, D])
        din("w_ffn2_in", [D, 2 * DFF]); din("w_ffn2_out", [DFF, D])
        din("gains", [128, 3, 16]); din("gF", [128, D]); din("wcv", [128, 8, 3])
        din("cosd", [NT, 32, TT]); din("sind", [NT, 32, TT])
        din("masks", [128, 8, 128], BF16); din("ident", [128, 128], BF16); din("ones", [128, 128], BF16)
        din("pswapb", [32, 32], BF16)
        dr["y"] = nc.dram_tensor("y", [T_OWN, D], F32, kind="ExternalOutput").ap()
        for k, shp in self.dbg.items():
            dr[k] = nc.dram_tensor(k, shp, F32, kind="ExternalOutput").ap()
        self.dr = dr

        self.P = Prog({k: None for k in ["pe", "act", "dve", "pool", "sp"]})
        self.P.dry = True
        self.dry = True
        self.useq = []
        self.t = {}
        self.emit_all()
        useq = self.useq
        uniq = []
        seen = {}
        for s in useq:
            if s not in seen:
                seen[s] = len(uniq)
                uniq.append(s)
        self.uidx = seen
        self.uoff = {}
        off = 0
        for sp in uniq:
            self.uoff[sp] = off
            off += 128 * sp[2] * sp[4]
        dr["scr"] = nc.dram_tensor("scr", [off], BF16, kind="Internal").ap()

        with ExitStack() as es:
            def sb(name, shape, dt):
                return es.enter_context(nc.sbuf_tensor("sb_" + name, shape, dt))

            def sem(name):
                return es.enter_context(nc.semaphore(name))

            t = {}
            t["xt"] = sb("xt", [128, 4, D], F32)
            t["hT"] = sb("hT", [128, 16, TT], BF16)
            t["sB"] = sb("sB", [128, 44 * 512], BF16)
            for g, ns in enumerate([2, 2, 5]):
                t[f"KT{g}"] = sb(f"KT{g}", [128, ns, 4, TT], BF16)
                t[f"V{g}"] = sb(f"V{g}", [128, ns, 4, TT], BF16)
            t["wring"] = sb("wring", [128, NGRAN * GRAN], BF16)
            t["gF"] = sb("gF", [128, D], F32)
            t["cosT"] = sb("cosT", [32, TT], F32)
            t["sinT"] = sb("sinT", [32, TT], F32)
            t["sgt"] = sb("sgt", [128, TT], F32)
            t["uh"] = sb("uh", [128, 8, 2], F32)
            t["masks"] = sb("masks_sb", [128, 8, 128], BF16)
            t["ident"] = sb("ident_sb", [128, 128], BF16)
            t["ones"] = sb("ones_sb", [128, 128], BF16)
            t["pswapb"] = sb("pswapb_sb", [32, 32], BF16)
            t["gains"] = sb("gains_sb", [128, 3, 16], F32)
            t["wcv"] = sb("wcv_sb", [128, 8, 3], F32)
            t["ss"] = sb("ss", [128, 4], F32)
            t["ssp"] = sb("ssp", [128, 16], F32)
            t["sq"] = sb("sq", [128, 4], F32)
            t["rstd"] = sb("rstd", [128, 4], F32)
            t["eps"] = sb("eps_sb", [128, 1], F32)
            if self.dbg:
                t["dtmp"] = sb("dtmp", [128, TT], F32)
            t["pb"] = [es.enter_context(nc.psum_tensor(f"pb{i}", [128, 512], F32)) for i in range(NPS)]
            self.t = t
            sems = {k: sem("s_" + k) for k in ["pe", "act", "dve", "pool", "sp"]}
            self.wsem = [sem(f"w{i}") for i in range(NWSEM)]
            self.csem = [sem(f"c{i}") for i in range(NCAST)]
            self.xsem = [sem(f"x{i}") for i in range(4)]
            self.osem = [sem(f"o{i}") for i in range(4)]
            self.msem = [sem(f"m{i}") for i in range(12)]
            self.dsem = [sem(f"dbg{i}") for i in range(4)]
            self.P = Prog(sems)
            self.dry = False
            self.emit_all()
            P = self.P
            with nc.Block() as block:
                @block.tensor
                def _(h):
                    P.replay("pe", h)

                @block.scalar
                def _(h):
                    P.replay("act", h)

                @block.vector
                def _(h):
                    P.replay("dve", h)

                @block.gpsimd
                def _(h):
                    P.replay("pool", h)

                @block.sync
                def _(h):
                    P.replay("sp", h)
        return nc

    def S(self, a, n=1, f32=False):
        keys = [("sB", a + i) for i in range(n)]
        if self.dry:
            return DUMMY, keys
        ap = self.t["sB"][:, a * 512:(a + n) * 512]
        if f32:
            ap = ap.bitcast(F32)
        return ap, keys

    def alloc(self):
        i = self.psn % NPS
        self.psn += 1
        if self.dry:
            return DUMMY, ("ps", i)
        return self.t["pb"][i], ("ps", i)

    def wnext(self, spec):
        if self.dry:
            self.useq.append(spec)
            return DUMMY, [("wr", 0)], None
        n = self.wpos
        self.wpos += 1
        assert self.useq[n] == spec, (n, self.useq[n], spec)
        self.prefetch()
        assert self.wissued > n, ("weight ring too small / unit not released", n, spec, self.wactive)
        g0, ng = self.wplace[n]
        nr, ncols = spec[2], spec[4]
        ap = self.t["wring"][:, g0 * GRAN:g0 * GRAN + nr * ncols].rearrange("p (r n) -> p r n", r=nr)
        return ap, [("wr", g0 + i) for i in range(ng)], n

    def wdone(self, h):
        if self.dry:
            return
        self.wactive = [a for a in self.wactive if a[0] != h]
        self.prefetch()

    def prefetch(self):
        P = self.P
        while self.wissued < len(self.useq):
            m = self.wissued
            spec = self.useq[m]
            name, r0, nr, c0, ncols = spec
            ng = -(-(nr * ncols) // GRAN)
            cand = [self.whead] if self.whead + ng <= NGRAN else []
            cand.append(0)
            g0 = None
            for c in cand:
                if all(c + ng <= a[1] or c >= a[1] + a[2] for a in self.wactive):
                    g0 = c
                    break
            if g0 is None:
                return
            self.whead = g0 + ng
            self.wactive.append((m, g0, ng))
            self.wplace[m] = (g0, ng)
            sid = self.uidx[spec]
            o0 = self.uoff[spec]
            scr_v = self.dr["scr"][o0:o0 + 128 * nr * ncols].rearrange("(p r n) -> p r n", p=128, r=nr)
            if sid not in self.casted:
                self.casted.add(sid)
                src = self.dr[name][r0 * 128:(r0 + nr) * 128, c0:c0 + ncols].rearrange("(r p) n -> p r n", p=128)
                k = self.ncast % NCAST
                self.ncast += 1
                if m >= CAST_LEAD:
                    P.wait_tok("pool", self.wtok[m - CAST_LEAD])
                P.dma("pool", scr_v, src, self.csem[k], writes=[("scr", sid)])
            dst = self.t["wring"][:, g0 * GRAN:g0 * GRAN + nr * ncols].rearrange("p (r n) -> p r n", r=nr)
            k = self.nwdma % NWSEM
            self.nwdma += 1
            self.wtok[m] = P.dma("sp", dst, scr_v, self.wsem[k], reads=[("scr", sid)], writes=[("wr", g0 + i) for i in range(ng)])
            self.wissued += 1

    def mm(self, out_ap, lhsT, rhs, start, stop, reads, bank, signal=None):
        sig = stop if signal is None else signal
        self.P.op("pe", lambda e: e.matmul(out_ap, lhsT=lhsT, rhs=rhs, start=start, stop=stop),
                  reads=reads, writes=[bank], signal=sig)

    def xkeys(self, s, dgs=range(4)):
        return [("x", s, dg) for dg in dgs]

    def emit_all(self):
        P = self.P
        t = self.t if not self.dry else None
        self.psn = 0
        self.wpos = 0
        self.wissued = 0
        self.whead = 0
        self.wactive = []
        self.wplace = {}
        self.wtok = {}
        self.nwdma = 0
        self.casted = set()
        self.ncast = 0
        self.store_toks = []
        dr = self.dr
        if not self.dry:
            for i, (nm, dst) in enumerate([("gains", t["gains"]), ("gF", t["gF"]), ("wcv", t["wcv"]), ("masks", t["masks"]),
                                           ("ident", t["ident"]), ("ones", t["ones"]), ("pswapb", t["pswapb"])]):
                P.dma("sp", dst[:], dr[nm], self.msem[i], writes=[("c", nm)])
            P.op("dve", lambda e: e.memset(t["eps"][:], EPS), writes=[("c", "eps")])
            P.op("dve", lambda e: e.memset(t["uh"][:], 0.0), writes=[("uh", ch) for ch in range(8)])
        for T in range(self.n_tiles):
            own = T >= N_HALO_T
            if T == 0:
                self.load_x(T)
            self.ffn(T, 0)
            self.mixer(T, own)
            if own:
                self.ffn(T, 2)
                self.final(T)
        if not self.dry:
            for tok in self.store_toks:
                P.wait_tok("sp", tok)

    def load_x(self, T, subs=range(4)):
        if self.dry or T >= self.n_tiles:
            return
        for s in subs:
            self.P.dma("sp", self.t["xt"][:, s, :], self.dr["xin"][T * TT + s * 128:T * TT + (s + 1) * 128, :],
                       self.xsem[s], writes=self.xkeys(s))

    def dump(self, name, T, src_ap, keys, rows=None):
        if self.dry or name not in self.dbg:
            return
        k = self.ndbg = getattr(self, "ndbg", 0) + 1
        tok = self.P.dma("sp", rows, src_ap, self.dsem[k % 4], reads=keys)
        self.store_toks.append(tok)

    def dump_any(self, name, row0, src_ap, keys, eng="dve"):
        if self.dry or name not in self.dbg:
            return
        P = self.P
        t = self.t
        P.op(eng, (lambda e: e.tensor_copy(out=t["dtmp"][:], in_=src_ap)) if eng == "dve" else (lambda e: e.activation(out=t["dtmp"][:], in_=src_ap, func=AF.Copy)),
             reads=keys, writes=[("dtmp",)])
        k = self.ndbg = getattr(self, "ndbg", 0) + 1
        tok = P.dma("sp", self.dr[name][row0:row0 + 128, :], t["dtmp"][:], self.dsem[k % 4], reads=[("dtmp",)])
        self.store_toks.append(tok)

    def stats(self, s, partial, junk, junkk):
        P = self.P
        t = self.t
        xs = t["xt"][:, s, :]
        if partial:
            P.op("dve", lambda e: e.reduce_sum(out=t["ss"][:, s:s + 1], in_=t["ssp"][:, 4 * s:4 * s + 4], axis=mybir.AxisListType.X),
                 reads=[("ssp", s, dg) for dg in range(4)], writes=[("ss", s)])
        else:
            P.op("act", lambda e: e.activation(out=junk, in_=xs, func=AF.Square, accum_out=t["ss"][:, s:s + 1]),
                 reads=self.xkeys(s), writes=junkk + [("ss", s)])
        P.op("act", lambda e: e.activation(out=t["sq"][:, s:s + 1], in_=t["ss"][:, s:s + 1], func=AF.Sqrt, scale=1.0 / D, bias=t["eps"][:]),
             reads=[("ss", s), ("c", "eps")], writes=[("sq", s)])
        P.op("dve", lambda e: e.reciprocal(out=t["rstd"][:, s:s + 1], in_=t["sq"][:, s:s + 1]),
             reads=[("sq", s)], writes=[("rstd", s)])

    def sq_partial(self, s, dg):
        P = self.P
        t = self.t
        xs = t["xt"][:, s, dg * 512:(dg + 1) * 512]
        P.op("act", lambda e: e.activation(out=t["sgt"][:], in_=xs, func=AF.Square, accum_out=t["ssp"][:, 4 * s + dg:4 * s + dg + 1]),
             reads=[("x", s, dg)], writes=[("sgt",), ("ssp", s, dg)])

    def norm_T(self, gi, partial=False):
        P = self.P
        t = self.t
        dry = self.dry
        for s in range(4):
            hn, hk = self.S(4 * s, 4)
            if not dry:
                xs = t["xt"][:, s, :]
                self.stats(s, partial, hn, hk)
                P.op("dve", lambda e, hn=hn, xs=xs, s=s: e.tensor_scalar(out=hn, in0=xs, scalar1=t["rstd"][:, s:s + 1], scalar2=0.0, op0=ALU.mult, op1=ALU.add),
                     reads=self.xkeys(s) + [("rstd", s)], writes=hk)
        for c in range(16):
            bank, bk = self.alloc()
            if dry:
                continue
            bb = bank[:].bitcast(BF16)
            for s in range(4):
                hn, hk = self.S(4 * s, 4)
                P.op("pe", lambda e, bb=bb, hn=hn, s=s, c=c: e.transpose(out=bb[:, s * 128:(s + 1) * 128], in_=hn[:, c * 128:(c + 1) * 128], identity=t["ident"][:]),
                     reads=hk + [("c", "ident")], writes=[bk], signal=(s == 3))
            eng = "act" if c % 2 == 0 else "dve"
            if eng == "act":
                P.op("act", lambda e, bb=bb, c=c: e.activation(out=t["hT"][:, c, :], in_=bb[:, 0:TT], func=AF.Copy, scale=t["gains"][:, gi, c:c + 1]),
                     reads=[bk, ("c", "gains")], writes=[("hT", c)])
            else:
                P.op("dve", lambda e, bb=bb, c=c: e.tensor_scalar(out=t["hT"][:, c, :], in0=bb[:, 0:TT], scalar1=t["gains"][:, gi, c:c + 1], scalar2=0.0, op0=ALU.mult, op1=ALU.add),
                     reads=[bk, ("c", "gains")], writes=[("hT", c)])

    def ffn(self, T, gi):
        P = self.P
        t = self.t
        dry = self.dry
        w_in = "w_ffn1_in" if gi == 0 else "w_ffn2_in"
        w_out = "w_ffn1_out" if gi == 0 else "w_ffn2_out"
        self.norm_T(gi, partial=(gi == 2))
        if gi == 0 and T == 0 and not dry:
            for c in range(16):
                self.dump_any("d_hT", c * 128, t["hT"][:, c, :], [("hT", c)])
        for f in range(44):
            wg, kg, hg = self.wnext((w_in, 0, 16, 128 * f, 128))
            wu, ku, hu = self.wnext((w_in, 0, 16, DFF + 128 * f, 128))
            bg, bgk = self.alloc()
            for c in range(16):
                if not dry:
                    self.mm(bg[:], wg[:, c, :], t["hT"][:, c, :], c == 0, c == 15, kg + [("hT", c)], bgk)
            self.wdone(hg)
            bu, buk = self.alloc()
            for c in range(16):
                if not dry:
                    self.mm(bu[:], wu[:, c, :], t["hT"][:, c, :], c == 0, c == 15, ku + [("hT", c)], buk)
            self.wdone(hu)
            a, ak = self.S(f, 1)
            if not dry and gi == 0 and T == 0 and f in (0, 1, 43):
                fi = (0, 1, 43).index(f)
                self.dump_any("d_gate", fi * 128, bg[:], [bgk])
                self.dump_any("d_up", fi * 128, bu[:], [buk])
            if not dry:
                P.op("act", lambda e, bg=bg: e.activation(out=t["sgt"][:], in_=bg[:], func=AF.Silu), reads=[bgk], writes=[("sgt",)])
                P.op("dve", lambda e, bu=bu, a=a: e.tensor_tensor(out=a, in0=bu[:], in1=t["sgt"][:], op=ALU.mult), reads=[buk, ("sgt",)], writes=ak)
        if gi == 0 and T == 0 and not dry:
            for fi, f in enumerate((0, 1, 43)):
                a, ak = self.S(f, 1)
                self.dump_any("d_act", fi * 128, a, ak)
        fgs = [(0, 8), (8, 8), (16, 8), (24, 8), (32, 8), (40, 4)]
        for dg in range(4):
            banks = [self.alloc() for _ in range(4)]
            for (f0, nf) in fgs:
                wo, ko, ho = self.wnext((w_out, f0, nf, 512 * dg, 512))
                for s in range(4):
                    for r in range(nf):
                        f = f0 + r
                        if not dry:
                            a, ak = self.S(f, 1)
                            last_use = (s == 3 and r == nf - 1)
                            self.mm(banks[s][0][:], a[:, s * 128:(s + 1) * 128], wo[:, r, :], f == 0, f == 43, ak + ko, banks[s][1],
                                    signal=(f == 43 or last_use))
                self.wdone(ho)
            for s in range(4):
                if not dry:
                    xs = t["xt"][:, s, dg * 512:(dg + 1) * 512]
                    P.op("dve", lambda e, b=banks[s][0], xs=xs: e.scalar_tensor_tensor(out=xs, in0=b[:], scalar=0.5, in1=xs, op0=ALU.mult, op1=ALU.add),
                         reads=[banks[s][1], ("x", s, dg)], writes=[("x", s, dg)])
                    self.sq_partial(s, dg)
        if "x1" in self.dbg and gi == 0 and not dry:
            for s in range(4):
                self.dump("x1", T, t["xt"][:, s, :], self.xkeys(s), rows=self.dr["x1"][T * TT + s * 128:T * TT + (s + 1) * 128, :])

    def rope_A(self, w, wk, i, g, dst, dstk, wh=None):
        P = self.P
        t = self.t
        dry = self.dry
        bank, bk = self.alloc()
        for c in range(16):
            if not dry:
                self.mm(bank[:], w[:, c, i * 128:(i + 1) * 128], t["hT"][:, c, :], c == 0, c == 15, wk + [("hT", c)], bk)
        if wh is not None:
            self.wdone(wh)
        par = self.rope_par = 1 - getattr(self, "rope_par", 0)
        r32, rk = self.S(28 + 2 * par, 2, f32=True)
        r16, r16k = self.S(36 + par, 1)
        if not dry:
            P.op("act", lambda e: e.activation(out=blk(dst), in_=perm(g, bank[:]), func=AF.Copy), reads=[bk], writes=dstk)
            P.op("act", lambda e: e.activation(out=r16[0:32, :], in_=bank[0:32, :], func=AF.Copy), reads=[bk], writes=r16k)
            P.op("act", lambda e: e.activation(out=r32[0:32, :], in_=bank[0:32, :], func=AF.Copy), reads=[bk], writes=rk)
        return (r32, rk, dst, dstk, g, r16, r16k)

    def rope_B(self, st):
        if st is None:
            return
        P = self.P
        t = self.t
        r32, rk, dst, dstk, g, r16, r16k = st
        t1, t1k = self.S(32, 2, f32=True)
        t2, t2k = self.S(34, 2, f32=True)
        sw, swk = self.alloc()
        if self.dry:
            return
        self.mm(sw[0:32, :], t["pswapb"][:], r16[0:32, :], True, True, r16k + [("c", "pswapb")], swk)
        P.op("dve", lambda e: e.tensor_tensor(out=t1[0:32, :], in0=r32[0:32, :], in1=t["cosT"][:], op=ALU.mult), reads=rk + [("tab",)], writes=t1k)
        P.op("dve", lambda e: e.tensor_tensor(out=t2[0:32, :], in0=sw[0:32, :], in1=t["sinT"][:], op=ALU.mult), reads=[swk, ("tab",)], writes=t2k)
        P.op("dve", lambda e: e.tensor_tensor(out=blk(dst[0:32, :]), in0=perm(g, t1[0:32, :]), in1=perm(g, t2[0:32, :]), op=ALU.add),
             reads=t1k + t2k, writes=dstk)

    def mixer(self, T, own):
        P = self.P
        t = self.t
        dry = self.dry
        dr = self.dr
        sl = [T % 2, T % 2, T % 5]
        last_halo = (T == N_HALO_T - 1)
        if not dry:
            P.dma("sp", t["cosT"][:], dr["cosd"][T], self.msem[8], writes=[("tab",)])
            P.dma("sp", t["sinT"][:], dr["sind"][T], self.msem[9], writes=[("tab",)])
        self.norm_T(1, partial=True)
        if not own:
            self.load_x(T + 1)
        pending = None
        kv_groups = (0, 1, 2) if (own or last_halo) else (2,)
        if own:
            for u in range(6):
                w, wk, wh = self.wnext(("w_in", 0, 16, OFF_Q + 256 * u, 256))
                for i in range(2):
                    h = 2 * u + i
                    dst, dstk = self.S(16 + h, 1)
                    st = self.rope_A(w, wk, i, h // 4, dst, dstk, wh if i == 1 else None)
                    self.rope_B(pending)
                    pending = st
        for u in range(6):
            if u // 2 not in kv_groups:
                continue
            w, wk, wh = self.wnext(("w_in", 0, 16, OFF_K + 256 * u, 256))
            for i in range(2):
                h = 2 * u + i
                g, j = h // 4, h % 4
                dst = DUMMY if dry else t[f"KT{g}"][:, sl[g], j, :]
                st = self.rope_A(w, wk, i, g, dst, [("KT", g, sl[g], j)], wh if i == 1 else None)
                self.rope_B(pending)
                pending = st
        for u in range(6):
            if u // 2 not in kv_groups:
                continue
            w, wk, wh = self.wnext(("w_in", 0, 16, OFF_V + 256 * u, 256))
            g, half = u // 2, u % 2
            for qbp in range(2):
                bank, bk = self.alloc()
                for qq in range(2):
                    qb = 2 * qbp + qq
                    for c in range(16):
                        if not dry:
                            hc = t["hT"][:, c, :]
                            lhsT = hc[:, qb * 128:(qb + 1) * 128] if g == 0 else hc[:, qb:TT:4]
                            self.mm(bank[:, qq * 256:(qq + 1) * 256], lhsT, w[:, c, :], c == 0, c == 15, wk + [("hT", c)], bk)
                if not dry:
                    dst = t[f"V{g}"][:, sl[g], 2 * qbp:2 * qbp + 2, half * 256:(half + 1) * 256]
                    P.op("act", lambda e, dst=dst, bank=bank: e.activation(out=dst, in_=bank[:].rearrange("p (q n) -> p q n", q=2), func=AF.Copy),
                         reads=[bk], writes=[("V", g, sl[g], qbp, half)])
                if pending is not None:
                    self.rope_B(pending)
                    pending = None
            self.wdone(wh)
        if not own:
            if last_halo:
                self.conv_halo()
            return
        self.attention(T, sl)
        self.conv()
        self.merge_out()
        if "x2" in self.dbg and not dry:
            for s in range(4):
                self.dump("x2", T, t["xt"][:, s, :], self.xkeys(s), rows=dr["x2"][T * TT + s * 128:T * TT + (s + 1) * 128, :])

    def attention(self, T, sl):
        pend = None
        idx = 0
        for j in range(4):
            for g in range(3):
                st = self.attn_A(T, sl, j, g, idx % 2)
                if pend is not None:
                    self.attn_B(pend)
                pend = st
                idx += 1
        self.attn_B(pend)

    def attn_A(self, T, sl, j, g, pbuf):
        P = self.P
        t = self.t
        dry = self.dry
        halo = lambda TT_: TT_ < N_HALO_T
        h = 4 * g + j
        QT, QTk = self.S(16 + h, 1)
        deltas = []
        if g == 0:
            own_l, prev_l = [], []
            for qb in range(4):
                own_l.append((sl[0], qb))
                prev_l.append((sl[0], qb - 1) if qb > 0 else ((T - 1) % 2, 3))
            deltas.append((own_l, [0] * 4))
            deltas.append((prev_l, [2 if (halo(T - 1)) else 1] + [1] * 3))
        elif g == 1:
            deltas.append(([(sl[1], qb) for qb in range(4)], [0] * 4))
            deltas.append(([((T - 1) % 2, qb) for qb in range(4)], [2 if halo(T - 1) else 1] * 4))
        else:
            for dl in range(5):
                Tk = T - dl
                if dl == 0:
                    mi = 3
                elif dl < 4:
                    mi = 6 if halo(Tk) else 4
                else:
                    mi = 7 if halo(Tk) else 5
                deltas.append(([(Tk % 5, qb) for qb in range(4)], [mi] * 4))
        pts = []
        for di, (blks, mis) in enumerate(deltas):
            sc, sck = self.alloc()
            pt, ptk = self.S(5 * pbuf + di, 1)
            pts.append((pt, ptk))
            if dry:
                continue
            for qb, (slot, kb) in enumerate(blks):
                self.mm(sc[:, qb * 128:(qb + 1) * 128], t[f"KT{g}"][:, slot, j, kb * 128:(kb + 1) * 128], QT[:, qb * 128:(qb + 1) * 128],
                        True, True, [("KT", g, slot, j)] + QTk, sck, signal=(qb == 3))
            P.op("act", lambda e, pt=pt, sc=sc: e.activation(out=pt, in_=sc[:], func=AF.Exp, scale=SCALE), reads=[sck], writes=ptk)
            if len(set(mis)) == 1:
                mi = mis[0]
                P.op("dve", lambda e, pt=pt, mi=mi: e.tensor_tensor(out=blk(pt), in0=blk(pt), in1=t["masks"][:, mi, :].unsqueeze(1).to_broadcast([128, 4, 128]), op=ALU.mult),
                     reads=ptk + [("c", "masks")], writes=ptk)
            else:
                mi0, mi = mis[0], mis[1]
                P.op("dve", lambda e, pt=pt, mi0=mi0: e.tensor_tensor(out=pt[:, 0:128], in0=pt[:, 0:128], in1=t["masks"][:, mi0, :], op=ALU.mult),
                     reads=ptk + [("c", "masks")], writes=ptk)
                P.op("dve", lambda e, pt=pt, mi=mi: e.tensor_tensor(out=pt[:, 128:512].rearrange("p (q n) -> p q n", q=3), in0=pt[:, 128:512].rearrange("p (q n) -> p q n", q=3),
                                                                    in1=t["masks"][:, mi, :].unsqueeze(1).to_broadcast([128, 3, 128]), op=ALU.mult),
                     reads=ptk + [("c", "masks")], writes=ptk)
        return (j, g, deltas, pts)

    def attn_B(self, st):
        P = self.P
        t = self.t
        dry = self.dry
        j, g, deltas, pts = st
        buf = j % 2
        OT, OTk = self.S(28 + 2 * buf, 2, f32=True)
        LB, LBk = self.S(32 + 2 * buf, 2, f32=True)
        ob, obk = self.alloc()
        lb, lbk = self.alloc()
        nd = len(deltas)
        if not dry:
            for qb in range(4):
                for di, (blks, mis) in enumerate(deltas):
                    slot, kb = blks[qb]
                    pt, ptk = pts[di]
                    self.mm(ob[:, qb * 128:(qb + 1) * 128], t[f"V{g}"][:, slot, kb, j * 128:(j + 1) * 128], pt[:, qb * 128:(qb + 1) * 128],
                            di == 0, di == nd - 1, [("V", g, slot, kb // 2, j // 2)] + ptk, obk, signal=(qb == 3 and di == nd - 1))
            for di in range(nd):
                pt, ptk = pts[di]
                self.mm(lb[:], t["ones"][:], pt, di == 0, di == nd - 1, ptk + [("c", "ones")], lbk)
            if g == 0:
                P.op("dve", lambda e: e.tensor_copy(out=OT, in_=ob[:]), reads=[obk], writes=OTk)
                P.op("dve", lambda e: e.tensor_copy(out=LB, in_=lb[:]), reads=[lbk], writes=LBk)
            else:
                P.op("dve", lambda e: e.tensor_tensor(out=perm(g, OT), in0=blk(ob[:]), in1=perm(g, OT), op=ALU.add), reads=[obk] + OTk, writes=OTk)
                P.op("dve", lambda e: e.tensor_tensor(out=perm(g, LB), in0=blk(lb[:]), in1=perm(g, LB), op=ALU.add), reads=[lbk] + LBk, writes=LBk)
        if g == 2:
            rc, rck = self.S(10, 2, f32=True)
            on, onk = self.S(40 + j, 1)
            if not dry:
                P.op("dve", lambda e: e.reciprocal(out=rc, in_=LB), reads=LBk, writes=rck)
                P.op("dve", lambda e: e.tensor_tensor(out=on, in0=OT, in1=rc, op=ALU.mult), reads=OTk + rck, writes=onk)

    def conv_halo(self):
        P = self.P
        t = self.t
        dry = self.dry
        for ch in range(8):
            wx, wxk, hx = self.wnext(("w_in", 0, 16, OFF_XC + 128 * ch, 128))
            wc, wck, hc = self.wnext(("w_in", 0, 16, OFF_C + 128 * ch, 128))
            bx, bxk = self.alloc()
            bc, bck = self.alloc()
            xs, xsk = self.S(36, 2, f32=True)
            if not dry:
                for c in range(16):
                    self.mm(bx[:, 0:2], wx[:, c, :], t["hT"][:, c, TT - 2:TT], c == 0, c == 15, wxk + [("hT", c)], bxk)
                for c in range(16):
                    self.mm(bc[:, 0:2], wc[:, c, :], t["hT"][:, c, TT - 2:TT], c == 0, c == 15, wck + [("hT", c)], bck)
                P.op("act", lambda e, bx=bx, xs=xs: e.activation(out=xs[:, 0:2], in_=bx[:, 0:2], func=AF.Copy), reads=[bxk], writes=xsk)
                P.op("dve", lambda e, bc=bc, xs=xs, ch=ch: e.tensor_tensor(out=t["uh"][:, ch, :], in0=bc[:, 0:2], in1=xs[:, 0:2], op=ALU.mult),
                     reads=[bck] + xsk, writes=[("uh", ch)])
            self.wdone(hx)
            self.wdone(hc)

    def conv(self):
        P = self.P
        t = self.t
        dry = self.dry
        for ch in range(8):
            wx, wxk, hx = self.wnext(("w_in", 0, 16, OFF_XC + 128 * ch, 128))
            wc, wck, hc = self.wnext(("w_in", 0, 16, OFF_C + 128 * ch, 128))
            wb, wbk, hb = self.wnext(("w_in", 0, 16, OFF_B + 128 * ch, 128))
            for _one in range(1):
                bx, bxk = self.alloc()
                bc, bck = self.alloc()
                bb, bbk = self.alloc()
                xs, xsk = self.S(36, 2, f32=True)
                acc, acck = self.S(38, 2, f32=True)
                ub, ubk = self.S(8, 3, f32=True)
                cv, cvk = self.S(28 + ch, 1)
                if dry:
                    continue
                for (b, bk_, w_, wk_) in ((bx, bxk, wx, wxk), (bc, bck, wc, wck), (bb, bbk, wb, wbk)):
                    for c in range(16):
                        self.mm(b[:], w_[:, c, :], t["hT"][:, c, :], c == 0, c == 15, wk_ + [("hT", c)], bk_)
                self.wdone(hx)
                self.wdone(hc)
                self.wdone(hb)
                cw = t["wcv"]
                P.op("act", lambda e, bx=bx, xs=xs: e.activation(out=xs, in_=bx[:], func=AF.Copy), reads=[bxk], writes=xsk)
                P.op("dve", lambda e, ub=ub, ch=ch: e.tensor_copy(out=ub[:, 0:2], in_=t["uh"][:, ch, :]), reads=[("uh", ch)], writes=ubk)
                P.op("dve", lambda e, ub=ub, bc=bc, xs=xs: e.tensor_tensor(out=ub[:, 2:514], in0=bc[:], in1=xs, op=ALU.mult), reads=[bck] + xsk, writes=ubk)
                P.op("dve", lambda e, ub=ub, ch=ch: e.tensor_copy(out=t["uh"][:, ch, :], in_=ub[:, 512:514]), reads=ubk, writes=[("uh", ch)])
                P.op("dve", lambda e, ub=ub, acc=acc, ch=ch: e.tensor_scalar(out=acc, in0=ub[:, 2:514], scalar1=cw[:, ch, 2:3], scalar2=0.0, op0=ALU.mult, op1=ALU.add),
                     reads=ubk + [("c", "wcv")], writes=acck)
                P.op("dve", lambda e, ub=ub, acc=acc, ch=ch: e.scalar_tensor_tensor(out=acc, in0=ub[:, 1:513], scalar=cw[:, ch, 1:2], in1=acc, op0=ALU.mult, op1=ALU.add),
                     reads=ubk + acck + [("c", "wcv")], writes=acck)
                P.op("dve", lambda e, ub=ub, acc=acc, ch=ch: e.scalar_tensor_tensor(out=acc, in0=ub[:, 0:512], scalar=cw[:, ch, 0:1], in1=acc, op0=ALU.mult, op1=ALU.add),
                     reads=ubk + acck + [("c", "wcv")], writes=acck)
                P.op("dve", lambda e, bb=bb, acc=acc, cv=cv: e.tensor_tensor(out=cv, in0=bb[:], in1=acc, op=ALU.mult), reads=[bbk] + acck, writes=cvk)

    def merge_out(self):
        P = self.P
        t = self.t
        dry = self.dry
        for d in range(16):
            wgc, kgc, hgc = self.wnext(("w_in", 0, 16, OFF_GC + 128 * d, 128))
            wga, kga, hga = self.wnext(("w_in", 0, 16, OFF_GA + 128 * d, 128))
            wco, kco, hco = self.wnext(("w_conv_out", 0, 8, 128 * d, 128))
            wao, kao, hao = self.wnext(("w_attn_out", 0, 4, 128 * d, 128))
            for _one in range(1):
                b1, b1k = self.alloc()
                b2, b2k = self.alloc()
                b3, b3k = self.alloc()
                b4, b4k = self.alloc()
                par = d % 2
                s1, s1k = self.S(16 + 2 * par, 2, f32=True)
                s2, s2k = self.S(20 + 2 * par, 2, f32=True)
                m1, m1k = self.S(24, 2, f32=True)
                m2, m2k = self.S(26, 2, f32=True)
                mg, mgk = self.S(d, 1)
                if dry:
                    continue
                for c in range(16):
                    self.mm(b1[:], wgc[:, c, :], t["hT"][:, c, :], c == 0, c == 15, kgc + [("hT", c)], b1k)
                for c in range(16):
                    self.mm(b2[:], wga[:, c, :], t["hT"][:, c, :], c == 0, c == 15, kga + [("hT", c)], b2k)
                for ch in range(8):
                    cv, cvk = self.S(28 + ch, 1)
                    self.mm(b3[:], wco[:, ch, :], cv, ch == 0, ch == 7, kco + cvk, b3k)
                for jj in range(4):
                    on, onk = self.S(40 + jj, 1)
                    self.mm(b4[:], wao[:, jj, :], on, jj == 0, jj == 3, kao + onk, b4k)
                for hh in (hgc, hga, hco, hao):
                    self.wdone(hh)
                P.op("act", lambda e, b1=b1, s1=s1: e.activation(out=s1, in_=b1[:], func=AF.Sigmoid), reads=[b1k], writes=s1k)
                P.op("act", lambda e, b2=b2, s2=s2: e.activation(out=s2, in_=b2[:], func=AF.Sigmoid), reads=[b2k], writes=s2k)
                P.op("dve", lambda e, b3=b3, s1=s1, m1=m1: e.tensor_tensor(out=m1, in0=b3[:], in1=s1, op=ALU.mult), reads=[b3k] + s1k, writes=m1k)
                P.op("dve", lambda e, b4=b4, s2=s2, m2=m2: e.tensor_tensor(out=m2, in0=b4[:], in1=s2, op=ALU.mult), reads=[b4k] + s2k, writes=m2k)
                P.op("dve", lambda e, m1=m1, m2=m2, mg=mg: e.tensor_tensor(out=mg, in0=m1, in1=m2, op=ALU.add), reads=m1k + m2k, writes=mgk)
        for dg in range(4):
            banks = [self.alloc() for _ in range(4)]
            for half in range(2):
                wo, ko, ho = self.wnext(("w_o", 8 * half, 8, 512 * dg, 512))
                for s in range(4):
                    for r in range(8):
                        dch = 8 * half + r
                        if not dry:
                            mg, mgk = self.S(dch, 1)
                            self.mm(banks[s][0][:], mg[:, s * 128:(s + 1) * 128], wo[:, r, :], dch == 0, dch == 15, mgk + ko, banks[s][1],
                                    signal=(dch == 15 or (s == 3 and r == 7)))
                self.wdone(ho)
            for s in range(4):
                if not dry:
                    xs = t["xt"][:, s, dg * 512:(dg + 1) * 512]
                    P.op("dve", lambda e, b=banks[s][0], xs=xs: e.tensor_tensor(out=xs, in0=b[:], in1=xs, op=ALU.add),
                         reads=[banks[s][1], ("x", s, dg)], writes=[("x", s, dg)])
                    self.sq_partial(s, dg)

    def final(self, T):
        P = self.P
        t = self.t
        if self.dry:
            return
        for s in range(4):
            xs = t["xt"][:, s, :]
            self.stats(s, True, None, None)
            ob, obk = self.S(12 + 8 * s, 8, f32=True)
            P.op("dve", lambda e, xs=xs, s=s, ob=ob: e.scalar_tensor_tensor(out=ob, in0=xs, scalar=t["rstd"][:, s:s + 1], in1=t["gF"][:], op0=ALU.mult, op1=ALU.mult),
                 reads=self.xkeys(s) + [("rstd", s), ("c", "gF")], writes=obk)
            r0 = (T - N_HALO_T) * TT + s * 128
            tok = P.dma("sp", self.dr["y"][r0:r0 + 128, :], ob, self.osem[s], reads=obk)
            self.store_toks.append(tok)
            self.load_x(T + 1, [s])


def _const_inputs(jchunk):
    p = np.arange(128)
    k = p[:, None]
    q = p[None, :]
    same = (k % 4) == (q % 4)
    valid = 1.0 if jchunk > 0 else 0.0
    m = np.zeros((128, 8, 128), np.float32)
    m[:, 0] = (k <= q)
    m[:, 1] = (k >= q)
    m[:, 2] = (k >= q) * valid
    m[:, 3] = same & (k <= q)
    m[:, 4] = same
    m[:, 5] = same & (k >= q)
    m[:, 6] = same * valid
    m[:, 7] = (same & (k >= q)) * valid
    psw = np.zeros((32, 32), np.float32)
    for i in range(16):
        psw[i + 16, i] = 1.0
        psw[i, i + 16] = 1.0
    pos = (jchunk * T_OWN - N_HALO_T * TT + np.arange(T_ALL)).astype(np.float32)
    half = 16
    inv_freq = (np.float32(ROPE_THETA) ** (-(np.arange(half, dtype=np.float32) * np.float32(2.0)) / np.float32(32))).astype(np.float32)
    ang = (pos[:, None] * inv_freq[None, :]).astype(np.float32)
    cos = np.cos(ang).astype(np.float32).T
    sin = np.sin(ang).astype(np.float32).T
    cosd = np.concatenate([cos, cos], 0).reshape(32, NT, TT).transpose(1, 0, 2)
    sind = np.concatenate([-sin, sin], 0).reshape(32, NT, TT).transpose(1, 0, 2)
    return {
        "masks": m.astype(ml_dtypes.bfloat16),
        "ident": np.eye(128, dtype=np.float32).astype(ml_dtypes.bfloat16),
        "ones": np.ones((128, 128), np.float32).astype(ml_dtypes.bfloat16),
        "pswapb": psw.astype(ml_dtypes.bfloat16),
        "cosd": np.ascontiguousarray(cosd, dtype=np.float32),
        "sind": np.ascontiguousarray(sind, dtype=np.float32),
    }


def make_in_maps(inputs):
    x = np.asarray(inputs["x"], dtype=np.float32)
    shared = {}
    for nm in ["w_ffn1_in", "w_ffn1_out", "w_in", "w_conv_out", "w_attn_out", "w_o", "w_ffn2_in", "w_ffn2_out"]:
        shared[nm] = np.ascontiguousarray(np.asarray(inputs[nm], dtype=np.float32)[0])
    gains = np.stack([np.asarray(inputs[k], np.float32)[0].reshape(16, 128).T for k in ["ffn1_norm", "mix_norm", "ffn2_norm"]], axis=1)
    shared["gains"] = np.ascontiguousarray(gains)
    shared["gF"] = np.ascontiguousarray(np.broadcast_to(np.asarray(inputs["final_norm"], np.float32)[None, :], (128, D)))
    wc = np.asarray(inputs["w_conv"], np.float32)[0]
    shared["wcv"] = np.ascontiguousarray(wc.reshape(3, 8, 128).transpose(2, 1, 0))
    in_maps = []
    for c in range(NC):
        b, j = c // 4, c % 4
        xin = np.zeros((T_ALL, D), np.float32)
        xin[N_HALO_T * TT:] = x[b, j * T_OWN:(j + 1) * T_OWN]
        if j > 0:
            xin[:N_HALO_T * TT] = x[b, j * T_OWN - N_HALO_T * TT:j * T_OWN]
        m = dict(shared)
        m["xin"] = xin
        m.update(_const_inputs(j))
        in_maps.append(m)
    return in_maps


_NC_CACHE = {}


def kernel(**inputs):
    if "nc" not in _NC_CACHE:
        _NC_CACHE["nc"] = Builder().build()
    nc = _NC_CACHE["nc"]
    in_maps = make_in_maps(inputs)
    res = run_bass_kernel_spmd(nc, in_maps, core_ids=list(range(NC)))
    out = np.zeros((2, 16384, D), np.float32)
    for c in range(NC):
        b, j = c // 4, c % 4
        out[b, j * T_OWN:(j + 1) * T_OWN] = res.results[c]["y"]
    return out
```

```python
import numpy as np
import ml_dtypes
from contextlib import ExitStack
import concourse.bass as bass
import concourse.mybir as mybir
from concourse.bass_utils import run_bass_kernel_spmd

F32 = mybir.dt.float32
BF16 = mybir.dt.bfloat16
AF = mybir.ActivationFunctionType
ALU = mybir.AluOpType

D = 2048
DFF = 5632
NC = 8
TT = 512
N_HALO_T = 4
N_OWN_T = 8
NT = N_HALO_T + N_OWN_T
T_OWN = N_OWN_T * TT
T_ALL = NT * TT
NGRAN = 12
GRAN = 1024
NWSEM = 8
CAST_LEAD = 120
NCAST = 8
NPS = 8
OFF_Q, OFF_K, OFF_V, OFF_XC, OFF_B, OFF_C, OFF_GC, OFF_GA = 0, 1536, 3072, 4608, 5632, 6656, 7680, 9728
ROPE_THETA = 500000.0
SCALE = 128.0 ** -0.5
EPS = 1e-5


class _Dummy:
    def __getitem__(self, k):
        return self

    def __getattr__(self, k):
        return lambda *a, **kw: self


DUMMY = _Dummy()


class EngW:
    def __init__(self, name, sem, self_sync):
        self.name = name
        self.sem = sem
        self.count = 0
        self.waited = {}
        self.self_sync = self_sync
        self.items = []

    def wait(self, tok):
        sem, val = tok
        if sem is self.sem and not self.self_sync:
            return
        k = id(sem)
        if self.waited.get(k, 0) >= val:
            return
        self.waited[k] = val
        self.items.append(("wait", sem, val))


class Prog:
    def __init__(self, sems):
        self.dry = False
        self.E = {
            "pe": EngW("pe", sems["pe"], False),
            "act": EngW("act", sems["act"], True),
            "dve": EngW("dve", sems["dve"], True),
            "pool": EngW("pool", sems["pool"], True),
            "sp": EngW("sp", sems["sp"], False),
        }
        self.lastw = {}
        self.readers = {}
        self.dma_cnt = {}

    def _deps(self, reads, writes):
        d = []
        for k in reads:
            t = self.lastw.get(k)
            if t is not None:
                d.append(t)
        for k in writes:
            t = self.lastw.get(k)
            if t is not None:
                d.append(t)
            r = self.readers.get(k)
            if r:
                d.extend(r.values())
        return d

    def _record(self, tok, reads, writes):
        for k in reads:
            r = self.readers.setdefault(k, {})
            sid = id(tok[0])
            old = r.get(sid)
            if old is None or old[1] < tok[1]:
                r[sid] = tok
        for k in writes:
            self.lastw[k] = tok
            self.readers[k] = {}

    def op(self, eng, fn, reads=(), writes=(), signal=True):
        if self.dry:
            return
        E = self.E[eng]
        px = [k for k in reads if k[0] == "ps"]
        if px:
            reads = [k for k in reads if k[0] != "ps"]
            writes = list(writes) + px
        for tok in self._deps(reads, writes):
            E.wait(tok)
        if signal:
            E.count += 1
            tok = (E.sem, E.count)
        else:
            tok = (E.sem, E.count + 1)
        E.items.append(("op", fn, signal))
        self._record(tok, reads, writes)

    def dma(self, eng, out_ap, in_ap, sem, reads=(), writes=()):
        if self.dry:
            return None
        E = self.E[eng]
        for tok in self._deps(reads, writes):
            E.wait(tok)
        c = self.dma_cnt.get(id(sem), 0)
        if c > 0:
            E.wait((sem, c))
        c += 16
        self.dma_cnt[id(sem)] = c
        E.items.append(("dma", out_ap, in_ap, sem))
        self._record((sem, c), reads, writes)
        return (sem, c)

    def wait_tok(self, eng, tok):
        if not self.dry and tok is not None:
            self.E[eng].wait(tok)

    def replay(self, eng, h):
        E = self.E[eng]
        for it in E.items:
            if it[0] == "wait":
                h.wait_ge(it[1], it[2])
            elif it[0] == "op":
                ins = it[1](h)
                if it[2]:
                    ins.then_inc(E.sem, 1)
            else:
                h.dma_start(out=it[1], in_=it[2]).then_inc(it[3], 16)


def perm(g, ap):
    if g == 0:
        return ap.rearrange("p (q n) -> p q n", q=4)
    return ap.rearrange("p (n q) -> p q n", q=4)


def blk(ap):
    return ap.rearrange("p (q n) -> p q n", q=4)


class Builder:
    def __init__(self, n_tiles=NT, dbg=None):
        self.n_tiles = n_tiles
        self.dbg = dbg or {}

    def build(self):
        nc = bass.Bass("TRN2", target_bir_lowering=False)
        self.nc = nc
        dr = {}

        def din(name, shape, dt=F32):
            dr[name] = nc.dram_tensor(name, shape, dt, kind="ExternalInput").ap()

        din("xin", [T_ALL, D])
        din("w_ffn1_in", [D, 2 * DFF]); din("w_ffn1_out", [DFF, D])
        din("w_in", [D, 11776]); din("w_conv_out", [1024, D]); din("w_attn_out", [512, D]); din("w_o", [D, D])
        din("w_ffn2_in", [D, 2 * DFF]); din("w_ffn2_out", [DFF, D])
        din("gains", [128, 3, 16]); din("gF", [128, D]); din("wcv", [128, 8, 3])
        din("cosd", [NT, 32, TT]); din("sind", [NT, 32, TT])
        din("masks", [128, 8, 128], BF16); din("ident", [128, 128], BF16); din("ones", [128, 128], BF16)
        din("pswapb", [32, 32], BF16)
        dr["y"] = nc.dram_tensor("y", [T_OWN, D], F32, kind="ExternalOutput").ap()
        for k, shp in self.dbg.items():
            dr[k] = nc.dram_tensor(k, shp, F32, kind="ExternalOutput").ap()
        self.dr = dr

        self.P = Prog({k: None for k in ["pe", "act", "dve", "pool", "sp"]})
        self.P.dry = True
        self.dry = True
        self.useq = []
        self.t = {}
        self.emit_all()
        useq = self.useq
        uniq = []
        seen = {}
        for s in useq:
            if s not in seen:
                seen[s] = len(uniq)
                uniq.append(s)
        self.uidx = seen
        self.uoff = {}
        off = 0
        for sp in uniq:
            self.uoff[sp] = off
            off += 128 * sp[2] * sp[4]
        dr["scr"] = nc.dram_tensor("scr", [off], BF16, kind="Internal").ap()

        with ExitStack() as es:
            def sb(name, shape, dt):
                return es.enter_context(nc.sbuf_tensor("sb_" + name, shape, dt))

            def sem(name):
                return es.enter_context(nc.semaphore(name))

            t = {}
            t["xt"] = sb("xt", [128, 4, D], F32)
            t["hT"] = sb("hT", [128, 16, TT], BF16)
            t["sB"] = sb("sB", [128, 44 * 512], BF16)
            for g, ns in enumerate([2, 2, 5]):
                t[f"KT{g}"] = sb(f"KT{g}", [128, ns, 4, TT], BF16)
                t[f"V{g}"] = sb(f"V{g}", [128, ns, 4, TT], BF16)
            t["wring"] = sb("wring", [128, NGRAN * GRAN], BF16)
            t["gF"] = sb("gF", [128, D], F32)
            t["cosT"] = sb("cosT", [32, TT], F32)
            t["sinT"] = sb("sinT", [32, TT], F32)
            t["sgt"] = sb("sgt", [128, TT], F32)
            t["uh"] = sb("uh", [128, 8, 2], F32)
            t["masks"] = sb("masks_sb", [128, 8, 128], BF16)
            t["ident"] = sb("ident_sb", [128, 128], BF16)
            t["ones"] = sb("ones_sb", [128, 128], BF16)
            t["pswapb"] = sb("pswapb_sb", [32, 32], BF16)
            t["gains"] = sb("gains_sb", [128, 3, 16], F32)
            t["wcv"] = sb("wcv_sb", [128, 8, 3], F32)
            t["ss"] = sb("ss", [128, 4], F32)
            t["ssp"] = sb("ssp", [128, 16], F32)
            t["sq"] = sb("sq", [128, 4], F32)
            t["rstd"] = sb("rstd", [128, 4], F32)
            t["eps"] = sb("eps_sb", [128, 1], F32)
            if self.dbg:
                t["dtmp"] = sb("dtmp", [128, TT], F32)
            t["pb"] = [es.enter_context(nc.psum_tensor(f"pb{i}", [128, 512], F32)) for i in range(NPS)]
            self.t = t
            sems = {k: sem("s_" + k) for k in ["pe", "act", "dve", "pool", "sp"]}
            self.wsem = [sem(f"w{i}") for i in range(NWSEM)]
            self.csem = [sem(f"c{i}") for i in range(NCAST)]
            self.xsem = [sem(f"x{i}") for i in range(4)]
            self.osem = [sem(f"o{i}") for i in range(4)]
            self.msem = [sem(f"m{i}") for i in range(12)]
            self.dsem = [sem(f"dbg{i}") for i in range(4)]
            self.P = Prog(sems)
            self.dry = False
            self.emit_all()
            P = self.P
            with nc.Block() as block:
                @block.tensor
                def _(h):
                    P.replay("pe", h)

                @block.scalar
                def _(h):
                    P.replay("act", h)

                @block.vector
                def _(h):
                    P.replay("dve", h)

                @block.gpsimd
                def _(h):
                    P.replay("pool", h)

                @block.sync
                def _(h):
                    P.replay("sp", h)
        return nc

    def S(self, a, n=1, f32=False):
        keys = [("sB", a + i) for i in range(n)]
        if self.dry:
            return DUMMY, keys
        ap = self.t["sB"][:, a * 512:(a + n) * 512]
        if f32:
            ap = ap.bitcast(F32)
        return ap, keys

    def alloc(self):
        i = self.psn % NPS
        self.psn += 1
        if self.dry:
            return DUMMY, ("ps", i)
        return self.t["pb"][i], ("ps", i)

    def wnext(self, spec):
        if self.dry:
            self.useq.append(spec)
            return DUMMY, [("wr", 0)], None
        n = self.wpos
        self.wpos += 1
        assert self.useq[n] == spec, (n, self.useq[n], spec)
        self.prefetch()
        assert self.wissued > n, ("weight ring too small / unit not released", n, spec, self.wactive)
        g0, ng = self.wplace[n]
        nr, ncols = spec[2], spec[4]
        ap = self.t["wring"][:, g0 * GRAN:g0 * GRAN + nr * ncols].rearrange("p (r n) -> p r n", r=nr)
        return ap, [("wr", g0 + i) for i in range(ng)], n

    def wdone(self, h):
        if self.dry:
            return
        self.wactive = [a for a in self.wactive if a[0] != h]
        self.prefetch()

    def prefetch(self):
        P = self.P
        while self.wissued < len(self.useq):
            m = self.wissued
            spec = self.useq[m]
            name, r0, nr, c0, ncols = spec
            ng = -(-(nr * ncols) // GRAN)
            cand = [self.whead] if self.whead + ng <= NGRAN else []
            cand.append(0)
            g0 = None
            for c in cand:
                if all(c + ng <= a[1] or c >= a[1] + a[2] for a in self.wactive):
                    g0 = c
                    break
            if g0 is None:
                return
            self.whead = g0 + ng
            self.wactive.append((m, g0, ng))
            self.wplace[m] = (g0, ng)
            sid = self.uidx[spec]
            o0 = self.uoff[spec]
            scr_v = self.dr["scr"][o0:o0 + 128 * nr * ncols].rearrange("(p r n) -> p r n", p=128, r=nr)
            if sid not in self.casted:
                self.casted.add(sid)
                src = self.dr[name][r0 * 128:(r0 + nr) * 128, c0:c0 + ncols].rearrange("(r p) n -> p r n", p=128)
                k = self.ncast % NCAST
                self.ncast += 1
                if m >= CAST_LEAD:
                    P.wait_tok("pool", self.wtok[m - CAST_LEAD])
                P.dma("pool", scr_v, src, self.csem[k], writes=[("scr", sid)])
            dst = self.t["wring"][:, g0 * GRAN:g0 * GRAN + nr * ncols].rearrange("p (r n) -> p r n", r=nr)
            k = self.nwdma % NWSEM
            self.nwdma += 1
            self.wtok[m] = P.dma("sp", dst, scr_v, self.wsem[k], reads=[("scr", sid)], writes=[("wr", g0 + i) for i in range(ng)])
            self.wissued += 1

    def mm(self, out_ap, lhsT, rhs, start, stop, reads, bank, signal=None):
        sig = stop if signal is None else signal
        self.P.op("pe", lambda e: e.matmul(out_ap, lhsT=lhsT, rhs=rhs, start=start, stop=stop),
                  reads=reads, writes=[bank], signal=sig)

    def xkeys(self, s, dgs=range(4)):
        return [("x", s, dg) for dg in dgs]

    def emit_all(self):
        P = self.P
        t = self.t if not self.dry else None
        self.psn = 0
        self.wpos = 0
        self.wissued = 0
        self.whead = 0
        self.wactive = []
        self.wplace = {}
        self.wtok = {}
        self.nwdma = 0
        self.casted = set()
        self.ncast = 0
        self.store_toks = []
        dr = self.dr
        if not self.dry:
            for i, (nm, dst) in enumerate([("gains", t["gains"]), ("gF", t["gF"]), ("wcv", t["wcv"]), ("masks", t["masks"]),
                                           ("ident", t["ident"]), ("ones", t["ones"]), ("pswapb", t["pswapb"])]):
                P.dma("sp", dst[:], dr[nm], self.msem[i], writes=[("c", nm)])
            P.op("dve", lambda e: e.memset(t["eps"][:], EPS), writes=[("c", "eps")])
            P.op("dve", lambda e: e.memset(t["uh"][:], 0.0), writes=[("uh", ch) for ch in range(8)])
        for T in range(self.n_tiles):
            own = T >= N_HALO_T
            if T == 0:
                self.load_x(T)
            self.ffn(T, 0)
            self.mixer(T, own)
            if own:
                self.ffn(T, 2)
                self.final(T)
        if not self.dry:
            for tok in self.store_toks:
                P.wait_tok("sp", tok)

    def load_x(self, T, subs=range(4)):
        if self.dry or T >= self.n_tiles:
            return
        for s in subs:
            self.P.dma("sp", self.t["xt"][:, s, :], self.dr["xin"][T * TT + s * 128:T * TT + (s + 1) * 128, :],
                       self.xsem[s], writes=self.xkeys(s))

    def dump(self, name, T, src_ap, keys, rows=None):
        if self.dry or name not in self.dbg:
            return
        k = self.ndbg = getattr(self, "ndbg", 0) + 1
        tok = self.P.dma("sp", rows, src_ap, self.dsem[k % 4], reads=keys)
        self.store_toks.append(tok)

    def dump_any(self, name, row0, src_ap, keys, eng="dve"):
        if self.dry or name not in self.dbg:
            return
        P = self.P
        t = self.t
        P.op(eng, (lambda e: e.tensor_copy(out=t["dtmp"][:], in_=src_ap)) if eng == "dve" else (lambda e: e.activation(out=t["dtmp"][:], in_=src_ap, func=AF.Copy)),
             reads=keys, writes=[("dtmp",)])
        k = self.ndbg = getattr(self, "ndbg", 0) + 1
        tok = P.dma("sp", self.dr[name][row0:row0 + 128, :], t["dtmp"][:], self.dsem[k % 4], reads=[("dtmp",)])
        self.store_toks.append(tok)

    def stats(self, s, partial, junk, junkk):
        P = self.P
        t = self.t
        xs = t["xt"][:, s, :]
        if partial:
            P.op("dve", lambda e: e.reduce_sum(out=t["ss"][:, s:s + 1], in_=t["ssp"][:, 4 * s:4 * s + 4], axis=mybir.AxisListType.X),
                 reads=[("ssp", s, dg) for dg in range(4)], writes=[("ss", s)])
        else:
            P.op("act", lambda e: e.activation(out=junk, in_=xs, func=AF.Square, accum_out=t["ss"][:, s:s + 1]),
                 reads=self.xkeys(s), writes=junkk + [("ss", s)])
        P.op("act", lambda e: e.activation(out=t["sq"][:, s:s + 1], in_=t["ss"][:, s:s + 1], func=AF.Sqrt, scale=1.0 / D, bias=t["eps"][:]),
             reads=[("ss", s), ("c", "eps")], writes=[("sq", s)])
        P.op("dve", lambda e: e.reciprocal(out=t["rstd"][:, s:s + 1], in_=t["sq"][:, s:s + 1]),
             reads=[("sq", s)], writes=[("rstd", s)])

    def stats_batch(self):
        P = self.P
        t = self.t
        allp = [("ssp", s, dg) for s in range(4) for dg in range(4)]
        P.op("dve", lambda e: e.reduce_sum(out=t["ss"][:, 0:4], in_=t["ssp"][:, 0:16].rearrange("p (s d) -> p s d", d=4), axis=mybir.AxisListType.X),
             reads=allp, writes=[("ss", s) for s in range(4)])
        P.op("act", lambda e: e.activation(out=t["sq"][:, 0:4], in_=t["ss"][:, 0:4], func=AF.Sqrt, scale=1.0 / D, bias=t["eps"][:]),
             reads=[("ss", s) for s in range(4)] + [("c", "eps")], writes=[("sq", s) for s in range(4)])
        P.op("dve", lambda e: e.reciprocal(out=t["rstd"][:, 0:4], in_=t["sq"][:, 0:4]),
             reads=[("sq", s) for s in range(4)], writes=[("rstd", s) for s in range(4)])

    def sq_partial(self, s, dg):
        P = self.P
        t = self.t
        xs = t["xt"][:, s, dg * 512:(dg + 1) * 512]
        P.op("act", lambda e: e.activation(out=t["sgt"][:], in_=xs, func=AF.Square, accum_out=t["ssp"][:, 4 * s + dg:4 * s + dg + 1]),
             reads=[("x", s, dg)], writes=[("sgt",), ("ssp", s, dg)])

    def norm_T(self, gi, partial=False):
        P = self.P
        t = self.t
        dry = self.dry
        if partial and not dry:
            self.stats_batch()
        for s in range(4):
            hn, hk = self.S(4 * s, 4)
            if not dry:
                xs = t["xt"][:, s, :]
                if not partial:
                    self.stats(s, False, hn, hk)
                if s == 3:
                    P.op("act", lambda e, hn=hn, xs=xs, s=s: e.activation(out=hn, in_=xs, func=AF.Copy, scale=t["rstd"][:, s:s + 1]),
                         reads=self.xkeys(s) + [("rstd", s)], writes=hk)
                else:
                    P.op("dve", lambda e, hn=hn, xs=xs, s=s: e.tensor_scalar(out=hn, in0=xs, scalar1=t["rstd"][:, s:s + 1], scalar2=0.0, op0=ALU.mult, op1=ALU.add),
                         reads=self.xkeys(s) + [("rstd", s)], writes=hk)
        for c in range(16):
            bank, bk = self.alloc()
            if dry:
                continue
            bb = bank[:].bitcast(BF16)
            for s in range(4):
                hn, hk = self.S(4 * s, 4)
                P.op("pe", lambda e, bb=bb, hn=hn, s=s, c=c: e.transpose(out=bb[:, s * 128:(s + 1) * 128], in_=hn[:, c * 128:(c + 1) * 128], identity=t["ident"][:]),
                     reads=hk + [("c", "ident")], writes=[bk], signal=(s == 3))
            eng = "act" if c % 2 == 0 else "dve"
            if eng == "act":
                P.op("act", lambda e, bb=bb, c=c: e.activation(out=t["hT"][:, c, :], in_=bb[:, 0:TT], func=AF.Copy, scale=t["gains"][:, gi, c:c + 1]),
                     reads=[bk, ("c", "gains")], writes=[("hT", c)])
            else:
                P.op("dve", lambda e, bb=bb, c=c: e.tensor_scalar(out=t["hT"][:, c, :], in0=bb[:, 0:TT], scalar1=t["gains"][:, gi, c:c + 1], scalar2=0.0, op0=ALU.mult, op1=ALU.add),
                     reads=[bk, ("c", "gains")], writes=[("hT", c)])

    def ffn(self, T, gi):
        P = self.P
        t = self.t
        dry = self.dry
        w_in = "w_ffn1_in" if gi == 0 else "w_ffn2_in"
        w_out = "w_ffn1_out" if gi == 0 else "w_ffn2_out"
        self.norm_T(gi, partial=(gi == 2))
        if gi == 0 and T == 0 and not dry:
            for c in range(16):
                self.dump_any("d_hT", c * 128, t["hT"][:, c, :], [("hT", c)])
        for f in range(44):
            wg, kg, hg = self.wnext((w_in, 0, 16, 128 * f, 128))
            wu, ku, hu = self.wnext((w_in, 0, 16, DFF + 128 * f, 128))
            bg, bgk = self.alloc()
            for c in range(16):
                if not dry:
                    self.mm(bg[:], wg[:, c, :], t["hT"][:, c, :], c == 0, c == 15, kg + [("hT", c)], bgk)
            self.wdone(hg)
            bu, buk = self.alloc()
            for c in range(16):
                if not dry:
                    self.mm(bu[:], wu[:, c, :], t["hT"][:, c, :], c == 0, c == 15, ku + [("hT", c)], buk)
            self.wdone(hu)
            a, ak = self.S(f, 1)
            if not dry and gi == 0 and T == 0 and f in (0, 1, 43):
                fi = (0, 1, 43).index(f)
                self.dump_any("d_gate", fi * 128, bg[:], [bgk])
                self.dump_any("d_up", fi * 128, bu[:], [buk])
            if not dry:
                P.op("act", lambda e, bg=bg: e.activation(out=t["sgt"][:], in_=bg[:], func=AF.Silu), reads=[bgk], writes=[("sgt",)])
                P.op("dve", lambda e, bu=bu, a=a: e.tensor_tensor(out=a, in0=bu[:], in1=t["sgt"][:], op=ALU.mult), reads=[buk, ("sgt",)], writes=ak)
        if gi == 0 and T == 0 and not dry:
            for fi, f in enumerate((0, 1, 43)):
                a, ak = self.S(f, 1)
                self.dump_any("d_act", fi * 128, a, ak)
        fgs = [(0, 8), (8, 8), (16, 8), (24, 8), (32, 8), (40, 4)]
        for dg in range(4):
            banks = [self.alloc() for _ in range(4)]
            for (f0, nf) in fgs:
                wo, ko, ho = self.wnext((w_out, f0, nf, 512 * dg, 512))
                for s in range(4):
                    for r in range(nf):
                        f = f0 + r
                        if not dry:
                            a, ak = self.S(f, 1)
                            last_use = (s == 3 and r == nf - 1)
                            self.mm(banks[s][0][:], a[:, s * 128:(s + 1) * 128], wo[:, r, :], f == 0, f == 43, ak + ko, banks[s][1],
                                    signal=(f == 43 or last_use))
                self.wdone(ho)
            for s in range(4):
                if not dry:
                    xs = t["xt"][:, s, dg * 512:(dg + 1) * 512]
                    P.op("dve", lambda e, b=banks[s][0], xs=xs: e.scalar_tensor_tensor(out=xs, in0=b[:], scalar=0.5, in1=xs, op0=ALU.mult, op1=ALU.add),
                         reads=[banks[s][1], ("x", s, dg)], writes=[("x", s, dg)])
                    self.sq_partial(s, dg)
        if "x1" in self.dbg and gi == 0 and not dry:
            for s in range(4):
                self.dump("x1", T, t["xt"][:, s, :], self.xkeys(s), rows=self.dr["x1"][T * TT + s * 128:T * TT + (s + 1) * 128, :])

    def rope_A(self, w, wk, i, g, dst, dstk, wh=None):
        P = self.P
        t = self.t
        dry = self.dry
        bank, bk = self.alloc()
        for c in range(16):
            if not dry:
                self.mm(bank[:], w[:, c, i * 128:(i + 1) * 128], t["hT"][:, c, :], c == 0, c == 15, wk + [("hT", c)], bk)
        if wh is not None:
            self.wdone(wh)
        par = self.rope_par = 1 - getattr(self, "rope_par", 0)
        r32, rk = self.S(28 + 2 * par, 2, f32=True)
        r16, r16k = self.S(36 + par, 1)
        if not dry:
            P.op("act", lambda e: e.activation(out=blk(dst), in_=perm(g, bank[:]), func=AF.Copy), reads=[bk], writes=dstk)
            P.op("act", lambda e: e.activation(out=r16[0:32, :], in_=bank[0:32, :], func=AF.Copy), reads=[bk], writes=r16k)
            P.op("act", lambda e: e.activation(out=r32[0:32, :], in_=bank[0:32, :], func=AF.Copy), reads=[bk], writes=rk)
        return (r32, rk, dst, dstk, g, r16, r16k)

    def rope_B(self, st):
        if st is None:
            return
        P = self.P
        t = self.t
        r32, rk, dst, dstk, g, r16, r16k = st
        t1, t1k = self.S(32, 2, f32=True)
        t2, t2k = self.S(34, 2, f32=True)
        sw, swk = self.alloc()
        if self.dry:
            return
        self.mm(sw[0:32, :], t["pswapb"][:], r16[0:32, :], True, True, r16k + [("c", "pswapb")], swk)
        P.op("dve", lambda e: e.tensor_tensor(out=t1[0:32, :], in0=r32[0:32, :], in1=t["cosT"][:], op=ALU.mult), reads=rk + [("tab",)], writes=t1k)
        P.op("dve", lambda e: e.tensor_tensor(out=t2[0:32, :], in0=sw[0:32, :], in1=t["sinT"][:], op=ALU.mult), reads=[swk, ("tab",)], writes=t2k)
        P.op("dve", lambda e: e.tensor_tensor(out=blk(dst[0:32, :]), in0=perm(g, t1[0:32, :]), in1=perm(g, t2[0:32, :]), op=ALU.add),
             reads=t1k + t2k, writes=dstk)

    def mixer(self, T, own):
        P = self.P
        t = self.t
        dry = self.dry
        dr = self.dr
        sl = [T % 2, T % 2, T % 5]
        last_halo = (T == N_HALO_T - 1)
        if not dry:
            P.dma("sp", t["cosT"][:], dr["cosd"][T], self.msem[8], writes=[("tab",)])
            P.dma("sp", t["sinT"][:], dr["sind"][T], self.msem[9], writes=[("tab",)])
        self.norm_T(1, partial=True)
        if not own:
            self.load_x(T + 1)
        pending = None
        kv_groups = (0, 1, 2) if (own or last_halo) else (2,)
        if own:
            for u in range(6):
                w, wk, wh = self.wnext(("w_in", 0, 16, OFF_Q + 256 * u, 256))
                for i in range(2):
                    h = 2 * u + i
                    dst, dstk = self.S(16 + h, 1)
                    st = self.rope_A(w, wk, i, h // 4, dst, dstk, wh if i == 1 else None)
                    self.rope_B(pending)
                    pending = st
        for u in range(6):
            if u // 2 not in kv_groups:
                continue
            w, wk, wh = self.wnext(("w_in", 0, 16, OFF_K + 256 * u, 256))
            for i in range(2):
                h = 2 * u + i
                g, j = h // 4, h % 4
                dst = DUMMY if dry else t[f"KT{g}"][:, sl[g], j, :]
                st = self.rope_A(w, wk, i, g, dst, [("KT", g, sl[g], j)], wh if i == 1 else None)
                self.rope_B(pending)
                pending = st
        for u in range(6):
            if u // 2 not in kv_groups:
                continue
            w, wk, wh = self.wnext(("w_in", 0, 16, OFF_V + 256 * u, 256))
            g, half = u // 2, u % 2
            for qbp in range(2):
                bank, bk = self.alloc()
                for qq in range(2):
                    qb = 2 * qbp + qq
                    for c in range(16):
                        if not dry:
                            hc = t["hT"][:, c, :]
                            lhsT = hc[:, qb * 128:(qb + 1) * 128] if g == 0 else hc[:, qb:TT:4]
                            self.mm(bank[:, qq * 256:(qq + 1) * 256], lhsT, w[:, c, :], c == 0, c == 15, wk + [("hT", c)], bk)
                if not dry:
                    dst = t[f"V{g}"][:, sl[g], 2 * qbp:2 * qbp + 2, half * 256:(half + 1) * 256]
                    P.op("act", lambda e, dst=dst, bank=bank: e.activation(out=dst, in_=bank[:].rearrange("p (q n) -> p q n", q=2), func=AF.Copy),
                         reads=[bk], writes=[("V", g, sl[g], qbp, half)])
                if pending is not None:
                    self.rope_B(pending)
                    pending = None
            self.wdone(wh)
        if not own:
            if last_halo:
                self.conv_halo()
            return
        self.attention(T, sl)
        self.conv()
        self.merge_out()
        if "x2" in self.dbg and not dry:
            for s in range(4):
                self.dump("x2", T, t["xt"][:, s, :], self.xkeys(s), rows=dr["x2"][T * TT + s * 128:T * TT + (s + 1) * 128, :])

    def attention(self, T, sl):
        pend = None
        idx = 0
        for j in range(4):
            for g in range(3):
                st = self.attn_A(T, sl, j, g, idx % 2)
                if pend is not None:
                    self.attn_B(pend)
                pend = st
                idx += 1
        self.attn_B(pend)

    def attn_A(self, T, sl, j, g, pbuf):
        P = self.P
        t = self.t
        dry = self.dry
        halo = lambda TT_: TT_ < N_HALO_T
        h = 4 * g + j
        QT, QTk = self.S(16 + h, 1)
        deltas = []
        if g == 0:
            own_l, prev_l = [], []
            for qb in range(4):
                own_l.append((sl[0], qb))
                prev_l.append((sl[0], qb - 1) if qb > 0 else ((T - 1) % 2, 3))
            deltas.append((own_l, [0] * 4))
            deltas.append((prev_l, [2 if (halo(T - 1)) else 1] + [1] * 3))
        elif g == 1:
            deltas.append(([(sl[1], qb) for qb in range(4)], [0] * 4))
            deltas.append(([((T - 1) % 2, qb) for qb in range(4)], [2 if halo(T - 1) else 1] * 4))
        else:
            for dl in range(5):
                Tk = T - dl
                if dl == 0:
                    mi = 3
                elif dl < 4:
                    mi = 6 if halo(Tk) else 4
                else:
                    mi = 7 if halo(Tk) else 5
                deltas.append(([(Tk % 5, qb) for qb in range(4)], [mi] * 4))
        pts = []
        for di, (blks, mis) in enumerate(deltas):
            sc, sck = self.alloc()
            pt, ptk = self.S(5 * pbuf + di, 1)
            pts.append((pt, ptk))
            if dry:
                continue
            for qb, (slot, kb) in enumerate(blks):
                self.mm(sc[:, qb * 128:(qb + 1) * 128], t[f"KT{g}"][:, slot, j, kb * 128:(kb + 1) * 128], QT[:, qb * 128:(qb + 1) * 128],
                        True, True, [("KT", g, slot, j)] + QTk, sck, signal=(qb == 3))
            P.op("act", lambda e, pt=pt, sc=sc: e.activation(out=pt, in_=sc[:], func=AF.Exp, scale=SCALE), reads=[sck], writes=ptk)
            if len(set(mis)) == 1:
                mi = mis[0]
                P.op("dve", lambda e, pt=pt, mi=mi: e.tensor_tensor(out=blk(pt), in0=blk(pt), in1=t["masks"][:, mi, :].unsqueeze(1).to_broadcast([128, 4, 128]), op=ALU.mult),
                     reads=ptk + [("c", "masks")], writes=ptk)
            else:
                mi0, mi = mis[0], mis[1]
                P.op("dve", lambda e, pt=pt, mi0=mi0: e.tensor_tensor(out=pt[:, 0:128], in0=pt[:, 0:128], in1=t["masks"][:, mi0, :], op=ALU.mult),
                     reads=ptk + [("c", "masks")], writes=ptk)
                P.op("dve", lambda e, pt=pt, mi=mi: e.tensor_tensor(out=pt[:, 128:512].rearrange("p (q n) -> p q n", q=3), in0=pt[:, 128:512].rearrange("p (q n) -> p q n", q=3),
                                                                    in1=t["masks"][:, mi, :].unsqueeze(1).to_broadcast([128, 3, 128]), op=ALU.mult),
                     reads=ptk + [("c", "masks")], writes=ptk)
        return (j, g, deltas, pts)

    def attn_B(self, st):
        P = self.P
        t = self.t
        dry = self.dry
        j, g, deltas, pts = st
        buf = j % 2
        OT, OTk = self.S(28 + 2 * buf, 2, f32=True)
        LB, LBk = self.S(32 + 2 * buf, 2, f32=True)
        ob, obk = self.alloc()
        lb, lbk = self.alloc()
        nd = len(deltas)
        if not dry:
            for qb in range(4):
                for di, (blks, mis) in enumerate(deltas):
                    slot, kb = blks[qb]
                    pt, ptk = pts[di]
                    self.mm(ob[:, qb * 128:(qb + 1) * 128], t[f"V{g}"][:, slot, kb, j * 128:(j + 1) * 128], pt[:, qb * 128:(qb + 1) * 128],
                            di == 0, di == nd - 1, [("V", g, slot, kb // 2, j // 2)] + ptk, obk, signal=(qb == 3 and di == nd - 1))
            for di in range(nd):
                pt, ptk = pts[di]
                self.mm(lb[:], t["ones"][:], pt, di == 0, di == nd - 1, ptk + [("c", "ones")], lbk)
            if g == 0:
                P.op("dve", lambda e: e.tensor_copy(out=OT, in_=ob[:]), reads=[obk], writes=OTk)
                P.op("dve", lambda e: e.tensor_copy(out=LB, in_=lb[:]), reads=[lbk], writes=LBk)
            else:
                P.op("dve", lambda e: e.tensor_tensor(out=perm(g, OT), in0=blk(ob[:]), in1=perm(g, OT), op=ALU.add), reads=[obk] + OTk, writes=OTk)
                P.op("dve", lambda e: e.tensor_tensor(out=perm(g, LB), in0=blk(lb[:]), in1=perm(g, LB), op=ALU.add), reads=[lbk] + LBk, writes=LBk)
        if g == 2:
            rc, rck = self.S(10, 2, f32=True)
            on, onk = self.S(40 + j, 1)
            if not dry:
                P.op("dve", lambda e: e.reciprocal(out=rc, in_=LB), reads=LBk, writes=rck)
                P.op("dve", lambda e: e.tensor_tensor(out=on, in0=OT, in1=rc, op=ALU.mult), reads=OTk + rck, writes=onk)

    def conv_halo(self):
        P = self.P
        t = self.t
        dry = self.dry
        for ch in range(8):
            wx, wxk, hx = self.wnext(("w_in", 0, 16, OFF_XC + 128 * ch, 128))
            wc, wck, hc = self.wnext(("w_in", 0, 16, OFF_C + 128 * ch, 128))
            bx, bxk = self.alloc()
            bc, bck = self.alloc()
            xs, xsk = self.S(36, 2, f32=True)
            if not dry:
                for c in range(16):
                    self.mm(bx[:, 0:2], wx[:, c, :], t["hT"][:, c, TT - 2:TT], c == 0, c == 15, wxk + [("hT", c)], bxk)
                for c in range(16):
                    self.mm(bc[:, 0:2], wc[:, c, :], t["hT"][:, c, TT - 2:TT], c == 0, c == 15, wck + [("hT", c)], bck)
                P.op("act", lambda e, bx=bx, xs=xs: e.activation(out=xs[:, 0:2], in_=bx[:, 0:2], func=AF.Copy), reads=[bxk], writes=xsk)
                P.op("dve", lambda e, bc=bc, xs=xs, ch=ch: e.tensor_tensor(out=t["uh"][:, ch, :], in0=bc[:, 0:2], in1=xs[:, 0:2], op=ALU.mult),
                     reads=[bck] + xsk, writes=[("uh", ch)])
            self.wdone(hx)
            self.wdone(hc)

    def conv(self):
        P = self.P
        t = self.t
        dry = self.dry
        for ch in range(8):
            wx, wxk, hx = self.wnext(("w_in", 0, 16, OFF_XC + 128 * ch, 128))
            wc, wck, hc = self.wnext(("w_in", 0, 16, OFF_C + 128 * ch, 128))
            wb, wbk, hb = self.wnext(("w_in", 0, 16, OFF_B + 128 * ch, 128))
            for _one in range(1):
                bx, bxk = self.alloc()
                bc, bck = self.alloc()
                bb, bbk = self.alloc()
                xs, xsk = self.S(36, 2, f32=True)
                acc, acck = self.S(38, 2, f32=True)
                ub, ubk = self.S(8, 3, f32=True)
                cv, cvk = self.S(28 + ch, 1)
                if dry:
                    continue
                for (b, bk_, w_, wk_) in ((bx, bxk, wx, wxk), (bc, bck, wc, wck), (bb, bbk, wb, wbk)):
                    for c in range(16):
                        self.mm(b[:], w_[:, c, :], t["hT"][:, c, :], c == 0, c == 15, wk_ + [("hT", c)], bk_)
                self.wdone(hx)
                self.wdone(hc)
                self.wdone(hb)
                cw = t["wcv"]
                P.op("act", lambda e, bx=bx, xs=xs: e.activation(out=xs, in_=bx[:], func=AF.Copy), reads=[bxk], writes=xsk)
                P.op("dve", lambda e, ub=ub, ch=ch: e.tensor_copy(out=ub[:, 0:2], in_=t["uh"][:, ch, :]), reads=[("uh", ch)], writes=ubk)
                P.op("dve", lambda e, ub=ub, bc=bc, xs=xs: e.tensor_tensor(out=ub[:, 2:514], in0=bc[:], in1=xs, op=ALU.mult), reads=[bck] + xsk, writes=ubk)
                P.op("dve", lambda e, ub=ub, ch=ch: e.tensor_copy(out=t["uh"][:, ch, :], in_=ub[:, 512:514]), reads=ubk, writes=[("uh", ch)])
                P.op("dve", lambda e, ub=ub, acc=acc, ch=ch: e.tensor_scalar(out=acc, in0=ub[:, 2:514], scalar1=cw[:, ch, 2:3], scalar2=0.0, op0=ALU.mult, op1=ALU.add),
                     reads=ubk + [("c", "wcv")], writes=acck)
                P.op("dve", lambda e, ub=ub, acc=acc, ch=ch: e.scalar_tensor_tensor(out=acc, in0=ub[:, 1:513], scalar=cw[:, ch, 1:2], in1=acc, op0=ALU.mult, op1=ALU.add),
                     reads=ubk + acck + [("c", "wcv")], writes=acck)
                P.op("dve", lambda e, ub=ub, acc=acc, ch=ch: e.scalar_tensor_tensor(out=acc, in0=ub[:, 0:512], scalar=cw[:, ch, 0:1], in1=acc, op0=ALU.mult, op1=ALU.add),
                     reads=ubk + acck + [("c", "wcv")], writes=acck)
                P.op("dve", lambda e, bb=bb, acc=acc, cv=cv: e.tensor_tensor(out=cv, in0=bb[:], in1=acc, op=ALU.mult), reads=[bbk] + acck, writes=cvk)

    def merge_out(self):
        P = self.P
        t = self.t
        dry = self.dry
        for d in range(16):
            wgc, kgc, hgc = self.wnext(("w_in", 0, 16, OFF_GC + 128 * d, 128))
            wga, kga, hga = self.wnext(("w_in", 0, 16, OFF_GA + 128 * d, 128))
            wco, kco, hco = self.wnext(("w_conv_out", 0, 8, 128 * d, 128))
            wao, kao, hao = self.wnext(("w_attn_out", 0, 4, 128 * d, 128))
            for _one in range(1):
                b1, b1k = self.alloc()
                b2, b2k = self.alloc()
                b3, b3k = self.alloc()
                b4, b4k = self.alloc()
                par = d % 2
                s1, s1k = self.S(16 + 2 * par, 2, f32=True)
                s2, s2k = self.S(20 + 2 * par, 2, f32=True)
                m1, m1k = self.S(24, 2, f32=True)
                m2, m2k = self.S(26, 2, f32=True)
                mg, mgk = self.S(d, 1)
                if dry:
                    continue
                for c in range(16):
                    self.mm(b1[:], wgc[:, c, :], t["hT"][:, c, :], c == 0, c == 15, kgc + [("hT", c)], b1k)
                for c in range(16):
                    self.mm(b2[:], wga[:, c, :], t["hT"][:, c, :], c == 0, c == 15, kga + [("hT", c)], b2k)
                for ch in range(8):
                    cv, cvk = self.S(28 + ch, 1)
                    self.mm(b3[:], wco[:, ch, :], cv, ch == 0, ch == 7, kco + cvk, b3k)
                for jj in range(4):
                    on, onk = self.S(40 + jj, 1)
                    self.mm(b4[:], wao[:, jj, :], on, jj == 0, jj == 3, kao + onk, b4k)
                for hh in (hgc, hga, hco, hao):
                    self.wdone(hh)
                P.op("act", lambda e, b1=b1, s1=s1: e.activation(out=s1, in_=b1[:], func=AF.Sigmoid), reads=[b1k], writes=s1k)
                P.op("act", lambda e, b2=b2, s2=s2: e.activation(out=s2, in_=b2[:], func=AF.Sigmoid), reads=[b2k], writes=s2k)
                P.op("dve", lambda e, b3=b3, s1=s1, m1=m1: e.tensor_tensor(out=m1, in0=b3[:], in1=s1, op=ALU.mult), reads=[b3k] + s1k, writes=m1k)
                P.op("dve", lambda e, b4=b4, s2=s2, m2=m2: e.tensor_tensor(out=m2, in0=b4[:], in1=s2, op=ALU.mult), reads=[b4k] + s2k, writes=m2k)
                P.op("dve", lambda e, m1=m1, m2=m2, mg=mg: e.tensor_tensor(out=mg, in0=m1, in1=m2, op=ALU.add), reads=m1k + m2k, writes=mgk)
        for dg in range(4):
            banks = [self.alloc() for _ in range(4)]
            for half in range(2):
                wo, ko, ho = self.wnext(("w_o", 8 * half, 8, 512 * dg, 512))
                for s in range(4):
                    for r in range(8):
                        dch = 8 * half + r
                        if not dry:
                            mg, mgk = self.S(dch, 1)
                            self.mm(banks[s][0][:], mg[:, s * 128:(s + 1) * 128], wo[:, r, :], dch == 0, dch == 15, mgk + ko, banks[s][1],
                                    signal=(dch == 15 or (s == 3 and r == 7)))
                self.wdone(ho)
            for s in range(4):
                if not dry:
                    xs = t["xt"][:, s, dg * 512:(dg + 1) * 512]
                    P.op("dve", lambda e, b=banks[s][0], xs=xs: e.tensor_tensor(out=xs, in0=b[:], in1=xs, op=ALU.add),
                         reads=[banks[s][1], ("x", s, dg)], writes=[("x", s, dg)])
                    self.sq_partial(s, dg)

    def final(self, T):
        P = self.P
        t = self.t
        if self.dry:
            return
        self.stats_batch()
        obs = []
        for s in range(4):
            xs = t["xt"][:, s, :]
            ob, obk = self.S(12 + 8 * s, 8, f32=True)
            P.op("dve", lambda e, xs=xs, s=s, ob=ob: e.scalar_tensor_tensor(out=ob, in0=xs, scalar=t["rstd"][:, s:s + 1], in1=t["gF"][:], op0=ALU.mult, op1=ALU.mult),
                 reads=self.xkeys(s) + [("rstd", s), ("c", "gF")], writes=obk)
            obs.append((ob, obk))
            self.load_x(T + 1, [s])
        for s in range(4):
            ob, obk = obs[s]
            r0 = (T - N_HALO_T) * TT + s * 128
            tok = P.dma("sp", self.dr["y"][r0:r0 + 128, :], ob, self.osem[s], reads=obk)
            self.store_toks.append(tok)


def _const_inputs(jchunk):
    p = np.arange(128)
    k = p[:, None]
    q = p[None, :]
    same = (k % 4) == (q % 4)
    valid = 1.0 if jchunk > 0 else 0.0
    m = np.zeros((128, 8, 128), np.float32)
    m[:, 0] = (k <= q)
    m[:, 1] = (k >= q)
    m[:, 2] = (k >= q) * valid
    m[:, 3] = same & (k <= q)
    m[:, 4] = same
    m[:, 5] = same & (k >= q)
    m[:, 6] = same * valid
    m[:, 7] = (same & (k >= q)) * valid
    psw = np.zeros((32, 32), np.float32)
    for i in range(16):
        psw[i + 16, i] = 1.0
        psw[i, i + 16] = 1.0
    pos = (jchunk * T_OWN - N_HALO_T * TT + np.arange(T_ALL)).astype(np.float32)
    half = 16
    inv_freq = (np.float32(ROPE_THETA) ** (-(np.arange(half, dtype=np.float32) * np.float32(2.0)) / np.float32(32))).astype(np.float32)
    ang = (pos[:, None] * inv_freq[None, :]).astype(np.float32)
    cos = np.cos(ang).astype(np.float32).T
    sin = np.sin(ang).astype(np.float32).T
    cosd = np.concatenate([cos, cos], 0).reshape(32, NT, TT).transpose(1, 0, 2)
    sind = np.concatenate([-sin, sin], 0).reshape(32, NT, TT).transpose(1, 0, 2)
    return {
        "masks": m.astype(ml_dtypes.bfloat16),
        "ident": np.eye(128, dtype=np.float32).astype(ml_dtypes.bfloat16),
        "ones": np.ones((128, 128), np.float32).astype(ml_dtypes.bfloat16),
        "pswapb": psw.astype(ml_dtypes.bfloat16),
        "cosd": np.ascontiguousarray(cosd, dtype=np.float32),
        "sind": np.ascontiguousarray(sind, dtype=np.float32),
    }


def make_in_maps(inputs):
    x = np.asarray(inputs["x"], dtype=np.float32)
    shared = {}
    for nm in ["w_ffn1_in", "w_ffn1_out", "w_in", "w_conv_out", "w_attn_out", "w_o", "w_ffn2_in", "w_ffn2_out"]:
        shared[nm] = np.ascontiguousarray(np.asarray(inputs[nm], dtype=np.float32)[0])
    gains = np.stack([np.asarray(inputs[k], np.float32)[0].reshape(16, 128).T for k in ["ffn1_norm", "mix_norm", "ffn2_norm"]], axis=1)
    shared["gains"] = np.ascontiguousarray(gains)
    shared["gF"] = np.ascontiguousarray(np.broadcast_to(np.asarray(inputs["final_norm"], np.float32)[None, :], (128, D)))
    wc = np.asarray(inputs["w_conv"], np.float32)[0]
    shared["wcv"] = np.ascontiguousarray(wc.reshape(3, 8, 128).transpose(2, 1, 0))
    in_maps = []
    for c in range(NC):
        b, j = c // 4, c % 4
        xin = np.zeros((T_ALL, D), np.float32)
        xin[N_HALO_T * TT:] = x[b, j * T_OWN:(j + 1) * T_OWN]
        if j > 0:
            xin[:N_HALO_T * TT] = x[b, j * T_OWN - N_HALO_T * TT:j * T_OWN]
        m = dict(shared)
        m["xin"] = xin
        m.update(_const_inputs(j))
        in_maps.append(m)
    return in_maps


_NC_CACHE = {}


def kernel(**inputs):
    if "nc" not in _NC_CACHE:
        _NC_CACHE["nc"] = Builder().build()
    nc = _NC_CACHE["nc"]
    in_maps = make_in_maps(inputs)
    res = run_bass_kernel_spmd(nc, in_maps, core_ids=list(range(NC)))
    out = np.zeros((2, 16384, D), np.float32)
    for c in range(NC):
        b, j = c // 4, c % 4
        out[b, j * T_OWN:(j + 1) * T_OWN] = res.results[c]["y"]
    return out
```

```python
import numpy as np
import ml_dtypes
from contextlib import ExitStack
import concourse.bass as bass
import concourse.mybir as mybir
from concourse.bass_utils import run_bass_kernel_spmd

F32 = mybir.dt.float32
BF16 = mybir.dt.bfloat16
AF = mybir.ActivationFunctionType
ALU = mybir.AluOpType

D = 2048
DFF = 5632
NC = 8
TT = 512
N_HALO_T = 4
N_OWN_T = 8
NT = N_HALO_T + N_OWN_T
T_OWN = N_OWN_T * TT
T_ALL = NT * TT
NGRAN = 12
GRAN = 1024
NWSEM = 8
CAST_LEAD = 120
NCAST = 16
NPS = 8
OFF_Q, OFF_K, OFF_V, OFF_XC, OFF_B, OFF_C, OFF_GC, OFF_GA = 0, 1536, 3072, 4608, 5632, 6656, 7680, 9728
ROPE_THETA = 500000.0
SCALE = 128.0 ** -0.5
EPS = 1e-5


class _Dummy:
    def __getitem__(self, k):
        return self

    def __getattr__(self, k):
        return lambda *a, **kw: self


DUMMY = _Dummy()


class EngW:
    def __init__(self, name, sem, self_sync):
        self.name = name
        self.sem = sem
        self.count = 0
        self.waited = {}
        self.self_sync = self_sync
        self.items = []

    def wait(self, tok):
        sem, val = tok
        if sem is self.sem and not self.self_sync:
            return
        k = id(sem)
        if self.waited.get(k, 0) >= val:
            return
        self.waited[k] = val
        self.items.append(("wait", sem, val))


class Prog:
    def __init__(self, sems):
        self.dry = False
        self.E = {
            "pe": EngW("pe", sems["pe"], False),
            "act": EngW("act", sems["act"], True),
            "dve": EngW("dve", sems["dve"], True),
            "pool": EngW("pool", sems["pool"], True),
            "sp": EngW("sp", sems["sp"], False),
        }
        self.lastw = {}
        self.readers = {}
        self.dma_cnt = {}

    def _deps(self, reads, writes):
        d = []
        for k in reads:
            t = self.lastw.get(k)
            if t is not None:
                d.append(t)
        for k in writes:
            t = self.lastw.get(k)
            if t is not None:
                d.append(t)
            r = self.readers.get(k)
            if r:
                d.extend(r.values())
        return d

    def _record(self, tok, reads, writes):
        for k in reads:
            r = self.readers.setdefault(k, {})
            sid = id(tok[0])
            old = r.get(sid)
            if old is None or old[1] < tok[1]:
                r[sid] = tok
        for k in writes:
            self.lastw[k] = tok
            self.readers[k] = {}

    def op(self, eng, fn, reads=(), writes=(), signal=True):
        if self.dry:
            return
        E = self.E[eng]
        px = [k for k in reads if k[0] == "ps"]
        if px:
            reads = [k for k in reads if k[0] != "ps"]
            writes = list(writes) + px
        for tok in self._deps(reads, writes):
            E.wait(tok)
        if signal:
            E.count += 1
            tok = (E.sem, E.count)
        else:
            tok = (E.sem, E.count + 1)
        E.items.append(("op", fn, signal))
        self._record(tok, reads, writes)

    def dma(self, eng, out_ap, in_ap, sem, reads=(), writes=()):
        if self.dry:
            return None
        E = self.E[eng]
        for tok in self._deps(reads, writes):
            E.wait(tok)
        c = self.dma_cnt.get(id(sem), 0)
        if c > 0:
            E.wait((sem, c))
        c += 16
        self.dma_cnt[id(sem)] = c
        E.items.append(("dma", out_ap, in_ap, sem))
        self._record((sem, c), reads, writes)
        return (sem, c)

    def wait_tok(self, eng, tok):
        if not self.dry and tok is not None:
            self.E[eng].wait(tok)

    def replay(self, eng, h):
        E = self.E[eng]
        for it in E.items:
            if it[0] == "wait":
                h.wait_ge(it[1], it[2])
            elif it[0] == "op":
                ins = it[1](h)
                if it[2]:
                    ins.then_inc(E.sem, 1)
            else:
                h.dma_start(out=it[1], in_=it[2]).then_inc(it[3], 16)


def perm(g, ap):
    if g == 0:
        return ap.rearrange("p (q n) -> p q n", q=4)
    return ap.rearrange("p (n q) -> p q n", q=4)


def blk(ap):
    return ap.rearrange("p (q n) -> p q n", q=4)


class Builder:
    def __init__(self, n_tiles=NT, dbg=None):
        self.n_tiles = n_tiles
        self.dbg = dbg or {}

    def build(self):
        nc = bass.Bass("TRN2", target_bir_lowering=False)
        self.nc = nc
        dr = {}

        def din(name, shape, dt=F32):
            dr[name] = nc.dram_tensor(name, shape, dt, kind="ExternalInput").ap()

        din("xin", [T_ALL, D])
        din("w_ffn1_in", [D, 2 * DFF]); din("w_ffn1_out", [DFF, D])
        din("w_in", [D, 11776]); din("w_conv_out", [1024, D]); din("w_attn_out", [512, D]); din("w_o", [D, D])
        din("w_ffn2_in", [D, 2 * DFF]); din("w_ffn2_out", [DFF, D])
        din("gains", [128, 3, 16]); din("gF", [128, D]); din("wcv", [128, 8, 3])
        din("cosd", [NT, 32, TT]); din("sind", [NT, 32, TT])
        din("masks", [128, 8, 128], BF16); din("ident", [128, 128], BF16); din("ones", [128, 128], BF16)
        din("pswapb", [32, 32], BF16)
        dr["y"] = nc.dram_tensor("y", [T_OWN, D], F32, kind="ExternalOutput").ap()
        for k, shp in self.dbg.items():
            dr[k] = nc.dram_tensor(k, shp, F32, kind="ExternalOutput").ap()
        self.dr = dr

        self.P = Prog({k: None for k in ["pe", "act", "dve", "pool", "sp"]})
        self.P.dry = True
        self.dry = True
        self.useq = []
        self.t = {}
        self.emit_all()
        useq = self.useq
        uniq = []
        seen = {}
        for s in useq:
            if s not in seen:
                seen[s] = len(uniq)
                uniq.append(s)
        self.uidx = seen
        self.uoff = {}
        off = 0
        for sp in uniq:
            self.uoff[sp] = off
            off += 128 * sp[2] * sp[4]
        dr["scr"] = nc.dram_tensor("scr", [off], BF16, kind="Internal").ap()

        with ExitStack() as es:
            def sb(name, shape, dt):
                return es.enter_context(nc.sbuf_tensor("sb_" + name, shape, dt))

            def sem(name):
                return es.enter_context(nc.semaphore(name))

            t = {}
            t["xt"] = sb("xt", [128, 4, D], F32)
            t["hT"] = sb("hT", [128, 16, TT], BF16)
            t["sB"] = sb("sB", [128, 44 * 512], BF16)
            for g, ns in enumerate([2, 2, 5]):
                t[f"KT{g}"] = sb(f"KT{g}", [128, ns, 4, TT], BF16)
                t[f"V{g}"] = sb(f"V{g}", [128, ns, 4, TT], BF16)
            t["wring"] = sb("wring", [128, NGRAN * GRAN], BF16)
            t["gF"] = sb("gF", [128, D], F32)
            t["cosT"] = sb("cosT", [32, TT], F32)
            t["sinT"] = sb("sinT", [32, TT], F32)
            t["sgt"] = sb("sgt", [128, TT], F32)
            t["uh"] = sb("uh", [128, 8, 2], F32)
            t["masks"] = sb("masks_sb", [128, 8, 128], BF16)
            t["ident"] = sb("ident_sb", [128, 128], BF16)
            t["ones"] = sb("ones_sb", [128, 128], BF16)
            t["pswapb"] = sb("pswapb_sb", [32, 32], BF16)
            t["gains"] = sb("gains_sb", [128, 3, 16], F32)
            t["wcv"] = sb("wcv_sb", [128, 8, 3], F32)
            t["ss"] = sb("ss", [128, 4], F32)
            t["ssp"] = sb("ssp", [128, 16], F32)
            t["sq"] = sb("sq", [128, 4], F32)
            t["rstd"] = sb("rstd", [128, 4], F32)
            t["eps"] = sb("eps_sb", [128, 1], F32)
            if self.dbg:
                t["dtmp"] = sb("dtmp", [128, TT], F32)
            t["pb"] = [es.enter_context(nc.psum_tensor(f"pb{i}", [128, 512], F32)) for i in range(NPS)]
            self.t = t
            sems = {k: sem("s_" + k) for k in ["pe", "act", "dve", "pool", "sp"]}
            self.wsem = [sem(f"w{i}") for i in range(NWSEM)]
            self.csem = [sem(f"c{i}") for i in range(NCAST)]
            self.xsem = [sem(f"x{i}") for i in range(4)]
            self.osem = [sem(f"o{i}") for i in range(4)]
            self.msem = [sem(f"m{i}") for i in range(12)]
            self.dsem = [sem(f"dbg{i}") for i in range(4)]
            self.P = Prog(sems)
            self.dry = False
            self.emit_all()
            P = self.P
            with nc.Block() as block:
                @block.tensor
                def _(h):
                    P.replay("pe", h)

                @block.scalar
                def _(h):
                    P.replay("act", h)

                @block.vector
                def _(h):
                    P.replay("dve", h)

                @block.gpsimd
                def _(h):
                    P.replay("pool", h)

                @block.sync
                def _(h):
                    P.replay("sp", h)
        return nc

    def S(self, a, n=1, f32=False):
        keys = [("sB", a + i) for i in range(n)]
        if self.dry:
            return DUMMY, keys
        ap = self.t["sB"][:, a * 512:(a + n) * 512]
        if f32:
            ap = ap.bitcast(F32)
        return ap, keys

    def alloc(self):
        i = self.psn % NPS
        self.psn += 1
        if self.dry:
            return DUMMY, ("ps", i)
        return self.t["pb"][i], ("ps", i)

    def wnext(self, spec):
        if self.dry:
            self.useq.append(spec)
            return DUMMY, [("wr", 0)], None
        n = self.wpos
        self.wpos += 1
        assert self.useq[n] == spec, (n, self.useq[n], spec)
        self.prefetch()
        assert self.wissued > n, ("weight ring too small / unit not released", n, spec, self.wactive)
        g0, ng = self.wplace[n]
        nr, ncols = spec[2], spec[4]
        ap = self.t["wring"][:, g0 * GRAN:g0 * GRAN + nr * ncols].rearrange("p (r n) -> p r n", r=nr)
        return ap, [("wr", g0 + i) for i in range(ng)], n

    def wdone(self, h):
        if self.dry:
            return
        self.wactive = [a for a in self.wactive if a[0] != h]
        self.prefetch()

    def prefetch(self):
        P = self.P
        while self.wissued < len(self.useq):
            m = self.wissued
            spec = self.useq[m]
            name, r0, nr, c0, ncols = spec
            ng = -(-(nr * ncols) // GRAN)
            cand = [self.whead] if self.whead + ng <= NGRAN else []
            cand.append(0)
            g0 = None
            for c in cand:
                if all(c + ng <= a[1] or c >= a[1] + a[2] for a in self.wactive):
                    g0 = c
                    break
            if g0 is None:
                return
            self.whead = g0 + ng
            self.wactive.append((m, g0, ng))
            self.wplace[m] = (g0, ng)
            sid = self.uidx[spec]
            o0 = self.uoff[spec]
            scr_v = self.dr["scr"][o0:o0 + 128 * nr * ncols].rearrange("(p r n) -> p r n", p=128, r=nr)
            if sid not in self.casted:
                self.casted.add(sid)
                src = self.dr[name][r0 * 128:(r0 + nr) * 128, c0:c0 + ncols].rearrange("(r p) n -> p r n", p=128)
                k = self.ncast % NCAST
                self.ncast += 1
                if m >= CAST_LEAD:
                    P.wait_tok("pool", self.wtok[m - CAST_LEAD])
                P.dma("pool", scr_v, src, self.csem[k], writes=[("scr", sid)])
            dst = self.t["wring"][:, g0 * GRAN:g0 * GRAN + nr * ncols].rearrange("p (r n) -> p r n", r=nr)
            k = self.nwdma % NWSEM
            self.nwdma += 1
            self.wtok[m] = P.dma("sp", dst, scr_v, self.wsem[k], reads=[("scr", sid)], writes=[("wr", g0 + i) for i in range(ng)])
            self.wissued += 1

    def mm(self, out_ap, lhsT, rhs, start, stop, reads, bank, signal=None):
        sig = stop if signal is None else signal
        self.P.op("pe", lambda e: e.matmul(out_ap, lhsT=lhsT, rhs=rhs, start=start, stop=stop),
                  reads=reads, writes=[bank], signal=sig)

    def xkeys(self, s, dgs=range(4)):
        return [("x", s, dg) for dg in dgs]

    def emit_all(self):
        P = self.P
        t = self.t if not self.dry else None
        self.psn = 0
        self.wpos = 0
        self.wissued = 0
        self.whead = 0
        self.wactive = []
        self.wplace = {}
        self.wtok = {}
        self.nwdma = 0
        self.casted = set()
        self.ncast = 0
        self.store_toks = []
        dr = self.dr
        if not self.dry:
            for i, (nm, dst) in enumerate([("gains", t["gains"]), ("gF", t["gF"]), ("wcv", t["wcv"]), ("masks", t["masks"]),
                                           ("ident", t["ident"]), ("ones", t["ones"]), ("pswapb", t["pswapb"])]):
                P.dma("sp", dst[:], dr[nm], self.msem[i], writes=[("c", nm)])
            P.op("dve", lambda e: e.memset(t["eps"][:], EPS), writes=[("c", "eps")])
            P.op("dve", lambda e: e.memset(t["uh"][:], 0.0), writes=[("uh", ch) for ch in range(8)])
        for T in range(self.n_tiles):
            own = T >= N_HALO_T
            if T == 0:
                self.load_x(T)
            self.ffn(T, 0)
            self.mixer(T, own)
            if own:
                self.ffn(T, 2)
                self.final(T)
        if not self.dry:
            for tok in self.store_toks:
                P.wait_tok("sp", tok)

    def load_x(self, T, subs=range(4)):
        if self.dry or T >= self.n_tiles:
            return
        for s in subs:
            self.P.dma("sp", self.t["xt"][:, s, :], self.dr["xin"][T * TT + s * 128:T * TT + (s + 1) * 128, :],
                       self.xsem[s], writes=self.xkeys(s))

    def dump(self, name, T, src_ap, keys, rows=None):
        if self.dry or name not in self.dbg:
            return
        k = self.ndbg = getattr(self, "ndbg", 0) + 1
        tok = self.P.dma("sp", rows, src_ap, self.dsem[k % 4], reads=keys)
        self.store_toks.append(tok)

    def dump_any(self, name, row0, src_ap, keys, eng="dve"):
        if self.dry or name not in self.dbg:
            return
        P = self.P
        t = self.t
        P.op(eng, (lambda e: e.tensor_copy(out=t["dtmp"][:], in_=src_ap)) if eng == "dve" else (lambda e: e.activation(out=t["dtmp"][:], in_=src_ap, func=AF.Copy)),
             reads=keys, writes=[("dtmp",)])
        k = self.ndbg = getattr(self, "ndbg", 0) + 1
        tok = P.dma("sp", self.dr[name][row0:row0 + 128, :], t["dtmp"][:], self.dsem[k % 4], reads=[("dtmp",)])
        self.store_toks.append(tok)

    def stats(self, s, partial, junk, junkk):
        P = self.P
        t = self.t
        xs = t["xt"][:, s, :]
        if partial:
            P.op("dve", lambda e: e.reduce_sum(out=t["ss"][:, s:s + 1], in_=t["ssp"][:, 4 * s:4 * s + 4], axis=mybir.AxisListType.X),
                 reads=[("ssp", s, dg) for dg in range(4)], writes=[("ss", s)])
        else:
            P.op("act", lambda e: e.activation(out=junk, in_=xs, func=AF.Square, accum_out=t["ss"][:, s:s + 1]),
                 reads=self.xkeys(s), writes=junkk + [("ss", s)])
        P.op("act", lambda e: e.activation(out=t["sq"][:, s:s + 1], in_=t["ss"][:, s:s + 1], func=AF.Sqrt, scale=1.0 / D, bias=t["eps"][:]),
             reads=[("ss", s), ("c", "eps")], writes=[("sq", s)])
        P.op("dve", lambda e: e.reciprocal(out=t["rstd"][:, s:s + 1], in_=t["sq"][:, s:s + 1]),
             reads=[("sq", s)], writes=[("rstd", s)])

    def stats_batch(self):
        P = self.P
        t = self.t
        allp = [("ssp", s, dg) for s in range(4) for dg in range(4)]
        P.op("dve", lambda e: e.reduce_sum(out=t["ss"][:, 0:4], in_=t["ssp"][:, 0:16].rearrange("p (s d) -> p s d", d=4), axis=mybir.AxisListType.X),
             reads=allp, writes=[("ss", s) for s in range(4)])
        P.op("act", lambda e: e.activation(out=t["sq"][:, 0:4], in_=t["ss"][:, 0:4], func=AF.Sqrt, scale=1.0 / D, bias=t["eps"][:]),
             reads=[("ss", s) for s in range(4)] + [("c", "eps")], writes=[("sq", s) for s in range(4)])
        P.op("dve", lambda e: e.reciprocal(out=t["rstd"][:, 0:4], in_=t["sq"][:, 0:4]),
             reads=[("sq", s) for s in range(4)], writes=[("rstd", s) for s in range(4)])

    def sq_partial(self, s, dg):
        P = self.P
        t = self.t
        xs = t["xt"][:, s, dg * 512:(dg + 1) * 512]
        P.op("act", lambda e: e.activation(out=t["sgt"][:], in_=xs, func=AF.Square, accum_out=t["ssp"][:, 4 * s + dg:4 * s + dg + 1]),
             reads=[("x", s, dg)], writes=[("sgt",), ("ssp", s, dg)])

    def norm_T(self, gi, partial=False):
        P = self.P
        t = self.t
        dry = self.dry
        if partial and not dry:
            self.stats_batch()
        for s in range(4):
            hn, hk = self.S(4 * s, 4)
            if not dry:
                xs = t["xt"][:, s, :]
                if not partial:
                    self.stats(s, False, hn, hk)
                if s == 3:
                    P.op("act", lambda e, hn=hn, xs=xs, s=s: e.activation(out=hn, in_=xs, func=AF.Copy, scale=t["rstd"][:, s:s + 1]),
                         reads=self.xkeys(s) + [("rstd", s)], writes=hk)
                else:
                    P.op("dve", lambda e, hn=hn, xs=xs, s=s: e.tensor_scalar(out=hn, in0=xs, scalar1=t["rstd"][:, s:s + 1], scalar2=0.0, op0=ALU.mult, op1=ALU.add),
                         reads=self.xkeys(s) + [("rstd", s)], writes=hk)
        for c in range(16):
            bank, bk = self.alloc()
            if dry:
                continue
            bb = bank[:].bitcast(BF16)
            for s in range(4):
                hn, hk = self.S(4 * s, 4)
                P.op("pe", lambda e, bb=bb, hn=hn, s=s, c=c: e.transpose(out=bb[:, s * 128:(s + 1) * 128], in_=hn[:, c * 128:(c + 1) * 128], identity=t["ident"][:]),
                     reads=hk + [("c", "ident")], writes=[bk], signal=(s == 3))
            eng = "act" if c % 2 == 0 else "dve"
            if eng == "act":
                P.op("act", lambda e, bb=bb, c=c: e.activation(out=t["hT"][:, c, :], in_=bb[:, 0:TT], func=AF.Copy, scale=t["gains"][:, gi, c:c + 1]),
                     reads=[bk, ("c", "gains")], writes=[("hT", c)])
            else:
                P.op("dve", lambda e, bb=bb, c=c: e.tensor_scalar(out=t["hT"][:, c, :], in0=bb[:, 0:TT], scalar1=t["gains"][:, gi, c:c + 1], scalar2=0.0, op0=ALU.mult, op1=ALU.add),
                     reads=[bk, ("c", "gains")], writes=[("hT", c)])

    def ffn(self, T, gi):
        P = self.P
        t = self.t
        dry = self.dry
        w_in = "w_ffn1_in" if gi == 0 else "w_ffn2_in"
        w_out = "w_ffn1_out" if gi == 0 else "w_ffn2_out"
        self.norm_T(gi, partial=(gi == 2))
        if gi == 0 and T == 0 and not dry:
            for c in range(16):
                self.dump_any("d_hT", c * 128, t["hT"][:, c, :], [("hT", c)])
        for f in range(44):
            wg, kg, hg = self.wnext((w_in, 0, 16, 128 * f, 128))
            wu, ku, hu = self.wnext((w_in, 0, 16, DFF + 128 * f, 128))
            bg, bgk = self.alloc()
            for c in range(16):
                if not dry:
                    self.mm(bg[:], wg[:, c, :], t["hT"][:, c, :], c == 0, c == 15, kg + [("hT", c)], bgk)
            self.wdone(hg)
            bu, buk = self.alloc()
            for c in range(16):
                if not dry:
                    self.mm(bu[:], wu[:, c, :], t["hT"][:, c, :], c == 0, c == 15, ku + [("hT", c)], buk)
            self.wdone(hu)
            a, ak = self.S(f, 1)
            if not dry and gi == 0 and T == 0 and f in (0, 1, 43):
                fi = (0, 1, 43).index(f)
                self.dump_any("d_gate", fi * 128, bg[:], [bgk])
                self.dump_any("d_up", fi * 128, bu[:], [buk])
            if not dry:
                P.op("act", lambda e, bg=bg: e.activation(out=t["sgt"][:], in_=bg[:], func=AF.Silu), reads=[bgk], writes=[("sgt",)])
                P.op("dve", lambda e, bu=bu, a=a: e.tensor_tensor(out=a, in0=bu[:], in1=t["sgt"][:], op=ALU.mult), reads=[buk, ("sgt",)], writes=ak)
        if gi == 0 and T == 0 and not dry:
            for fi, f in enumerate((0, 1, 43)):
                a, ak = self.S(f, 1)
                self.dump_any("d_act", fi * 128, a, ak)
        fgs = [(0, 8), (8, 8), (16, 8), (24, 8), (32, 8), (40, 4)]
        for dg in range(4):
            banks = [self.alloc() for _ in range(4)]
            for (f0, nf) in fgs:
                wo, ko, ho = self.wnext((w_out, f0, nf, 512 * dg, 512))
                for s in range(4):
                    for r in range(nf):
                        f = f0 + r
                        if not dry:
                            a, ak = self.S(f, 1)
                            last_use = (s == 3 and r == nf - 1)
                            self.mm(banks[s][0][:], a[:, s * 128:(s + 1) * 128], wo[:, r, :], f == 0, f == 43, ak + ko, banks[s][1],
                                    signal=(f == 43 or last_use))
                self.wdone(ho)
            for s in range(4):
                if not dry:
                    xs = t["xt"][:, s, dg * 512:(dg + 1) * 512]
                    P.op("dve", lambda e, b=banks[s][0], xs=xs: e.scalar_tensor_tensor(out=xs, in0=b[:], scalar=0.5, in1=xs, op0=ALU.mult, op1=ALU.add),
                         reads=[banks[s][1], ("x", s, dg)], writes=[("x", s, dg)])
                    self.sq_partial(s, dg)
        if "x1" in self.dbg and gi == 0 and not dry:
            for s in range(4):
                self.dump("x1", T, t["xt"][:, s, :], self.xkeys(s), rows=self.dr["x1"][T * TT + s * 128:T * TT + (s + 1) * 128, :])

    def rope_A(self, w, wk, i, g, dst, dstk, wh=None):
        P = self.P
        t = self.t
        dry = self.dry
        bank, bk = self.alloc()
        for c in range(16):
            if not dry:
                self.mm(bank[:], w[:, c, i * 128:(i + 1) * 128], t["hT"][:, c, :], c == 0, c == 15, wk + [("hT", c)], bk)
        if wh is not None:
            self.wdone(wh)
        par = self.rope_par = 1 - getattr(self, "rope_par", 0)
        r32, rk = self.S(28 + 2 * par, 2, f32=True)
        r16, r16k = self.S(36 + par, 1)
        if not dry:
            P.op("act", lambda e: e.activation(out=blk(dst), in_=perm(g, bank[:]), func=AF.Copy), reads=[bk], writes=dstk)
            P.op("act", lambda e: e.activation(out=r16[0:32, :], in_=bank[0:32, :], func=AF.Copy), reads=[bk], writes=r16k)
            P.op("act", lambda e: e.activation(out=r32[0:32, :], in_=bank[0:32, :], func=AF.Copy), reads=[bk], writes=rk)
        return (r32, rk, dst, dstk, g, r16, r16k)

    def rope_B(self, st):
        if st is None:
            return
        P = self.P
        t = self.t
        r32, rk, dst, dstk, g, r16, r16k = st
        t1, t1k = self.S(32, 2, f32=True)
        t2, t2k = self.S(34, 2, f32=True)
        sw, swk = self.alloc()
        if self.dry:
            return
        self.mm(sw[0:32, :], t["pswapb"][:], r16[0:32, :], True, True, r16k + [("c", "pswapb")], swk)
        P.op("dve", lambda e: e.tensor_tensor(out=t1[0:32, :], in0=r32[0:32, :], in1=t["cosT"][:], op=ALU.mult), reads=rk + [("tab",)], writes=t1k)
        P.op("dve", lambda e: e.tensor_tensor(out=t2[0:32, :], in0=sw[0:32, :], in1=t["sinT"][:], op=ALU.mult), reads=[swk, ("tab",)], writes=t2k)
        P.op("dve", lambda e: e.tensor_tensor(out=blk(dst[0:32, :]), in0=perm(g, t1[0:32, :]), in1=perm(g, t2[0:32, :]), op=ALU.add),
             reads=t1k + t2k, writes=dstk)

    def mixer(self, T, own):
        P = self.P
        t = self.t
        dry = self.dry
        dr = self.dr
        sl = [T % 2, T % 2, T % 5]
        last_halo = (T == N_HALO_T - 1)
        if not dry:
            P.dma("sp", t["cosT"][:], dr["cosd"][T], self.msem[8], writes=[("tab",)])
            P.dma("sp", t["sinT"][:], dr["sind"][T], self.msem[9], writes=[("tab",)])
        self.norm_T(1, partial=True)
        if not own:
            self.load_x(T + 1)
        pending = None
        kv_groups = (0, 1, 2) if (own or last_halo) else (2,)
        if own:
            for u in range(6):
                w, wk, wh = self.wnext(("w_in", 0, 16, OFF_Q + 256 * u, 256))
                for i in range(2):
                    h = 2 * u + i
                    dst, dstk = self.S(16 + h, 1)
                    st = self.rope_A(w, wk, i, h // 4, dst, dstk, wh if i == 1 else None)
                    self.rope_B(pending)
                    pending = st
        for u in range(6):
            if u // 2 not in kv_groups:
                continue
            w, wk, wh = self.wnext(("w_in", 0, 16, OFF_K + 256 * u, 256))
            for i in range(2):
                h = 2 * u + i
                g, j = h // 4, h % 4
                dst = DUMMY if dry else t[f"KT{g}"][:, sl[g], j, :]
                st = self.rope_A(w, wk, i, g, dst, [("KT", g, sl[g], j)], wh if i == 1 else None)
                self.rope_B(pending)
                pending = st
        for u in range(6):
            if u // 2 not in kv_groups:
                continue
            w, wk, wh = self.wnext(("w_in", 0, 16, OFF_V + 256 * u, 256))
            g, half = u // 2, u % 2
            for qbp in range(2):
                bank, bk = self.alloc()
                for qq in range(2):
                    qb = 2 * qbp + qq
                    for c in range(16):
                        if not dry:
                            hc = t["hT"][:, c, :]
                            lhsT = hc[:, qb * 128:(qb + 1) * 128] if g == 0 else hc[:, qb:TT:4]
                            self.mm(bank[:, qq * 256:(qq + 1) * 256], lhsT, w[:, c, :], c == 0, c == 15, wk + [("hT", c)], bk)
                if not dry:
                    dst = t[f"V{g}"][:, sl[g], 2 * qbp:2 * qbp + 2, half * 256:(half + 1) * 256]
                    P.op("act", lambda e, dst=dst, bank=bank: e.activation(out=dst, in_=bank[:].rearrange("p (q n) -> p q n", q=2), func=AF.Copy),
                         reads=[bk], writes=[("V", g, sl[g], qbp, half)])
                if pending is not None:
                    self.rope_B(pending)
                    pending = None
            self.wdone(wh)
        if not own:
            if last_halo:
                self.conv_halo()
            return
        self.attention(T, sl)
        self.conv()
        self.merge_out()
        if "x2" in self.dbg and not dry:
            for s in range(4):
                self.dump("x2", T, t["xt"][:, s, :], self.xkeys(s), rows=dr["x2"][T * TT + s * 128:T * TT + (s + 1) * 128, :])

    def attention(self, T, sl):
        pend = None
        idx = 0
        for j in range(4):
            for g in range(3):
                st = self.attn_A(T, sl, j, g, idx % 2)
                if pend is not None:
                    self.attn_B(pend)
                pend = st
                idx += 1
        self.attn_B(pend)

    def attn_A(self, T, sl, j, g, pbuf):
        P = self.P
        t = self.t
        dry = self.dry
        halo = lambda TT_: TT_ < N_HALO_T
        h = 4 * g + j
        QT, QTk = self.S(16 + h, 1)
        deltas = []
        if g == 0:
            own_l, prev_l = [], []
            for qb in range(4):
                own_l.append((sl[0], qb))
                prev_l.append((sl[0], qb - 1) if qb > 0 else ((T - 1) % 2, 3))
            deltas.append((own_l, [0] * 4))
            deltas.append((prev_l, [2 if (halo(T - 1)) else 1] + [1] * 3))
        elif g == 1:
            deltas.append(([(sl[1], qb) for qb in range(4)], [0] * 4))
            deltas.append(([((T - 1) % 2, qb) for qb in range(4)], [2 if halo(T - 1) else 1] * 4))
        else:
            for dl in range(5):
                Tk = T - dl
                if dl == 0:
                    mi = 3
                elif dl < 4:
                    mi = 6 if halo(Tk) else 4
                else:
                    mi = 7 if halo(Tk) else 5
                deltas.append(([(Tk % 5, qb) for qb in range(4)], [mi] * 4))
        pts = []
        for di, (blks, mis) in enumerate(deltas):
            sc, sck = self.alloc()
            pt, ptk = self.S(5 * pbuf + di, 1)
            pts.append((pt, ptk))
            if dry:
                continue
            for qb, (slot, kb) in enumerate(blks):
                self.mm(sc[:, qb * 128:(qb + 1) * 128], t[f"KT{g}"][:, slot, j, kb * 128:(kb + 1) * 128], QT[:, qb * 128:(qb + 1) * 128],
                        True, True, [("KT", g, slot, j)] + QTk, sck, signal=(qb == 3))
            P.op("act", lambda e, pt=pt, sc=sc: e.activation(out=pt, in_=sc[:], func=AF.Exp, scale=SCALE), reads=[sck], writes=ptk)
            if len(set(mis)) == 1:
                mi = mis[0]
                P.op("dve", lambda e, pt=pt, mi=mi: e.tensor_tensor(out=blk(pt), in0=blk(pt), in1=t["masks"][:, mi, :].unsqueeze(1).to_broadcast([128, 4, 128]), op=ALU.mult),
                     reads=ptk + [("c", "masks")], writes=ptk)
            else:
                mi0, mi = mis[0], mis[1]
                P.op("dve", lambda e, pt=pt, mi0=mi0: e.tensor_tensor(out=pt[:, 0:128], in0=pt[:, 0:128], in1=t["masks"][:, mi0, :], op=ALU.mult),
                     reads=ptk + [("c", "masks")], writes=ptk)
                P.op("dve", lambda e, pt=pt, mi=mi: e.tensor_tensor(out=pt[:, 128:512].rearrange("p (q n) -> p q n", q=3), in0=pt[:, 128:512].rearrange("p (q n) -> p q n", q=3),
                                                                    in1=t["masks"][:, mi, :].unsqueeze(1).to_broadcast([128, 3, 128]), op=ALU.mult),
                     reads=ptk + [("c", "masks")], writes=ptk)
        return (j, g, deltas, pts)

    def attn_B(self, st):
        P = self.P
        t = self.t
        dry = self.dry
        j, g, deltas, pts = st
        buf = j % 2
        OT, OTk = self.S(28 + 2 * buf, 2, f32=True)
        LB, LBk = self.S(32 + 2 * buf, 2, f32=True)
        ob, obk = self.alloc()
        lb, lbk = self.alloc()
        nd = len(deltas)
        if not dry:
            for qb in range(4):
                for di, (blks, mis) in enumerate(deltas):
                    slot, kb = blks[qb]
                    pt, ptk = pts[di]
                    self.mm(ob[:, qb * 128:(qb + 1) * 128], t[f"V{g}"][:, slot, kb, j * 128:(j + 1) * 128], pt[:, qb * 128:(qb + 1) * 128],
                            di == 0, di == nd - 1, [("V", g, slot, kb // 2, j // 2)] + ptk, obk, signal=(qb == 3 and di == nd - 1))
            for di in range(nd):
                pt, ptk = pts[di]
                self.mm(lb[:], t["ones"][:], pt, di == 0, di == nd - 1, ptk + [("c", "ones")], lbk)
            if g == 0:
                P.op("dve", lambda e: e.tensor_copy(out=OT, in_=ob[:]), reads=[obk], writes=OTk)
                P.op("dve", lambda e: e.tensor_copy(out=LB, in_=lb[:]), reads=[lbk], writes=LBk)
            else:
                P.op("dve", lambda e: e.tensor_tensor(out=perm(g, OT), in0=blk(ob[:]), in1=perm(g, OT), op=ALU.add), reads=[obk] + OTk, writes=OTk)
                P.op("dve", lambda e: e.tensor_tensor(out=perm(g, LB), in0=blk(lb[:]), in1=perm(g, LB), op=ALU.add), reads=[lbk] + LBk, writes=LBk)
        if g == 2:
            rc, rck = self.S(10, 2, f32=True)
            on, onk = self.S(40 + j, 1)
            if not dry:
                P.op("dve", lambda e: e.reciprocal(out=rc, in_=LB), reads=LBk, writes=rck)
                P.op("dve", lambda e: e.tensor_tensor(out=on, in0=OT, in1=rc, op=ALU.mult), reads=OTk + rck, writes=onk)

    def conv_halo(self):
        P = self.P
        t = self.t
        dry = self.dry
        for ch in range(8):
            wx, wxk, hx = self.wnext(("w_in", 0, 16, OFF_XC + 128 * ch, 128))
            wc, wck, hc = self.wnext(("w_in", 0, 16, OFF_C + 128 * ch, 128))
            bx, bxk = self.alloc()
            bc, bck = self.alloc()
            xs, xsk = self.S(36, 2, f32=True)
            if not dry:
                for c in range(16):
                    self.mm(bx[:, 0:2], wx[:, c, :], t["hT"][:, c, TT - 2:TT], c == 0, c == 15, wxk + [("hT", c)], bxk)
                for c in range(16):
                    self.mm(bc[:, 0:2], wc[:, c, :], t["hT"][:, c, TT - 2:TT], c == 0, c == 15, wck + [("hT", c)], bck)
                P.op("act", lambda e, bx=bx, xs=xs: e.activation(out=xs[:, 0:2], in_=bx[:, 0:2], func=AF.Copy), reads=[bxk], writes=xsk)
                P.op("dve", lambda e, bc=bc, xs=xs, ch=ch: e.tensor_tensor(out=t["uh"][:, ch, :], in0=bc[:, 0:2], in1=xs[:, 0:2], op=ALU.mult),
                     reads=[bck] + xsk, writes=[("uh", ch)])
            self.wdone(hx)
            self.wdone(hc)

    def conv(self):
        P = self.P
        t = self.t
        dry = self.dry
        for ch in range(8):
            wx, wxk, hx = self.wnext(("w_in", 0, 16, OFF_XC + 128 * ch, 128))
            wc, wck, hc = self.wnext(("w_in", 0, 16, OFF_C + 128 * ch, 128))
            wb, wbk, hb = self.wnext(("w_in", 0, 16, OFF_B + 128 * ch, 128))
            for _one in range(1):
                bx, bxk = self.alloc()
                bc, bck = self.alloc()
                bb, bbk = self.alloc()
                xs, xsk = self.S(36, 2, f32=True)
                acc, acck = self.S(38, 2, f32=True)
                ub, ubk = self.S(8, 3, f32=True)
                cv, cvk = self.S(28 + ch, 1)
                if dry:
                    continue
                for (b, bk_, w_, wk_) in ((bx, bxk, wx, wxk), (bc, bck, wc, wck), (bb, bbk, wb, wbk)):
                    for c in range(16):
                        self.mm(b[:], w_[:, c, :], t["hT"][:, c, :], c == 0, c == 15, wk_ + [("hT", c)], bk_)
                self.wdone(hx)
                self.wdone(hc)
                self.wdone(hb)
                cw = t["wcv"]
                P.op("act", lambda e, bx=bx, xs=xs: e.activation(out=xs, in_=bx[:], func=AF.Copy), reads=[bxk], writes=xsk)
                P.op("dve", lambda e, ub=ub, ch=ch: e.tensor_copy(out=ub[:, 0:2], in_=t["uh"][:, ch, :]), reads=[("uh", ch)], writes=ubk)
                P.op("dve", lambda e, ub=ub, bc=bc, xs=xs: e.tensor_tensor(out=ub[:, 2:514], in0=bc[:], in1=xs, op=ALU.mult), reads=[bck] + xsk, writes=ubk)
                P.op("dve", lambda e, ub=ub, ch=ch: e.tensor_copy(out=t["uh"][:, ch, :], in_=ub[:, 512:514]), reads=ubk, writes=[("uh", ch)])
                P.op("dve", lambda e, ub=ub, acc=acc, ch=ch: e.tensor_scalar(out=acc, in0=ub[:, 2:514], scalar1=cw[:, ch, 2:3], scalar2=0.0, op0=ALU.mult, op1=ALU.add),
                     reads=ubk + [("c", "wcv")], writes=acck)
                P.op("dve", lambda e, ub=ub, acc=acc, ch=ch: e.scalar_tensor_tensor(out=acc, in0=ub[:, 1:513], scalar=cw[:, ch, 1:2], in1=acc, op0=ALU.mult, op1=ALU.add),
                     reads=ubk + acck + [("c", "wcv")], writes=acck)
                P.op("dve", lambda e, ub=ub, acc=acc, ch=ch: e.scalar_tensor_tensor(out=acc, in0=ub[:, 0:512], scalar=cw[:, ch, 0:1], in1=acc, op0=ALU.mult, op1=ALU.add),
                     reads=ubk + acck + [("c", "wcv")], writes=acck)
                P.op("dve", lambda e, bb=bb, acc=acc, cv=cv: e.tensor_tensor(out=cv, in0=bb[:], in1=acc, op=ALU.mult), reads=[bbk] + acck, writes=cvk)

    def merge_out(self):
        P = self.P
        t = self.t
        dry = self.dry
        for d in range(16):
            wgc, kgc, hgc = self.wnext(("w_in", 0, 16, OFF_GC + 128 * d, 128))
            wga, kga, hga = self.wnext(("w_in", 0, 16, OFF_GA + 128 * d, 128))
            wco, kco, hco = self.wnext(("w_conv_out", 0, 8, 128 * d, 128))
            wao, kao, hao = self.wnext(("w_attn_out", 0, 4, 128 * d, 128))
            for _one in range(1):
                b1, b1k = self.alloc()
                b2, b2k = self.alloc()
                b3, b3k = self.alloc()
                b4, b4k = self.alloc()
                par = d % 2
                s1, s1k = self.S(16 + 2 * par, 2, f32=True)
                s2, s2k = self.S(20 + 2 * par, 2, f32=True)
                m1, m1k = self.S(24, 2, f32=True)
                m2, m2k = self.S(26, 2, f32=True)
                mg, mgk = self.S(d, 1)
                if dry:
                    continue
                for c in range(16):
                    self.mm(b1[:], wgc[:, c, :], t["hT"][:, c, :], c == 0, c == 15, kgc + [("hT", c)], b1k)
                for c in range(16):
                    self.mm(b2[:], wga[:, c, :], t["hT"][:, c, :], c == 0, c == 15, kga + [("hT", c)], b2k)
                for ch in range(8):
                    cv, cvk = self.S(28 + ch, 1)
                    self.mm(b3[:], wco[:, ch, :], cv, ch == 0, ch == 7, kco + cvk, b3k)
                for jj in range(4):
                    on, onk = self.S(40 + jj, 1)
                    self.mm(b4[:], wao[:, jj, :], on, jj == 0, jj == 3, kao + onk, b4k)
                for hh in (hgc, hga, hco, hao):
                    self.wdone(hh)
                P.op("act", lambda e, b1=b1, s1=s1: e.activation(out=s1, in_=b1[:], func=AF.Sigmoid), reads=[b1k], writes=s1k)
                P.op("act", lambda e, b2=b2, s2=s2: e.activation(out=s2, in_=b2[:], func=AF.Sigmoid), reads=[b2k], writes=s2k)
                P.op("dve", lambda e, b3=b3, s1=s1, m1=m1: e.tensor_tensor(out=m1, in0=b3[:], in1=s1, op=ALU.mult), reads=[b3k] + s1k, writes=m1k)
                P.op("dve", lambda e, b4=b4, s2=s2, m2=m2: e.tensor_tensor(out=m2, in0=b4[:], in1=s2, op=ALU.mult), reads=[b4k] + s2k, writes=m2k)
                P.op("dve", lambda e, m1=m1, m2=m2, mg=mg: e.tensor_tensor(out=mg, in0=m1, in1=m2, op=ALU.add), reads=m1k + m2k, writes=mgk)
        for dg in range(4):
            banks = [self.alloc() for _ in range(4)]
            for half in range(2):
                wo, ko, ho = self.wnext(("w_o", 8 * half, 8, 512 * dg, 512))
                for s in range(4):
                    for r in range(8):
                        dch = 8 * half + r
                        if not dry:
                            mg, mgk = self.S(dch, 1)
                            self.mm(banks[s][0][:], mg[:, s * 128:(s + 1) * 128], wo[:, r, :], dch == 0, dch == 15, mgk + ko, banks[s][1],
                                    signal=(dch == 15 or (s == 3 and r == 7)))
                self.wdone(ho)
            for s in range(4):
                if not dry:
                    xs = t["xt"][:, s, dg * 512:(dg + 1) * 512]
                    P.op("dve", lambda e, b=banks[s][0], xs=xs: e.tensor_tensor(out=xs, in0=b[:], in1=xs, op=ALU.add),
                         reads=[banks[s][1], ("x", s, dg)], writes=[("x", s, dg)])
                    self.sq_partial(s, dg)

    def final(self, T):
        P = self.P
        t = self.t
        if self.dry:
            return
        self.stats_batch()
        obs = []
        for s in range(4):
            xs = t["xt"][:, s, :]
            ob, obk = self.S(12 + 8 * s, 8, f32=True)
            P.op("dve", lambda e, xs=xs, s=s, ob=ob: e.scalar_tensor_tensor(out=ob, in0=xs, scalar=t["rstd"][:, s:s + 1], in1=t["gF"][:], op0=ALU.mult, op1=ALU.mult),
                 reads=self.xkeys(s) + [("rstd", s), ("c", "gF")], writes=obk)
            obs.append((ob, obk))
            self.load_x(T + 1, [s])
        for s in range(4):
            ob, obk = obs[s]
            r0 = (T - N_HALO_T) * TT + s * 128
            tok = P.dma("sp", self.dr["y"][r0:r0 + 128, :], ob, self.osem[s], reads=obk)
            self.store_toks.append(tok)


def _const_inputs(jchunk):
    p = np.arange(128)
    k = p[:, None]
    q = p[None, :]
    same = (k % 4) == (q % 4)
    valid = 1.0 if jchunk > 0 else 0.0
    m = np.zeros((128, 8, 128), np.float32)
    m[:, 0] = (k <= q)
    m[:, 1] = (k >= q)
    m[:, 2] = (k >= q) * valid
    m[:, 3] = same & (k <= q)
    m[:, 4] = same
    m[:, 5] = same & (k >= q)
    m[:, 6] = same * valid
    m[:, 7] = (same & (k >= q)) * valid
    psw = np.zeros((32, 32), np.float32)
    for i in range(16):
        psw[i + 16, i] = 1.0
        psw[i, i + 16] = 1.0
    pos = (jchunk * T_OWN - N_HALO_T * TT + np.arange(T_ALL)).astype(np.float32)
    half = 16
    inv_freq = (np.float32(ROPE_THETA) ** (-(np.arange(half, dtype=np.float32) * np.float32(2.0)) / np.float32(32))).astype(np.float32)
    ang = (pos[:, None] * inv_freq[None, :]).astype(np.float32)
    cos = np.cos(ang).astype(np.float32).T
    sin = np.sin(ang).astype(np.float32).T
    cosd = np.concatenate([cos, cos], 0).reshape(32, NT, TT).transpose(1, 0, 2)
    sind = np.concatenate([-sin, sin], 0).reshape(32, NT, TT).transpose(1, 0, 2)
    return {
        "masks": m.astype(ml_dtypes.bfloat16),
        "ident": np.eye(128, dtype=np.float32).astype(ml_dtypes.bfloat16),
        "ones": np.ones((128, 128), np.float32).astype(ml_dtypes.bfloat16),
        "pswapb": psw.astype(ml_dtypes.bfloat16),
        "cosd": np.ascontiguousarray(cosd, dtype=np.float32),
        "sind": np.ascontiguousarray(sind, dtype=np.float32),
    }


def make_in_maps(inputs):
    x = np.asarray(inputs["x"], dtype=np.float32)
    shared = {}
    for nm in ["w_ffn1_in", "w_ffn1_out", "w_in", "w_conv_out", "w_attn_out", "w_o", "w_ffn2_in", "w_ffn2_out"]:
        shared[nm] = np.ascontiguousarray(np.asarray(inputs[nm], dtype=np.float32)[0])
    gains = np.stack([np.asarray(inputs[k], np.float32)[0].reshape(16, 128).T for k in ["ffn1_norm", "mix_norm", "ffn2_norm"]], axis=1)
    shared["gains"] = np.ascontiguousarray(gains)
    shared["gF"] = np.ascontiguousarray(np.broadcast_to(np.asarray(inputs["final_norm"], np.float32)[None, :], (128, D)))
    wc = np.asarray(inputs["w_conv"], np.float32)[0]
    shared["wcv"] = np.ascontiguousarray(wc.reshape(3, 8, 128).transpose(2, 1, 0))
    in_maps = []
    for c in range(NC):
        b, j = c // 4, c % 4
        xin = np.zeros((T_ALL, D), np.float32)
        xin[N_HALO_T * TT:] = x[b, j * T_OWN:(j + 1) * T_OWN]
        if j > 0:
            xin[:N_HALO_T * TT] = x[b, j * T_OWN - N_HALO_T * TT:j * T_OWN]
        m = dict(shared)
        m["xin"] = xin
        m.update(_const_inputs(j))
        in_maps.append(m)
    return in_maps


_NC_CACHE = {}


def kernel(**inputs):
    if "nc" not in _NC_CACHE:
        _NC_CACHE["nc"] = Builder().build()
    nc = _NC_CACHE["nc"]
    in_maps = make_in_maps(inputs)
    res = run_bass_kernel_spmd(nc, in_maps, core_ids=list(range(NC)))
    out = np.zeros((2, 16384, D), np.float32)
    for c in range(NC):
        b, j = c // 4, c % 4
        out[b, j * T_OWN:(j + 1) * T_OWN] = res.results[c]["y"]
    return out
```

```python
import numpy as np
import ml_dtypes
from contextlib import ExitStack
import concourse.bass as bass
import concourse.mybir as mybir
from concourse.bass_utils import run_bass_kernel_spmd

F32 = mybir.dt.float32
BF16 = mybir.dt.bfloat16
AF = mybir.ActivationFunctionType
ALU = mybir.AluOpType

D = 2048
DFF = 5632
NC = 8
TT = 512
N_HALO_T = 4
N_OWN_T = 8
NT = N_HALO_T + N_OWN_T
T_OWN = N_OWN_T * TT
T_ALL = NT * TT
NGRAN = 12
GRAN = 1024
NWSEM = 8
CAST_LEAD = 120
NCAST = 8
NPS = 8
OFF_Q, OFF_K, OFF_V, OFF_XC, OFF_B, OFF_C, OFF_GC, OFF_GA = 0, 1536, 3072, 4608, 5632, 6656, 7680, 9728
ROPE_THETA = 500000.0
SCALE = 128.0 ** -0.5
EPS = 1e-5


class _Dummy:
    def __getitem__(self, k):
        return self

    def __getattr__(self, k):
        return lambda *a, **kw: self


DUMMY = _Dummy()


class EngW:
    def __init__(self, name, sem, self_sync):
        self.name = name
        self.sem = sem
        self.count = 0
        self.waited = {}
        self.self_sync = self_sync
        self.items = []

    def wait(self, tok):
        sem, val = tok
        if sem is self.sem and not self.self_sync:
            return
        k = id(sem)
        if self.waited.get(k, 0) >= val:
            return
        self.waited[k] = val
        self.items.append(("wait", sem, val))


class Prog:
    def __init__(self, sems):
        self.dry = False
        self.E = {
            "pe": EngW("pe", sems["pe"], False),
            "act": EngW("act", sems["act"], True),
            "dve": EngW("dve", sems["dve"], True),
            "pool": EngW("pool", sems["pool"], True),
            "sp": EngW("sp", sems["sp"], False),
        }
        self.lastw = {}
        self.readers = {}
        self.dma_cnt = {}

    def _deps(self, reads, writes):
        d = []
        for k in reads:
            t = self.lastw.get(k)
            if t is not None:
                d.append(t)
        for k in writes:
            t = self.lastw.get(k)
            if t is not None:
                d.append(t)
            r = self.readers.get(k)
            if r:
                d.extend(r.values())
        return d

    def _record(self, tok, reads, writes):
        for k in reads:
            r = self.readers.setdefault(k, {})
            sid = id(tok[0])
            old = r.get(sid)
            if old is None or old[1] < tok[1]:
                r[sid] = tok
        for k in writes:
            self.lastw[k] = tok
            self.readers[k] = {}

    def op(self, eng, fn, reads=(), writes=(), signal=True):
        if self.dry:
            return
        E = self.E[eng]
        px = [k for k in reads if k[0] == "ps"]
        if px:
            reads = [k for k in reads if k[0] != "ps"]
            writes = list(writes) + px
        for tok in self._deps(reads, writes):
            E.wait(tok)
        if signal:
            E.count += 1
            tok = (E.sem, E.count)
        else:
            tok = (E.sem, E.count + 1)
        E.items.append(("op", fn, signal))
        self._record(tok, reads, writes)

    def dma(self, eng, out_ap, in_ap, sem, reads=(), writes=()):
        if self.dry:
            return None
        E = self.E[eng]
        for tok in self._deps(reads, writes):
            E.wait(tok)
        c = self.dma_cnt.get(id(sem), 0)
        if c > 0:
            E.wait((sem, c))
        c += 16
        self.dma_cnt[id(sem)] = c
        E.items.append(("dma", out_ap, in_ap, sem))
        self._record((sem, c), reads, writes)
        return (sem, c)

    def wait_tok(self, eng, tok):
        if not self.dry and tok is not None:
            self.E[eng].wait(tok)

    def replay(self, eng, h):
        E = self.E[eng]
        for it in E.items:
            if it[0] == "wait":
                h.wait_ge(it[1], it[2])
            elif it[0] == "op":
                ins = it[1](h)
                if it[2]:
                    ins.then_inc(E.sem, 1)
            else:
                h.dma_start(out=it[1], in_=it[2]).then_inc(it[3], 16)


def perm(g, ap):
    if g == 0:
        return ap.rearrange("p (q n) -> p q n", q=4)
    return ap.rearrange("p (n q) -> p q n", q=4)


def blk(ap):
    return ap.rearrange("p (q n) -> p q n", q=4)


class Builder:
    def __init__(self, n_tiles=NT, dbg=None):
        self.n_tiles = n_tiles
        self.dbg = dbg or {}

    def build(self):
        nc = bass.Bass("TRN2", target_bir_lowering=False)
        self.nc = nc
        dr = {}

        def din(name, shape, dt=F32):
            dr[name] = nc.dram_tensor(name, shape, dt, kind="ExternalInput").ap()

        din("xin", [T_ALL, D])
        din("w_ffn1_in", [D, 2 * DFF]); din("w_ffn1_out", [DFF, D])
        din("w_in", [D, 11776]); din("w_conv_out", [1024, D]); din("w_attn_out", [512, D]); din("w_o", [D, D])
        din("w_ffn2_in", [D, 2 * DFF]); din("w_ffn2_out", [DFF, D])
        din("gains", [128, 3, 16]); din("gF", [128, D]); din("wcv", [128, 8, 3])
        din("cosd", [NT, 32, TT]); din("sind", [NT, 32, TT])
        din("masks", [128, 8, 128], BF16); din("ident", [128, 128], BF16); din("ones", [128, 128], BF16)
        din("pswapb", [32, 32], BF16)
        dr["y"] = nc.dram_tensor("y", [T_OWN, D], F32, kind="ExternalOutput").ap()
        for k, shp in self.dbg.items():
            dr[k] = nc.dram_tensor(k, shp, F32, kind="ExternalOutput").ap()
        self.dr = dr

        self.P = Prog({k: None for k in ["pe", "act", "dve", "pool", "sp"]})
        self.P.dry = True
        self.dry = True
        self.useq = []
        self.t = {}
        self.emit_all()
        useq = self.useq
        uniq = []
        seen = {}
        for s in useq:
            if s not in seen:
                seen[s] = len(uniq)
                uniq.append(s)
        self.uidx = seen
        self.uoff = {}
        off = 0
        for sp in uniq:
            self.uoff[sp] = off
            off += 128 * sp[2] * sp[4]
        dr["scr"] = nc.dram_tensor("scr", [off], BF16, kind="Internal").ap()

        with ExitStack() as es:
            def sb(name, shape, dt):
                return es.enter_context(nc.sbuf_tensor("sb_" + name, shape, dt))

            def sem(name):
                return es.enter_context(nc.semaphore(name))

            t = {}
            t["xt"] = sb("xt", [128, 4, D], F32)
            t["hT"] = sb("hT", [128, 16, TT], BF16)
            t["sB"] = sb("sB", [128, 44 * 512], BF16)
            for g, ns in enumerate([2, 2, 5]):
                t[f"KT{g}"] = sb(f"KT{g}", [128, ns, 4, TT], BF16)
                t[f"V{g}"] = sb(f"V{g}", [128, ns, 4, TT], BF16)
            t["wring"] = sb("wring", [128, NGRAN * GRAN], BF16)
            t["gF"] = sb("gF", [128, D], F32)
            t["cosT"] = sb("cosT", [32, TT], F32)
            t["sinT"] = sb("sinT", [32, TT], F32)
            t["sgt"] = sb("sgt", [128, TT], F32)
            t["uh"] = sb("uh", [128, 8, 2], F32)
            t["masks"] = sb("masks_sb", [128, 8, 128], BF16)
            t["ident"] = sb("ident_sb", [128, 128], BF16)
            t["ones"] = sb("ones_sb", [128, 128], BF16)
            t["pswapb"] = sb("pswapb_sb", [32, 32], BF16)
            t["gains"] = sb("gains_sb", [128, 3, 16], F32)
            t["wcv"] = sb("wcv_sb", [128, 8, 3], F32)
            t["ss"] = sb("ss", [128, 4], F32)
            t["ssp"] = sb("ssp", [128, 16], F32)
            t["sq"] = sb("sq", [128, 4], F32)
            t["rstd"] = sb("rstd", [128, 4], F32)
            t["eps"] = sb("eps_sb", [128, 1], F32)
            if self.dbg:
                t["dtmp"] = sb("dtmp", [128, TT], F32)
            t["pb"] = [es.enter_context(nc.psum_tensor(f"pb{i}", [128, 512], F32)) for i in range(NPS)]
            self.t = t
            sems = {k: sem("s_" + k) for k in ["pe", "act", "dve", "pool", "sp"]}
            self.wsem = [sem(f"w{i}") for i in range(NWSEM)]
            self.csem = [sem(f"c{i}") for i in range(NCAST)]
            self.xsem = [sem(f"x{i}") for i in range(4)]
            self.osem = [sem(f"o{i}") for i in range(4)]
            self.msem = [sem(f"m{i}") for i in range(12)]
            self.dsem = [sem(f"dbg{i}") for i in range(4)]
            self.P = Prog(sems)
            self.dry = False
            self.emit_all()
            P = self.P
            with nc.Block() as block:
                @block.tensor
                def _(h):
                    P.replay("pe", h)

                @block.scalar
                def _(h):
                    P.replay("act", h)

                @block.vector
                def _(h):
                    P.replay("dve", h)

                @block.gpsimd
                def _(h):
                    P.replay("pool", h)

                @block.sync
                def _(h):
                    P.replay("sp", h)
        return nc

    def S(self, a, n=1, f32=False):
        keys = [("sB", a + i) for i in range(n)]
        if self.dry:
            return DUMMY, keys
        ap = self.t["sB"][:, a * 512:(a + n) * 512]
        if f32:
            ap = ap.bitcast(F32)
        return ap, keys

    def alloc(self):
        i = self.psn % NPS
        self.psn += 1
        if self.dry:
            return DUMMY, ("ps", i)
        return self.t["pb"][i], ("ps", i)

    def wnext(self, spec):
        if self.dry:
            self.useq.append(spec)
            return DUMMY, [("wr", 0)], None
        n = self.wpos
        self.wpos += 1
        assert self.useq[n] == spec, (n, self.useq[n], spec)
        self.prefetch()
        assert self.wissued > n, ("weight ring too small / unit not released", n, spec, self.wactive)
        g0, ng = self.wplace[n]
        nr, ncols = spec[2], spec[4]
        ap = self.t["wring"][:, g0 * GRAN:g0 * GRAN + nr * ncols].rearrange("p (r n) -> p r n", r=nr)
        return ap, [("wr", g0 + i) for i in range(ng)], n

    def wdone(self, h):
        if self.dry:
            return
        self.wactive = [a for a in self.wactive if a[0] != h]
        self.prefetch()

    def prefetch(self):
        P = self.P
        while self.wissued < len(self.useq):
            m = self.wissued
            spec = self.useq[m]
            name, r0, nr, c0, ncols = spec
            ng = -(-(nr * ncols) // GRAN)
            cand = [self.whead] if self.whead + ng <= NGRAN else []
            cand.append(0)
            g0 = None
            for c in cand:
                if all(c + ng <= a[1] or c >= a[1] + a[2] for a in self.wactive):
                    g0 = c
                    break
            if g0 is None:
                return
            self.whead = g0 + ng
            self.wactive.append((m, g0, ng))
            self.wplace[m] = (g0, ng)
            sid = self.uidx[spec]
            o0 = self.uoff[spec]
            scr_v = self.dr["scr"][o0:o0 + 128 * nr * ncols].rearrange("(p r n) -> p r n", p=128, r=nr)
            if sid not in self.casted:
                self.casted.add(sid)
                src = self.dr[name][r0 * 128:(r0 + nr) * 128, c0:c0 + ncols].rearrange("(r p) n -> p r n", p=128)
                k = self.ncast % NCAST
                self.ncast += 1
                if m >= CAST_LEAD:
                    P.wait_tok("pool", self.wtok[m - CAST_LEAD])
                P.dma("pool", scr_v, src, self.csem[k], writes=[("scr", sid)])
            dst = self.t["wring"][:, g0 * GRAN:g0 * GRAN + nr * ncols].rearrange("p (r n) -> p r n", r=nr)
            k = self.nwdma % NWSEM
            self.nwdma += 1
            self.wtok[m] = P.dma("sp", dst, scr_v, self.wsem[k], reads=[("scr", sid)], writes=[("wr", g0 + i) for i in range(ng)])
            self.wissued += 1

    def mm(self, out_ap, lhsT, rhs, start, stop, reads, bank, signal=None):
        sig = stop if signal is None else signal
        self.P.op("pe", lambda e: e.matmul(out_ap, lhsT=lhsT, rhs=rhs, start=start, stop=stop),
                  reads=reads, writes=[bank], signal=sig)

    def xkeys(self, s, dgs=range(4)):
        return [("x", s, dg) for dg in dgs]

    def emit_all(self):
        P = self.P
        t = self.t if not self.dry else None
        self.psn = 0
        self.wpos = 0
        self.wissued = 0
        self.whead = 0
        self.wactive = []
        self.wplace = {}
        self.wtok = {}
        self.nwdma = 0
        self.casted = set()
        self.ncast = 0
        self.store_toks = []
        dr = self.dr
        if not self.dry:
            for i, (nm, dst) in enumerate([("gains", t["gains"]), ("gF", t["gF"]), ("wcv", t["wcv"]), ("masks", t["masks"]),
                                           ("ident", t["ident"]), ("ones", t["ones"]), ("pswapb", t["pswapb"])]):
                P.dma("sp", dst[:], dr[nm], self.msem[i], writes=[("c", nm)])
            P.op("dve", lambda e: e.memset(t["eps"][:], EPS), writes=[("c", "eps")])
            P.op("dve", lambda e: e.memset(t["uh"][:], 0.0), writes=[("uh", ch) for ch in range(8)])
        for T in range(self.n_tiles):
            own = T >= N_HALO_T
            if T == 0:
                self.load_x(T)
            self.ffn(T, 0)
            self.mixer(T, own)
            if own:
                self.ffn(T, 2)
                self.final(T)
        if not self.dry:
            for tok in self.store_toks:
                P.wait_tok("sp", tok)

    def load_x(self, T, subs=range(4)):
        if self.dry or T >= self.n_tiles:
            return
        for s in subs:
            self.P.dma("sp", self.t["xt"][:, s, :], self.dr["xin"][T * TT + s * 128:T * TT + (s + 1) * 128, :],
                       self.xsem[s], writes=self.xkeys(s))

    def dump(self, name, T, src_ap, keys, rows=None):
        if self.dry or name not in self.dbg:
            return
        k = self.ndbg = getattr(self, "ndbg", 0) + 1
        tok = self.P.dma("sp", rows, src_ap, self.dsem[k % 4], reads=keys)
        self.store_toks.append(tok)

    def dump_any(self, name, row0, src_ap, keys, eng="dve"):
        if self.dry or name not in self.dbg:
            return
        P = self.P
        t = self.t
        P.op(eng, (lambda e: e.tensor_copy(out=t["dtmp"][:], in_=src_ap)) if eng == "dve" else (lambda e: e.activation(out=t["dtmp"][:], in_=src_ap, func=AF.Copy)),
             reads=keys, writes=[("dtmp",)])
        k = self.ndbg = getattr(self, "ndbg", 0) + 1
        tok = P.dma("sp", self.dr[name][row0:row0 + 128, :], t["dtmp"][:], self.dsem[k % 4], reads=[("dtmp",)])
        self.store_toks.append(tok)

    def stats(self, s, partial, junk, junkk):
        P = self.P
        t = self.t
        xs = t["xt"][:, s, :]
        if partial:
            P.op("dve", lambda e: e.reduce_sum(out=t["ss"][:, s:s + 1], in_=t["ssp"][:, 4 * s:4 * s + 4], axis=mybir.AxisListType.X),
                 reads=[("ssp", s, dg) for dg in range(4)], writes=[("ss", s)])
        else:
            P.op("act", lambda e: e.activation(out=junk, in_=xs, func=AF.Square, accum_out=t["ss"][:, s:s + 1]),
                 reads=self.xkeys(s), writes=junkk + [("ss", s)])
        P.op("act", lambda e: e.activation(out=t["sq"][:, s:s + 1], in_=t["ss"][:, s:s + 1], func=AF.Sqrt, scale=1.0 / D, bias=t["eps"][:]),
             reads=[("ss", s), ("c", "eps")], writes=[("sq", s)])
        P.op("dve", lambda e: e.reciprocal(out=t["rstd"][:, s:s + 1], in_=t["sq"][:, s:s + 1]),
             reads=[("sq", s)], writes=[("rstd", s)])

    def stats_batch(self):
        P = self.P
        t = self.t
        allp = [("ssp", s, dg) for s in range(4) for dg in range(4)]
        P.op("dve", lambda e: e.reduce_sum(out=t["ss"][:, 0:4], in_=t["ssp"][:, 0:16].rearrange("p (s d) -> p s d", d=4), axis=mybir.AxisListType.X),
             reads=allp, writes=[("ss", s) for s in range(4)])
        P.op("act", lambda e: e.activation(out=t["sq"][:, 0:4], in_=t["ss"][:, 0:4], func=AF.Sqrt, scale=1.0 / D, bias=t["eps"][:]),
             reads=[("ss", s) for s in range(4)] + [("c", "eps")], writes=[("sq", s) for s in range(4)])
        P.op("dve", lambda e: e.reciprocal(out=t["rstd"][:, 0:4], in_=t["sq"][:, 0:4]),
             reads=[("sq", s) for s in range(4)], writes=[("rstd", s) for s in range(4)])

    def sq_partial(self, s, dg):
        P = self.P
        t = self.t
        xs = t["xt"][:, s, dg * 512:(dg + 1) * 512]
        P.op("act", lambda e: e.activation(out=t["sgt"][:], in_=xs, func=AF.Square, accum_out=t["ssp"][:, 4 * s + dg:4 * s + dg + 1]),
             reads=[("x", s, dg)], writes=[("sgt",), ("ssp", s, dg)])

    def norm_T(self, gi, partial=False):
        P = self.P
        t = self.t
        dry = self.dry
        if partial and not dry:
            self.stats_batch()
        for s in range(4):
            hn, hk = self.S(4 * s, 4)
            if not dry:
                xs = t["xt"][:, s, :]
                if not partial:
                    self.stats(s, False, hn, hk)
                if s == 3:
                    P.op("act", lambda e, hn=hn, xs=xs, s=s: e.activation(out=hn, in_=xs, func=AF.Copy, scale=t["rstd"][:, s:s + 1]),
                         reads=self.xkeys(s) + [("rstd", s)], writes=hk)
                else:
                    P.op("dve", lambda e, hn=hn, xs=xs, s=s: e.tensor_scalar(out=hn, in0=xs, scalar1=t["rstd"][:, s:s + 1], scalar2=0.0, op0=ALU.mult, op1=ALU.add),
                         reads=self.xkeys(s) + [("rstd", s)], writes=hk)
        for c in range(16):
            bank, bk = self.alloc()
            if dry:
                continue
            bb = bank[:].bitcast(BF16)
            for s in range(4):
                hn, hk = self.S(4 * s, 4)
                P.op("pe", lambda e, bb=bb, hn=hn, s=s, c=c: e.transpose(out=bb[:, s * 128:(s + 1) * 128], in_=hn[:, c * 128:(c + 1) * 128], identity=t["ident"][:]),
                     reads=hk + [("c", "ident")], writes=[bk], signal=(s == 3))
            eng = "act" if c % 2 == 0 else "dve"
            if eng == "act":
                P.op("act", lambda e, bb=bb, c=c: e.activation(out=t["hT"][:, c, :], in_=bb[:, 0:TT], func=AF.Copy, scale=t["gains"][:, gi, c:c + 1]),
                     reads=[bk, ("c", "gains")], writes=[("hT", c)])
            else:
                P.op("dve", lambda e, bb=bb, c=c: e.tensor_scalar(out=t["hT"][:, c, :], in0=bb[:, 0:TT], scalar1=t["gains"][:, gi, c:c + 1], scalar2=0.0, op0=ALU.mult, op1=ALU.add),
                     reads=[bk, ("c", "gains")], writes=[("hT", c)])

    def ffn(self, T, gi):
        P = self.P
        t = self.t
        dry = self.dry
        w_in = "w_ffn1_in" if gi == 0 else "w_ffn2_in"
        w_out = "w_ffn1_out" if gi == 0 else "w_ffn2_out"
        self.norm_T(gi, partial=(gi == 2))
        if gi == 0 and T == 0 and not dry:
            for c in range(16):
                self.dump_any("d_hT", c * 128, t["hT"][:, c, :], [("hT", c)])
        for f in range(44):
            wg, kg, hg = self.wnext((w_in, 0, 16, 128 * f, 128))
            wu, ku, hu = self.wnext((w_in, 0, 16, DFF + 128 * f, 128))
            bg, bgk = self.alloc()
            for c in range(16):
                if not dry:
                    self.mm(bg[:], wg[:, c, :], t["hT"][:, c, :], c == 0, c == 15, kg + [("hT", c)], bgk)
            self.wdone(hg)
            bu, buk = self.alloc()
            for c in range(16):
                if not dry:
                    self.mm(bu[:], wu[:, c, :], t["hT"][:, c, :], c == 0, c == 15, ku + [("hT", c)], buk)
            self.wdone(hu)
            a, ak = self.S(f, 1)
            if not dry and gi == 0 and T == 0 and f in (0, 1, 43):
                fi = (0, 1, 43).index(f)
                self.dump_any("d_gate", fi * 128, bg[:], [bgk])
                self.dump_any("d_up", fi * 128, bu[:], [buk])
            if not dry:
                P.op("act", lambda e, bg=bg: e.activation(out=t["sgt"][:], in_=bg[:], func=AF.Silu), reads=[bgk], writes=[("sgt",)])
                P.op("dve", lambda e, bu=bu, a=a: e.tensor_tensor(out=a, in0=bu[:], in1=t["sgt"][:], op=ALU.mult), reads=[buk, ("sgt",)], writes=ak)
        if gi == 0 and T == 0 and not dry:
            for fi, f in enumerate((0, 1, 43)):
                a, ak = self.S(f, 1)
                self.dump_any("d_act", fi * 128, a, ak)
        fgs = [(0, 8), (8, 8), (16, 8), (24, 8), (32, 8), (40, 4)]
        for dg in range(4):
            banks = [self.alloc() for _ in range(4)]
            for (f0, nf) in fgs:
                wo, ko, ho = self.wnext((w_out, f0, nf, 512 * dg, 512))
                for s in range(4):
                    for r in range(nf):
                        f = f0 + r
                        if not dry:
                            a, ak = self.S(f, 1)
                            last_use = (s == 3 and r == nf - 1)
                            self.mm(banks[s][0][:], a[:, s * 128:(s + 1) * 128], wo[:, r, :], f == 0, f == 43, ak + ko, banks[s][1],
                                    signal=(f == 43 or last_use))
                self.wdone(ho)
            for s in range(4):
                if not dry:
                    xs = t["xt"][:, s, dg * 512:(dg + 1) * 512]
                    P.op("dve", lambda e, b=banks[s][0], xs=xs: e.scalar_tensor_tensor(out=xs, in0=b[:], scalar=0.5, in1=xs, op0=ALU.mult, op1=ALU.add),
                         reads=[banks[s][1], ("x", s, dg)], writes=[("x", s, dg)])
                    self.sq_partial(s, dg)
        if "x1" in self.dbg and gi == 0 and not dry:
            for s in range(4):
                self.dump("x1", T, t["xt"][:, s, :], self.xkeys(s), rows=self.dr["x1"][T * TT + s * 128:T * TT + (s + 1) * 128, :])

    def rope_A(self, w, wk, i, g, dst, dstk, wh=None):
        P = self.P
        t = self.t
        dry = self.dry
        bank, bk = self.alloc()
        for c in range(16):
            if not dry:
                self.mm(bank[:], w[:, c, i * 128:(i + 1) * 128], t["hT"][:, c, :], c == 0, c == 15, wk + [("hT", c)], bk)
        if wh is not None:
            self.wdone(wh)
        par = self.rope_par = 1 - getattr(self, "rope_par", 0)
        r32, rk = self.S(28 + 2 * par, 2, f32=True)
        r16, r16k = self.S(36 + par, 1)
        if not dry:
            P.op("act", lambda e: e.activation(out=blk(dst), in_=perm(g, bank[:]), func=AF.Copy), reads=[bk], writes=dstk)
            P.op("act", lambda e: e.activation(out=r16[0:32, :], in_=bank[0:32, :], func=AF.Copy), reads=[bk], writes=r16k)
            P.op("act", lambda e: e.activation(out=r32[0:32, :], in_=bank[0:32, :], func=AF.Copy), reads=[bk], writes=rk)
        return (r32, rk, dst, dstk, g, r16, r16k)

    def rope_B(self, st):
        if st is None:
            return
        P = self.P
        t = self.t
        r32, rk, dst, dstk, g, r16, r16k = st
        t1, t1k = self.S(32, 2, f32=True)
        t2, t2k = self.S(34, 2, f32=True)
        sw, swk = self.alloc()
        if self.dry:
            return
        self.mm(sw[0:32, :], t["pswapb"][:], r16[0:32, :], True, True, r16k + [("c", "pswapb")], swk)
        P.op("dve", lambda e: e.tensor_tensor(out=t1[0:32, :], in0=r32[0:32, :], in1=t["cosT"][:], op=ALU.mult), reads=rk + [("tab",)], writes=t1k)
        P.op("dve", lambda e: e.tensor_tensor(out=t2[0:32, :], in0=sw[0:32, :], in1=t["sinT"][:], op=ALU.mult), reads=[swk, ("tab",)], writes=t2k)
        P.op("dve", lambda e: e.tensor_tensor(out=blk(dst[0:32, :]), in0=perm(g, t1[0:32, :]), in1=perm(g, t2[0:32, :]), op=ALU.add),
             reads=t1k + t2k, writes=dstk)

    def mixer(self, T, own):
        P = self.P
        t = self.t
        dry = self.dry
        dr = self.dr
        sl = [T % 2, T % 2, T % 5]
        last_halo = (T == N_HALO_T - 1)
        if not dry:
            P.dma("sp", t["cosT"][:], dr["cosd"][T], self.msem[8], writes=[("tab",)])
            P.dma("sp", t["sinT"][:], dr["sind"][T], self.msem[9], writes=[("tab",)])
        self.norm_T(1, partial=True)
        if not own:
            self.load_x(T + 1)
        pending = None
        kv_groups = (0, 1, 2) if (own or last_halo) else (2,)
        if own:
            for u in range(6):
                w, wk, wh = self.wnext(("w_in", 0, 16, OFF_Q + 256 * u, 256))
                for i in range(2):
                    h = 2 * u + i
                    dst, dstk = self.S(16 + h, 1)
                    st = self.rope_A(w, wk, i, h // 4, dst, dstk, wh if i == 1 else None)
                    self.rope_B(pending)
                    pending = st
        for u in range(6):
            if u // 2 not in kv_groups:
                continue
            w, wk, wh = self.wnext(("w_in", 0, 16, OFF_K + 256 * u, 256))
            for i in range(2):
                h = 2 * u + i
                g, j = h // 4, h % 4
                dst = DUMMY if dry else t[f"KT{g}"][:, sl[g], j, :]
                st = self.rope_A(w, wk, i, g, dst, [("KT", g, sl[g], j)], wh if i == 1 else None)
                self.rope_B(pending)
                pending = st
        for u in range(6):
            if u // 2 not in kv_groups:
                continue
            w, wk, wh = self.wnext(("w_in", 0, 16, OFF_V + 256 * u, 256))
            g, half = u // 2, u % 2
            for qbp in range(2):
                bank, bk = self.alloc()
                for qq in range(2):
                    qb = 2 * qbp + qq
                    for c in range(16):
                        if not dry:
                            hc = t["hT"][:, c, :]
                            lhsT = hc[:, qb * 128:(qb + 1) * 128] if g == 0 else hc[:, qb:TT:4]
                            self.mm(bank[:, qq * 256:(qq + 1) * 256], lhsT, w[:, c, :], c == 0, c == 15, wk + [("hT", c)], bk)
                if not dry:
                    dst = t[f"V{g}"][:, sl[g], 2 * qbp:2 * qbp + 2, half * 256:(half + 1) * 256]
                    P.op("act", lambda e, dst=dst, bank=bank: e.activation(out=dst, in_=bank[:].rearrange("p (q n) -> p q n", q=2), func=AF.Copy),
                         reads=[bk], writes=[("V", g, sl[g], qbp, half)])
                if pending is not None:
                    self.rope_B(pending)
                    pending = None
            self.wdone(wh)
        if not own:
            if last_halo:
                self.conv_halo()
            return
        self.attention(T, sl)
        self.conv()
        self.merge_out()
        if "x2" in self.dbg and not dry:
            for s in range(4):
                self.dump("x2", T, t["xt"][:, s, :], self.xkeys(s), rows=dr["x2"][T * TT + s * 128:T * TT + (s + 1) * 128, :])

    def attention(self, T, sl):
        pend = None
        idx = 0
        for j in range(4):
            for g in range(3):
                st = self.attn_A(T, sl, j, g, idx % 2)
                if pend is not None:
                    self.attn_B(pend)
                pend = st
                idx += 1
        self.attn_B(pend)

    def attn_A(self, T, sl, j, g, pbuf):
        P = self.P
        t = self.t
        dry = self.dry
        halo = lambda TT_: TT_ < N_HALO_T
        h = 4 * g + j
        QT, QTk = self.S(16 + h, 1)
        deltas = []
        if g == 0:
            own_l, prev_l = [], []
            for qb in range(4):
                own_l.append((sl[0], qb))
                prev_l.append((sl[0], qb - 1) if qb > 0 else ((T - 1) % 2, 3))
            deltas.append((own_l, [0] * 4))
            deltas.append((prev_l, [2 if (halo(T - 1)) else 1] + [1] * 3))
        elif g == 1:
            deltas.append(([(sl[1], qb) for qb in range(4)], [0] * 4))
            deltas.append(([((T - 1) % 2, qb) for qb in range(4)], [2 if halo(T - 1) else 1] * 4))
        else:
            for dl in range(5):
                Tk = T - dl
                if dl == 0:
                    mi = 3
                elif dl < 4:
                    mi = 6 if halo(Tk) else 4
                else:
                    mi = 7 if halo(Tk) else 5
                deltas.append(([(Tk % 5, qb) for qb in range(4)], [mi] * 4))
        pts = []
        for di, (blks, mis) in enumerate(deltas):
            sc, sck = self.alloc()
            pt, ptk = self.S(5 * pbuf + di, 1)
            pts.append((pt, ptk))
            if dry:
                continue
            for qb, (slot, kb) in enumerate(blks):
                self.mm(sc[:, qb * 128:(qb + 1) * 128], t[f"KT{g}"][:, slot, j, kb * 128:(kb + 1) * 128], QT[:, qb * 128:(qb + 1) * 128],
                        True, True, [("KT", g, slot, j)] + QTk, sck, signal=(qb == 3))
            P.op("act", lambda e, pt=pt, sc=sc: e.activation(out=pt, in_=sc[:], func=AF.Exp, scale=SCALE), reads=[sck], writes=ptk)
            if len(set(mis)) == 1:
                mi = mis[0]
                P.op("dve", lambda e, pt=pt, mi=mi: e.tensor_tensor(out=blk(pt), in0=blk(pt), in1=t["masks"][:, mi, :].unsqueeze(1).to_broadcast([128, 4, 128]), op=ALU.mult),
                     reads=ptk + [("c", "masks")], writes=ptk)
            else:
                mi0, mi = mis[0], mis[1]
                P.op("dve", lambda e, pt=pt, mi0=mi0: e.tensor_tensor(out=pt[:, 0:128], in0=pt[:, 0:128], in1=t["masks"][:, mi0, :], op=ALU.mult),
                     reads=ptk + [("c", "masks")], writes=ptk)
                P.op("dve", lambda e, pt=pt, mi=mi: e.tensor_tensor(out=pt[:, 128:512].rearrange("p (q n) -> p q n", q=3), in0=pt[:, 128:512].rearrange("p (q n) -> p q n", q=3),
                                                                    in1=t["masks"][:, mi, :].unsqueeze(1).to_broadcast([128, 3, 128]), op=ALU.mult),
                     reads=ptk + [("c", "masks")], writes=ptk)
        return (j, g, deltas, pts)

    def attn_B(self, st):
        P = self.P
        t = self.t
        dry = self.dry
        j, g, deltas, pts = st
        buf = j % 2
        OT, OTk = self.S(28 + 2 * buf, 2, f32=True)
        LB, LBk = self.S(32 + 2 * buf, 2, f32=True)
        ob, obk = self.alloc()
        lb, lbk = self.alloc()
        nd = len(deltas)
        if not dry:
            for qb in range(4):
                for di, (blks, mis) in enumerate(deltas):
                    slot, kb = blks[qb]
                    pt, ptk = pts[di]
                    self.mm(ob[:, qb * 128:(qb + 1) * 128], t[f"V{g}"][:, slot, kb, j * 128:(j + 1) * 128], pt[:, qb * 128:(qb + 1) * 128],
                            di == 0, di == nd - 1, [("V", g, slot, kb // 2, j // 2)] + ptk, obk, signal=(qb == 3 and di == nd - 1))
            for di in range(nd):
                pt, ptk = pts[di]
                self.mm(lb[:], t["ones"][:], pt, di == 0, di == nd - 1, ptk + [("c", "ones")], lbk)
            if g == 0:
                P.op("dve", lambda e: e.tensor_copy(out=OT, in_=ob[:]), reads=[obk], writes=OTk)
                P.op("dve", lambda e: e.tensor_copy(out=LB, in_=lb[:]), reads=[lbk], writes=LBk)
            else:
                P.op("dve", lambda e: e.tensor_tensor(out=perm(g, OT), in0=blk(ob[:]), in1=perm(g, OT), op=ALU.add), reads=[obk] + OTk, writes=OTk)
                P.op("dve", lambda e: e.tensor_tensor(out=perm(g, LB), in0=blk(lb[:]), in1=perm(g, LB), op=ALU.add), reads=[lbk] + LBk, writes=LBk)
        if g == 2:
            rc, rck = self.S(10, 2, f32=True)
            on, onk = self.S(40 + j, 1)
            if not dry:
                P.op("dve", lambda e: e.reciprocal(out=rc, in_=LB), reads=LBk, writes=rck)
                P.op("dve", lambda e: e.tensor_tensor(out=on, in0=OT, in1=rc, op=ALU.mult), reads=OTk + rck, writes=onk)

    def conv_halo(self):
        P = self.P
        t = self.t
        dry = self.dry
        for ch in range(8):
            wx, wxk, hx = self.wnext(("w_in", 0, 16, OFF_XC + 128 * ch, 128))
            wc, wck, hc = self.wnext(("w_in", 0, 16, OFF_C + 128 * ch, 128))
            bx, bxk = self.alloc()
            bc, bck = self.alloc()
            xs, xsk = self.S(36, 2, f32=True)
            if not dry:
                for c in range(16):
                    self.mm(bx[:, 0:2], wx[:, c, :], t["hT"][:, c, TT - 2:TT], c == 0, c == 15, wxk + [("hT", c)], bxk)
                for c in range(16):
                    self.mm(bc[:, 0:2], wc[:, c, :], t["hT"][:, c, TT - 2:TT], c == 0, c == 15, wck + [("hT", c)], bck)
                P.op("act", lambda e, bx=bx, xs=xs: e.activation(out=xs[:, 0:2], in_=bx[:, 0:2], func=AF.Copy), reads=[bxk], writes=xsk)
                P.op("dve", lambda e, bc=bc, xs=xs, ch=ch: e.tensor_tensor(out=t["uh"][:, ch, :], in0=bc[:, 0:2], in1=xs[:, 0:2], op=ALU.mult),
                     reads=[bck] + xsk, writes=[("uh", ch)])
            self.wdone(hx)
            self.wdone(hc)

    def conv(self):
        P = self.P
        t = self.t
        dry = self.dry
        for ch in range(8):
            wx, wxk, hx = self.wnext(("w_in", 0, 16, OFF_XC + 128 * ch, 128))
            wc, wck, hc = self.wnext(("w_in", 0, 16, OFF_C + 128 * ch, 128))
            wb, wbk, hb = self.wnext(("w_in", 0, 16, OFF_B + 128 * ch, 128))
            for _one in range(1):
                bx, bxk = self.alloc()
                bc, bck = self.alloc()
                bb, bbk = self.alloc()
                xs, xsk = self.S(36, 2, f32=True)
                acc, acck = self.S(38, 2, f32=True)
                ub, ubk = self.S(8, 3, f32=True)
                cv, cvk = self.S(28 + ch, 1)
                if dry:
                    continue
                for (b, bk_, w_, wk_) in ((bx, bxk, wx, wxk), (bc, bck, wc, wck), (bb, bbk, wb, wbk)):
                    for c in range(16):
                        self.mm(b[:], w_[:, c, :], t["hT"][:, c, :], c == 0, c == 15, wk_ + [("hT", c)], bk_)
                self.wdone(hx)
                self.wdone(hc)
                self.wdone(hb)
                cw = t["wcv"]
                P.op("act", lambda e, bx=bx, xs=xs: e.activation(out=xs, in_=bx[:], func=AF.Copy), reads=[bxk], writes=xsk)
                P.op("dve", lambda e, ub=ub, ch=ch: e.tensor_copy(out=ub[:, 0:2], in_=t["uh"][:, ch, :]), reads=[("uh", ch)], writes=ubk)
                P.op("dve", lambda e, ub=ub, bc=bc, xs=xs: e.tensor_tensor(out=ub[:, 2:514], in0=bc[:], in1=xs, op=ALU.mult), reads=[bck] + xsk, writes=ubk)
                P.op("dve", lambda e, ub=ub, ch=ch: e.tensor_copy(out=t["uh"][:, ch, :], in_=ub[:, 512:514]), reads=ubk, writes=[("uh", ch)])
                P.op("dve", lambda e, ub=ub, acc=acc, ch=ch: e.tensor_scalar(out=acc, in0=ub[:, 2:514], scalar1=cw[:, ch, 2:3], scalar2=0.0, op0=ALU.mult, op1=ALU.add),
                     reads=ubk + [("c", "wcv")], writes=acck)
                P.op("dve", lambda e, ub=ub, acc=acc, ch=ch: e.scalar_tensor_tensor(out=acc, in0=ub[:, 1:513], scalar=cw[:, ch, 1:2], in1=acc, op0=ALU.mult, op1=ALU.add),
                     reads=ubk + acck + [("c", "wcv")], writes=acck)
                P.op("dve", lambda e, ub=ub, acc=acc, ch=ch: e.scalar_tensor_tensor(out=acc, in0=ub[:, 0:512], scalar=cw[:, ch, 0:1], in1=acc, op0=ALU.mult, op1=ALU.add),
                     reads=ubk + acck + [("c", "wcv")], writes=acck)
                P.op("dve", lambda e, bb=bb, acc=acc, cv=cv: e.tensor_tensor(out=cv, in0=bb[:], in1=acc, op=ALU.mult), reads=[bbk] + acck, writes=cvk)

    def merge_out(self):
        P = self.P
        t = self.t
        dry = self.dry
        for d in range(16):
            wgc, kgc, hgc = self.wnext(("w_in", 0, 16, OFF_GC + 128 * d, 128))
            wga, kga, hga = self.wnext(("w_in", 0, 16, OFF_GA + 128 * d, 128))
            wco, kco, hco = self.wnext(("w_conv_out", 0, 8, 128 * d, 128))
            wao, kao, hao = self.wnext(("w_attn_out", 0, 4, 128 * d, 128))
            for _one in range(1):
                b1, b1k = self.alloc()
                b2, b2k = self.alloc()
                b3, b3k = self.alloc()
                b4, b4k = self.alloc()
                par = d % 2
                s1, s1k = self.S(16 + 2 * par, 2, f32=True)
                s2, s2k = self.S(20 + 2 * par, 2, f32=True)
                m1, m1k = self.S(24, 2, f32=True)
                m2, m2k = self.S(26, 2, f32=True)
                mg, mgk = self.S(d, 1)
                if dry:
                    continue
                for c in range(16):
                    self.mm(b1[:], wgc[:, c, :], t["hT"][:, c, :], c == 0, c == 15, kgc + [("hT", c)], b1k)
                for c in range(16):
                    self.mm(b2[:], wga[:, c, :], t["hT"][:, c, :], c == 0, c == 15, kga + [("hT", c)], b2k)
                for ch in range(8):
                    cv, cvk = self.S(28 + ch, 1)
                    self.mm(b3[:], wco[:, ch, :], cv, ch == 0, ch == 7, kco + cvk, b3k)
                for jj in range(4):
                    on, onk = self.S(40 + jj, 1)
                    self.mm(b4[:], wao[:, jj, :], on, jj == 0, jj == 3, kao + onk, b4k)
                for hh in (hgc, hga, hco, hao):
                    self.wdone(hh)
                P.op("act", lambda e, b1=b1, s1=s1: e.activation(out=s1, in_=b1[:], func=AF.Sigmoid), reads=[b1k], writes=s1k)
                P.op("act", lambda e, b2=b2, s2=s2: e.activation(out=s2, in_=b2[:], func=AF.Sigmoid), reads=[b2k], writes=s2k)
                P.op("dve", lambda e, b3=b3, s1=s1, m1=m1: e.tensor_tensor(out=m1, in0=b3[:], in1=s1, op=ALU.mult), reads=[b3k] + s1k, writes=m1k)
                P.op("dve", lambda e, b4=b4, s2=s2, m2=m2: e.tensor_tensor(out=m2, in0=b4[:], in1=s2, op=ALU.mult), reads=[b4k] + s2k, writes=m2k)
                P.op("dve", lambda e, m1=m1, m2=m2, mg=mg: e.tensor_tensor(out=mg, in0=m1, in1=m2, op=ALU.add), reads=m1k + m2k, writes=mgk)
        for dg in range(4):
            banks = [self.alloc() for _ in range(4)]
            for half in range(2):
                wo, ko, ho = self.wnext(("w_o", 8 * half, 8, 512 * dg, 512))
                for s in range(4):
                    for r in range(8):
                        dch = 8 * half + r
                        if not dry:
                            mg, mgk = self.S(dch, 1)
                            self.mm(banks[s][0][:], mg[:, s * 128:(s + 1) * 128], wo[:, r, :], dch == 0, dch == 15, mgk + ko, banks[s][1],
                                    signal=(dch == 15 or (s == 3 and r == 7)))
                self.wdone(ho)
            for s in range(4):
                if not dry:
                    xs = t["xt"][:, s, dg * 512:(dg + 1) * 512]
                    P.op("dve", lambda e, b=banks[s][0], xs=xs: e.tensor_tensor(out=xs, in0=b[:], in1=xs, op=ALU.add),
                         reads=[banks[s][1], ("x", s, dg)], writes=[("x", s, dg)])
                    self.sq_partial(s, dg)

    def final(self, T):
        P = self.P
        t = self.t
        if self.dry:
            return
        self.stats_batch()
        obs = []
        for s in range(4):
            xs = t["xt"][:, s, :]
            ob, obk = self.S(12 + 8 * s, 8, f32=True)
            if s % 2 == 1:
                P.op("act", lambda e, xs=xs, s=s, ob=ob: e.activation(out=ob, in_=xs, func=AF.Copy, scale=t["rstd"][:, s:s + 1]),
                     reads=self.xkeys(s) + [("rstd", s)], writes=obk)
            else:
                P.op("dve", lambda e, xs=xs, s=s, ob=ob: e.scalar_tensor_tensor(out=ob, in0=xs, scalar=t["rstd"][:, s:s + 1], in1=t["gF"][:], op0=ALU.mult, op1=ALU.mult),
                     reads=self.xkeys(s) + [("rstd", s), ("c", "gF")], writes=obk)
            obs.append((ob, obk))
            self.load_x(T + 1, [s])
        for s in (1, 3):
            ob, obk = obs[s]
            P.op("dve", lambda e, ob=ob: e.tensor_tensor(out=ob, in0=ob, in1=t["gF"][:], op=ALU.mult), reads=obk + [("c", "gF")], writes=obk)
        for s in range(4):
            ob, obk = obs[s]
            r0 = (T - N_HALO_T) * TT + s * 128
            tok = P.dma("sp", self.dr["y"][r0:r0 + 128, :], ob, self.osem[s], reads=obk)
            self.store_toks.append(tok)


def _const_inputs(jchunk):
    p = np.arange(128)
    k = p[:, None]
    q = p[None, :]
    same = (k % 4) == (q % 4)
    valid = 1.0 if jchunk > 0 else 0.0
    m = np.zeros((128, 8, 128), np.float32)
    m[:, 0] = (k <= q)
    m[:, 1] = (k >= q)
    m[:, 2] = (k >= q) * valid
    m[:, 3] = same & (k <= q)
    m[:, 4] = same
    m[:, 5] = same & (k >= q)
    m[:, 6] = same * valid
    m[:, 7] = (same & (k >= q)) * valid
    psw = np.zeros((32, 32), np.float32)
    for i in range(16):
        psw[i + 16, i] = 1.0
        psw[i, i + 16] = 1.0
    pos = (jchunk * T_OWN - N_HALO_T * TT + np.arange(T_ALL)).astype(np.float32)
    half = 16
    inv_freq = (np.float32(ROPE_THETA) ** (-(np.arange(half, dtype=np.float32) * np.float32(2.0)) / np.float32(32))).astype(np.float32)
    ang = (pos[:, None] * inv_freq[None, :]).astype(np.float32)
    cos = np.cos(ang).astype(np.float32).T
    sin = np.sin(ang).astype(np.float32).T
    cosd = np.concatenate([cos, cos], 0).reshape(32, NT, TT).transpose(1, 0, 2)
    sind = np.concatenate([-sin, sin], 0).reshape(32, NT, TT).transpose(1, 0, 2)
    return {
        "masks": m.astype(ml_dtypes.bfloat16),
        "ident": np.eye(128, dtype=np.float32).astype(ml_dtypes.bfloat16),
        "ones": np.ones((128, 128), np.float32).astype(ml_dtypes.bfloat16),
        "pswapb": psw.astype(ml_dtypes.bfloat16),
        "cosd": np.ascontiguousarray(cosd, dtype=np.float32),
        "sind": np.ascontiguousarray(sind, dtype=np.float32),
    }


def make_in_maps(inputs):
    x = np.asarray(inputs["x"], dtype=np.float32)
    shared = {}
    for nm in ["w_ffn1_in", "w_ffn1_out", "w_in", "w_conv_out", "w_attn_out", "w_o", "w_ffn2_in", "w_ffn2_out"]:
        shared[nm] = np.ascontiguousarray(np.asarray(inputs[nm], dtype=np.float32)[0])
    gains = np.stack([np.asarray(inputs[k], np.float32)[0].reshape(16, 128).T for k in ["ffn1_norm", "mix_norm", "ffn2_norm"]], axis=1)
    shared["gains"] = np.ascontiguousarray(gains)
    shared["gF"] = np.ascontiguousarray(np.broadcast_to(np.asarray(inputs["final_norm"], np.float32)[None, :], (128, D)))
    wc = np.asarray(inputs["w_conv"], np.float32)[0]
    shared["wcv"] = np.ascontiguousarray(wc.reshape(3, 8, 128).transpose(2, 1, 0))
    in_maps = []
    for c in range(NC):
        b, j = c // 4, c % 4
        xin = np.zeros((T_ALL, D), np.float32)
        xin[N_HALO_T * TT:] = x[b, j * T_OWN:(j + 1) * T_OWN]
        if j > 0:
            xin[:N_HALO_T * TT] = x[b, j * T_OWN - N_HALO_T * TT:j * T_OWN]
        m = dict(shared)
        m["xin"] = xin
        m.update(_const_inputs(j))
        in_maps.append(m)
    return in_maps


_NC_CACHE = {}


def kernel(**inputs):
    if "nc" not in _NC_CACHE:
        _NC_CACHE["nc"] = Builder().build()
    nc = _NC_CACHE["nc"]
    in_maps = make_in_maps(inputs)
    res = run_bass_kernel_spmd(nc, in_maps, core_ids=list(range(NC)))
    out = np.zeros((2, 16384, D), np.float32)
    for c in range(NC):
        b, j = c // 4, c % 4
        out[b, j * T_OWN:(j + 1) * T_OWN] = res.results[c]["y"]
    return out
```
